# Optimizing a Trainium2 kernel written in Bass

```python
import jax, jax.numpy as jnp
from jax import lax
import numpy as np

D_MODEL = 1024
BATCH = 4
SEQ = 4096
DEPTH = 2
DEC_BATCH = 32
DEC_SEQ = 16
PAST_LEN = 1024

CHUNK = 64
D_FF = 2816
N_HEADS = 16
N_KV_HEADS = 4
HEAD_DIM = 64
GROUP = N_HEADS // N_KV_HEADS
WINDOW = 128
WIN_CHUNKS = WINDOW // CHUNK
BAND = (WIN_CHUNKS + 1) * CHUNK
ATTN_Q = N_HEADS * HEAD_DIM
ATTN_KV = N_KV_HEADS * HEAD_DIM
SCALE = HEAD_DIM ** -0.5
HG_HEADS = 8
HG_DK = 128
HG_DV = D_MODEL // HG_HEADS
HG_FDIM = HG_HEADS * HG_DK
HG_IDIM = HG_HEADS * HG_DV
HG_BLOCK = 16
LOG_F_FLOOR = -60.0
LRU_WIDTH = D_MODEL
LRU_BLOCKS = 16
LRU_BW = LRU_WIDTH // LRU_BLOCKS
CONV_WIDTH = 4
LRU_C = 8.0
SPLIT_SIZES = (ATTN_Q, ATTN_KV, ATTN_KV, HG_FDIM, HG_FDIM, HG_IDIM, HG_IDIM, LRU_WIDTH, LRU_WIDTH, D_MODEL, D_MODEL, D_MODEL)
IN_COLS = ATTN_Q + 2 * ATTN_KV + 2 * HG_FDIM + 2 * HG_IDIM + 2 * LRU_WIDTH + 3 * D_MODEL
EPS = 1e-6
NEG = -1e30

kernel_name = "hybrid_streaming_swa_hgrn2_rglru_step"


def rms_norm(x, g):
    xf = x.astype(jnp.float32)
    y = xf * lax.rsqrt(jnp.mean(xf * xf, axis=-1, keepdims=True) + EPS)
    return y.astype(x.dtype) * g


def swiglu(h, w_up, w_down):
    gate, val = jnp.split(h @ w_up, 2, axis=-1)
    return (jax.nn.silu(gate) * val) @ w_down


def sink_softmax(s, sinks):
    m = jnp.maximum(jnp.max(s, axis=-1, keepdims=True), sinks)
    p = jnp.exp(s - m)
    return p / (jnp.sum(p, axis=-1, keepdims=True) + jnp.exp(sinks - m))


def swa_prompt(q, k, v, sinks):
    b, t = q.shape[:2]
    nc = t // CHUNK
    qc = q.reshape(b, nc, CHUNK, N_KV_HEADS, GROUP, HEAD_DIM)

    def band(a):
        ac = a.reshape(b, nc, CHUNK, N_KV_HEADS, HEAD_DIM)
        ap = jnp.pad(ac, ((0, 0), (WIN_CHUNKS, 0), (0, 0), (0, 0), (0, 0)))
        return jnp.concatenate([ap[:, j:j + nc] for j in range(WIN_CHUNKS + 1)], axis=2)

    kb, vb = band(k), band(v)
    valid = (np.arange(nc)[:, None] - WIN_CHUNKS + (np.arange(BAND) // CHUNK)[None, :]) >= 0
    s = jnp.einsum('bnqhgd,bnkhd->bnhgqk', qc, kb).astype(jnp.float32) * SCALE
    s = jnp.where(valid[None, :, None, None, None, :], s, NEG)
    sk = sinks.astype(jnp.float32).reshape(N_KV_HEADS, GROUP)[None, None, :, :, None, None]
    p = sink_softmax(s, sk)
    o = jnp.einsum('bnhgqk,bnkhd->bnqhgd', p.astype(v.dtype), vb)
    return o.reshape(b, t, ATTN_Q)


def swa_sample(q, k, v, ck, cv, sinks):
    b, t = q.shape[:2]
    keys = jnp.concatenate([ck.astype(k.dtype), k], axis=1)
    vals = jnp.concatenate([cv.astype(v.dtype), v], axis=1)
    qg = q.reshape(b, t, N_KV_HEADS, GROUP, HEAD_DIM)
    s = jnp.einsum('bqhgd,bkhd->bhgqk', qg, keys).astype(jnp.float32) * SCALE
    sk = sinks.astype(jnp.float32).reshape(N_KV_HEADS, GROUP)[None, :, :, None, None]
    p = sink_softmax(s, sk)
    o = jnp.einsum('bhgqk,bkhd->bqhgd', p.astype(v.dtype), vals)
    return o.reshape(b, t, ATTN_Q)


def hgrn2_blocks(q, logf, k, v, s0):
    bsz, t, h, dk = q.shape
    dv = v.shape[-1]
    pad = (-t) % HG_BLOCK
    if pad:
        pw = ((0, 0), (0, pad), (0, 0), (0, 0))
        q, logf, k, v = (jnp.pad(a, pw) for a in (q, logf, k, v))
    n = (t + pad) // HG_BLOCK

    def blocks(a):
        return jnp.moveaxis(a.reshape(bsz, n, HG_BLOCK, h, a.shape[-1]), 1, 0)

    causal = jnp.tril(jnp.ones((HG_BLOCK, HG_BLOCK), dtype=bool))[None, :, :, None, None]

    def step(s, blk):
        qb, gb, kb, vb = blk
        bc = jnp.cumsum(gb, axis=1)
        diff = jnp.where(causal, bc[:, :, None] - bc[:, None, :], NEG)
        att = jnp.einsum('blhk,blmhk,bmhk->bhlm', qb, jnp.exp(diff), kb)
        o = jnp.einsum('blhk,bhkv->blhv', qb * jnp.exp(bc), s) + jnp.einsum('bhlm,bmhv->blhv', att, vb)
        b_last = bc[:, -1]
        s_new = jnp.exp(b_last)[..., None] * s + jnp.einsum('blhk,blhv->bhkv', kb * jnp.exp(b_last[:, None] - bc), vb)
        return s_new, o

    s_fin, o = lax.scan(step, s0, (blocks(q), blocks(logf), blocks(k), blocks(v)))
    o = jnp.moveaxis(o, 0, 1).reshape(bsz, n * HG_BLOCK, h, dv)[:, :t]
    return o, s_fin


def rglru_scan(log_a, bx, h0):
    a = jnp.exp(log_a)
    b = jnp.sqrt(-jnp.expm1(2.0 * log_a)) * bx

    def combine(c1, c2):
        return c1[0] * c2[0], c2[0] * c1[1] + c2[1]

    a_cum, b_cum = lax.associative_scan(combine, (a, b), axis=1)
    return a_cum * h0[:, None] + b_cum


def mixer(h, p, lb, cache, win_rows):
    bsz, t, _ = h.shape
    dt = h.dtype
    idx = np.cumsum(SPLIT_SIZES)[:-1].tolist()
    (aq, ak, av, hq, hf, hi, hg, lx, lg, ga, gb, gc) = jnp.split(h @ p['w_in'], idx, axis=-1)

    q = rms_norm(aq.reshape(bsz, t, N_HEADS, HEAD_DIM), p['q_norm'])
    k = rms_norm(ak.reshape(bsz, t, N_KV_HEADS, HEAD_DIM), p['k_norm'])
    v = av.reshape(bsz, t, N_KV_HEADS, HEAD_DIM)
    if cache is None:
        ya = swa_prompt(q, k, v, p['attn_sinks'])
        k_rows, v_rows = k[:, t - win_rows:], v[:, t - win_rows:]
        s0 = jnp.zeros((bsz, HG_HEADS, HG_DK, HG_DV), jnp.float32)
        conv_buf = jnp.zeros((bsz, CONV_WIDTH - 1, LRU_WIDTH), dt)
        h0 = jnp.zeros((bsz, LRU_WIDTH), jnp.float32)
    else:
        ck, cv, s0, conv_buf, h0 = cache
        ya = swa_sample(q, k, v, ck, cv, p['attn_sinks'])
        k_rows, v_rows = k, v
        s0 = s0.astype(jnp.float32)
        conv_buf = conv_buf.astype(dt)
        h0 = h0.astype(jnp.float32)

    lbh = lb.reshape(HG_HEADS, HG_DK)
    hf32 = hf.astype(jnp.float32).reshape(bsz, t, HG_HEADS, HG_DK)
    f = lbh + (1.0 - lbh) * jax.nn.sigmoid(hf32)
    logf = jnp.maximum(jnp.log(jnp.maximum(f, 1e-26)), LOG_F_FLOOR)
    kk = (1.0 - lbh) * jax.nn.sigmoid(-hf32)
    qq = jax.nn.silu(hq).astype(jnp.float32).reshape(bsz, t, HG_HEADS, HG_DK)
    vv = hi.astype(jnp.float32).reshape(bsz, t, HG_HEADS, HG_DV)
    o, s_new = hgrn2_blocks(qq, logf, kk, vv, s0)
    o = rms_norm(o.astype(dt), p['hgrn_o_norm']) * jax.nn.silu(hg.reshape(bsz, t, HG_HEADS, HG_DV))
    yb = o.reshape(bsz, t, HG_IDIM)

    xpad = jnp.concatenate([conv_buf, lx], axis=1)
    xc = sum(xpad[:, j:j + t] * p['conv_w'][j] for j in range(CONV_WIDTH)) + p['conv_b']
    new_buf = xpad[:, t:]
    xr = xc.reshape(bsz, t, LRU_BLOCKS, LRU_BW)
    r = jax.nn.sigmoid(jnp.einsum('btnc,ncd->btnd', xr, p['lru_w_a']).reshape(bsz, t, LRU_WIDTH) + p['lru_b_a'])
    ig = jax.nn.sigmoid(jnp.einsum('btnc,ncd->btnd', xr, p['lru_w_x']).reshape(bsz, t, LRU_WIDTH) + p['lru_b_x'])
    log_a = -LRU_C * r.astype(jnp.float32) * jax.nn.softplus(-p['lru_lambda'].astype(jnp.float32))
    hs = rglru_scan(log_a, (ig * xc).astype(jnp.float32), h0)
    yc = hs.astype(dt) * jax.nn.gelu(lg)

    merged = (jax.nn.sigmoid(ga) * (ya @ p['w_attn_o'])
              + jax.nn.sigmoid(gb) * (yb @ p['w_hgrn_o'])
              + jax.nn.sigmoid(gc) * (yc @ p['w_lru_o']))
    out = merged @ p['w_out']
    return out, (k_rows, v_rows, s_new, new_buf, hs[:, -1])


def layer_forward(x, p, lb, cache, win_rows):
    x = x + 0.5 * swiglu(rms_norm(x, p['norm_ffn1']), p['w_ffn1_up'], p['w_ffn1_down'])
    m, st = mixer(rms_norm(x, p['norm_mix']), p, lb, cache, win_rows)
    x = x + m
    x = x + 0.5 * swiglu(rms_norm(x, p['norm_ffn2']), p['w_ffn2_up'], p['w_ffn2_down'])
    return x, st


def setup_inputs(seed: int = 0) -> dict:
    key = jax.random.key(seed)
    ks = iter(jax.random.split(key, 48))

    def nrm(shape, scale):
        return scale * jax.random.normal(next(ks), shape, jnp.float32)

    def gain(shape):
        return 1.0 + 0.01 * jax.random.normal(next(ks), shape, jnp.float32)

    win_rows = min(WINDOW, PAST_LEN)
    u = jax.random.uniform(next(ks), (DEPTH, LRU_WIDTH), jnp.float32, 0.9, 0.999)
    a_base = u ** (1.0 / LRU_C)
    lru_lambda = jnp.log(a_base) - jnp.log1p(-a_base)
    return {
        'x_prompt': nrm((BATCH, SEQ, D_MODEL), 1.0),
        'x_sample': nrm((DEC_BATCH, DEC_SEQ, D_MODEL), 1.0),
        'cache_attn_k': nrm((DEPTH, DEC_BATCH, win_rows, N_KV_HEADS, HEAD_DIM), 1.0),
        'cache_attn_v': nrm((DEPTH, DEC_BATCH, win_rows, N_KV_HEADS, HEAD_DIM), 1.0),
        'state_hgrn': nrm((DEPTH, DEC_BATCH, HG_HEADS, HG_DK, HG_DV), 0.3),
        'state_conv': nrm((DEPTH, DEC_BATCH, CONV_WIDTH - 1, LRU_WIDTH), 1.0),
        'state_lru': nrm((DEPTH, DEC_BATCH, LRU_WIDTH), 0.5),
        'norm_ffn1': gain((DEPTH, D_MODEL)),
        'w_ffn1_up': nrm((DEPTH, D_MODEL, 2 * D_FF), D_MODEL ** -0.5),
        'w_ffn1_down': nrm((DEPTH, D_FF, D_MODEL), D_FF ** -0.5),
        'norm_mix': gain((DEPTH, D_MODEL)),
        'w_in': nrm((DEPTH, D_MODEL, IN_COLS), D_MODEL ** -0.5),
        'q_norm': gain((DEPTH, HEAD_DIM)),
        'k_norm': gain((DEPTH, HEAD_DIM)),
        'attn_sinks': nrm((DEPTH, N_HEADS), 0.5),
        'w_attn_o': nrm((DEPTH, ATTN_Q, D_MODEL), ATTN_Q ** -0.5),
        'hgrn_lb_logits': nrm((DEPTH, HG_FDIM), 0.5),
        'hgrn_o_norm': gain((DEPTH, HG_DV)),
        'w_hgrn_o': nrm((DEPTH, HG_IDIM, D_MODEL), HG_IDIM ** -0.5),
        'conv_w': nrm((DEPTH, CONV_WIDTH, LRU_WIDTH), CONV_WIDTH ** -0.5),
        'conv_b': nrm((DEPTH, LRU_WIDTH), 0.01),
        'lru_w_a': nrm((DEPTH, LRU_BLOCKS, LRU_BW, LRU_BW), LRU_BW ** -0.5),
        'lru_b_a': nrm((DEPTH, LRU_WIDTH), 0.01),
        'lru_w_x': nrm((DEPTH, LRU_BLOCKS, LRU_BW, LRU_BW), LRU_BW ** -0.5),
        'lru_b_x': nrm((DEPTH, LRU_WIDTH), 0.01),
        'lru_lambda': lru_lambda,
        'w_lru_o': nrm((DEPTH, LRU_WIDTH, D_MODEL), LRU_WIDTH ** -0.5),
        'w_out': nrm((DEPTH, D_MODEL, D_MODEL), D_MODEL ** -0.5),
        'norm_ffn2': gain((DEPTH, D_MODEL)),
        'w_ffn2_up': nrm((DEPTH, D_MODEL, 2 * D_FF), D_MODEL ** -0.5),
        'w_ffn2_down': nrm((DEPTH, D_FF, D_MODEL), D_FF ** -0.5),
    }


def reference(x_prompt, x_sample, cache_attn_k, cache_attn_v, state_hgrn, state_conv, state_lru,
              norm_ffn1, w_ffn1_up, w_ffn1_down, norm_mix, w_in, q_norm, k_norm, attn_sinks, w_attn_o,
              hgrn_lb_logits, hgrn_o_norm, w_hgrn_o, conv_w, conv_b, lru_w_a, lru_b_a, lru_w_x, lru_b_x,
              lru_lambda, w_lru_o, w_out, norm_ffn2, w_ffn2_up, w_ffn2_down):
    win_rows = cache_attn_k.shape[2]
    lb_prob = jax.nn.softmax(hgrn_lb_logits.astype(jnp.float32), axis=0)
    lb_all = jnp.cumsum(lb_prob, axis=0) - lb_prob[0:1]

    xp, xs = x_prompt, x_sample
    st_p_all, st_s_all = [], []
    for l in range(DEPTH):
        p = {
            'norm_ffn1': norm_ffn1[l], 'w_ffn1_up': w_ffn1_up[l], 'w_ffn1_down': w_ffn1_down[l],
            'norm_mix': norm_mix[l], 'w_in': w_in[l], 'q_norm': q_norm[l], 'k_norm': k_norm[l],
            'attn_sinks': attn_sinks[l], 'w_attn_o': w_attn_o[l], 'hgrn_o_norm': hgrn_o_norm[l],
            'w_hgrn_o': w_hgrn_o[l], 'conv_w': conv_w[l], 'conv_b': conv_b[l], 'lru_w_a': lru_w_a[l],
            'lru_b_a': lru_b_a[l], 'lru_w_x': lru_w_x[l], 'lru_b_x': lru_b_x[l], 'lru_lambda': lru_lambda[l],
            'w_lru_o': w_lru_o[l], 'w_out': w_out[l], 'norm_ffn2': norm_ffn2[l],
            'w_ffn2_up': w_ffn2_up[l], 'w_ffn2_down': w_ffn2_down[l],
        }
        xp, st_p = layer_forward(xp, p, lb_all[l], None, win_rows)
        xs, st_s = layer_forward(xs, p, lb_all[l],
                                 (cache_attn_k[l], cache_attn_v[l], state_hgrn[l], state_conv[l], state_lru[l]),
                                 win_rows)
        st_p_all.append(st_p)
        st_s_all.append(st_s)

    def stack(sts, i):
        return jnp.stack([s[i] for s in sts], axis=0)

    return (xp, xs,
            stack(st_p_all, 0), stack(st_p_all, 1), stack(st_p_all, 2), stack(st_p_all, 3), stack(st_p_all, 4),
            stack(st_s_all, 0), stack(st_s_all, 1), stack(st_s_all, 2), stack(st_s_all, 3), stack(st_s_all, 4))
```

```python
import contextlib
import numpy as np
import concourse.bass as bass
import concourse.mybir as mybir
from concourse.bass_utils import run_bass_kernel_spmd

F32 = mybir.dt.float32
BF16 = mybir.dt.bfloat16
AF = mybir.ActivationFunctionType
ALU = mybir.AluOpType
ENGS = ("pe", "act", "dve", "pool", "sp")

D_MODEL = 1024
D_FF = 2816
SEQ = 4096
TTP = 512
NTILE = SEQ // TTP
NSS = 4
LS = 16
TTS = NSS * LS
IN_COLS = 10752
EPS = 1e-6
C_AQ, C_AK, C_AV, C_HQ, C_HF, C_HI, C_HG, C_LX, C_LG, C_GA, C_GB, C_GC = (
    0, 1024, 1280, 1536, 2560, 3584, 4608, 5632, 6656, 7680, 8704, 9728)


class Tracker:
    def __init__(self, nc, stack):
        self.nc = nc
        self.stack = stack
        self.streams = {e: [] for e in ENGS}
        self.sems = {}
        self.val = {}
        self.seen = {e: {} for e in ENGS}
        self.bufs = {}
        self.out_deps = []
        self.ranges = {}
        self.overl = {}
        for e in ("pe", "act", "dve", "pool"):
            self._sem(e)

    def _sem(self, key):
        if key not in self.sems:
            nm = "s_" + key.replace(":", "_")
            self.sems[key] = self.stack.enter_context(self.nc.semaphore(nm))
            self.val[key] = 0
        return self.sems[key]

    def set_range(self, name, off, end):
        if self.ranges.get(name) == (off, end):
            return
        if name in self.ranges:
            o0, e0 = self.ranges[name]
            off, end = min(off, o0), max(end, e0)
            if (off, end) == (o0, e0):
                return
            for n2 in self.overl[name]:
                self.overl[n2].remove(name)
        self.ranges[name] = (off, end)
        ov = []
        for n2, (o2, e2) in self.ranges.items():
            if n2 != name and o2 < end and off < e2:
                ov.append(n2)
                self.overl[n2].append(name)
        self.overl[name] = ov

    def _deps(self, reads, writes):
        deps = {}

        def add(d):
            if d is not None and deps.get(d[0], 0) < d[1]:
                deps[d[0]] = d[1]

        def allacc(b):
            st = self.bufs.get(b)
            if st:
                add(st[0])
                for r in st[1]:
                    add(r)

        for b in reads:
            st = self.bufs.get(b)
            if st:
                add(st[0])
                if b.startswith("pb"):
                    for r in st[1]:
                        add(r)
            for o in self.overl.get(b, ()):
                allacc(o)
        for b in writes:
            allacc(b)
            for o in self.overl.get(b, ()):
                allacc(o)
        return deps

    def _emit_waits(self, eng, deps):
        for k, v in deps.items():
            if eng == "pe" and k == "pe":
                continue
            if self.seen[eng].get(k, 0) >= v:
                continue
            self.seen[eng][k] = v
            sem = self.sems[k]
            self.streams[eng].append(I("wait_ge", sem, v))

    def _record(self, dep, reads, writes):
        for b in reads:
            st = self.bufs.setdefault(b, [None, []])
            st[1].append(dep)
            if len(st[1]) > 12:
                m = {}
                for k, v in st[1]:
                    m[k] = max(m.get(k, 0), v)
                st[1] = list(m.items())
        for b in writes:
            self.bufs[b] = [dep, []]

    mute = False

    def op(self, eng, fn, reads=(), writes=()):
        if self.mute:
            return
        deps = self._deps(reads, writes)
        self._emit_waits(eng, deps)
        self.val[eng] += 1
        n = self.val[eng]
        sem = self.sems[eng]
        self.streams[eng].append(lambda e, fn=fn, sem=sem: fn(e).then_inc(sem, 1))
        self._record((eng, n), reads, writes)

    def dma(self, q, semkey, fns, reads=(), writes=(), is_output=False):
        if self.mute:
            return
        key = "dma:" + semkey
        sem = self._sem(key)
        deps = self._deps(reads, writes)
        self._emit_waits(q, deps)
        for fn in fns:
            self.val[key] += 16
            self.streams[q].append(lambda e, fn=fn, sem=sem: fn(e).then_inc(sem, 16))
        dep = (key, self.val[key])
        self._record(dep, reads, writes)
        if is_output:
            self.out_deps.append(dep)

    def finish(self, eng="sp"):
        deps = {}
        for k, v in self.out_deps:
            deps[k] = max(deps.get(k, 0), v)
        for k, v in deps.items():
            sem = self.sems[k]
            self.streams[eng].append(I("wait_ge", sem, v))

    def emit(self):
        S = self.streams
        with self.nc.Block() as block:
            @block.tensor
            def _(e):
                for f in S["pe"]:
                    f(e)

            @block.scalar
            def _(e):
                for f in S["act"]:
                    f(e)

            @block.vector
            def _(e):
                for f in S["dve"]:
                    f(e)

            @block.gpsimd
            def _(e):
                for f in S["pool"]:
                    f(e)

            @block.sync
            def _(e):
                for f in S["sp"]:
                    f(e)


def I(name, *a, **k):
    return lambda e: getattr(e, name)(*a, **k)


def seq(fns):
    def f(e):
        r = None
        for g in fns:
            r = g(e)
        return r
    return f


def mm(out, lhsT, rhs, start=True, stop=True):
    return I("matmul", out, lhsT=lhsT, rhs=rhs, start=start, stop=stop)


CST = {}
_c = 0
for _n, _w in (("ident", 128), ("bd64", 128), ("ones", 128), ("onespad", 256), ("tri2", 128), ("tri4", 64),
               ("ssame", 256), ("rowm", 4), ("scanp", 512), ("scans", 64)):
    CST[_n] = (_c, _c + _w)
    _c += _w
NCST = _c


def make_consts():
    c = np.zeros((128, NCST), np.float32)
    p = np.arange(128)[:, None]

    def put(name, fn):
        a, b = CST[name]
        cc = np.arange(b - a)[None, :]
        c[:, a:b] = fn(p, cc).astype(np.float32)

    put("ident", lambda p, c_: p == c_)
    put("bd64", lambda p, c_: (p // 64) == (c_ // 64))
    put("ones", lambda p, c_: (p >= 0) & (c_ >= 0))
    put("onespad", lambda p, c_: np.where(c_ < 128, c_ < 64, (c_ - 128) >= 64) & (p >= 0))
    put("tri2", lambda p, c_: ((p // 64) == (c_ // 64)) & (p <= c_))
    put("tri4", lambda p, c_: ((p // 16) == (c_ // 16)) & (p <= c_))
    put("ssame", lambda p, c_: (p // 16) == ((c_ % 64) // 16))
    put("rowm", lambda p, c_: (p // 16) == c_)
    put("scanp", lambda p, c_: ((c_ % 64) != 0) & (p >= 0))
    put("scans", lambda p, c_: ((c_ % 16) != 0) & (p >= 0))
    return c


P_G1, P_GM, P_G2, P_LB, P_CW, P_CB, P_BA, P_BX, P_LAM, P_GQ, P_GK, P_SINK, P_GO = 0, 1, 2, 3, 4, 8, 9, 10, 11, 12, 13, 14, 15
NPR = 32


class _Stop(Exception):
    pass


def build(ntile_p=NTILE, do_sample=True, nlayer=2, stop=None):
    nc = bass.Bass("TRN2", target_bir_lowering=False)
    D = {}

    def din(name, shape):
        D[name] = nc.dram_tensor(name, list(shape), F32, kind="ExternalInput").ap()

    def dout(name, shape):
        D[name] = nc.dram_tensor(name, list(shape), F32, kind="ExternalOutput").ap()

    din("xp", (SEQ, 1024)); din("xs", (TTS, 1024))
    din("ck", (2, NSS, 128, 256)); din("cv", (2, NSS, 128, 256))
    din("shg", (2, NSS, 8, 128, 128)); din("scv", (2, NSS, 3, 1024)); din("slr", (2, NSS, 1024))
    din("w1u", (2, 1024, 2 * D_FF)); din("w1d", (2, D_FF, 1024)); din("win", (2, 1024, IN_COLS))
    din("wao", (2, 1024, 1024)); din("who", (2, 1024, 1024)); din("wlo", (2, 1024, 1024)); din("wout", (2, 1024, 1024))
    din("w2u", (2, 1024, 2 * D_FF)); din("w2d", (2, D_FF, 1024))
    din("lwa", (2, 16, 64, 64)); din("lwx", (2, 16, 64, 64))
    din("P", (NPR, 1024)); din("cst", (128, NCST))
    WB = {}
    WNAMES = ("w1u", "w1d", "win", "wao", "who", "wlo", "wout", "w2u", "w2d")
    for wn_ in WNAMES:
        WB[wn_] = nc.dram_tensor("wb_" + wn_, list(D[wn_].shape), BF16, kind="Internal").ap()
    dout("yp", (SEQ, 1024)); dout("ys", (TTS, 1024))
    dout("nkp", (2, 128, 256)); dout("nvp", (2, 128, 256)); dout("nhp", (2, 8, 128, 128))
    dout("ncp", (2, 3, 1024)); dout("nlp", (2, 1024))
    dout("nks", (2, TTS, 256)); dout("nvs", (2, TTS, 256)); dout("nhs", (2, NSS, 8, 128, 128))
    dout("ncs", (2, NSS, 3, 1024)); dout("nls", (2, NSS, 1024))
    import os
    DBG = os.environ.get("DBG")
    if DBG:
        dout("dbg", (128, 8, TTS))

    WQ_SP = bool(int(os.environ.get("WQ_SP", "1")))
    FINEPH = os.environ.get("FINEPH", "ffn,att,hgrn,lru").split(",")
    FINE = False
    CO = {"v": True}
    with contextlib.ExitStack() as st:
        T = Tracker(nc, st)

        def sb(name, shape, dt):
            return st.enter_context(nc.sbuf_tensor(name, list(shape), dt))

        cst32 = sb("cst32", (128, NCST), F32)
        cstb = sb("cstb", (128, NCST), BF16)
        Psb = sb("Psb", (NPR, 1024), F32)
        pcol = sb("pcol", (128, 8, NPR), F32)
        der = sb("der", (128, 2, 8, 8), F32)
        bda = sb("bda", (128, 2, 2, 8, 128), BF16)
        NSLOT = 5
        slots = [sb(f"ws{i}", (128, 4096), BF16) for i in range(NSLOT)]
        xin = [sb("xin0", (128, 1024), F32)] * 2
        xT = sb("xT", (128, 8, TTP), F32)
        hT = sb("hT", (128, 8, TTP), BF16)
        sq = [sb(f"sq{i}", (128, TTP), BF16) for i in range(2)]
        rstd = [sb(f"rstd{i}", (128, TTP), F32) for i in range(3)]
        merged = sb("merged", (128, 8, TTP), F32)
        gT = sb("gT", (128, 8, TTP), BF16)
        yT = sb("yT", (128, 8, TTP), BF16)
        khalo = sb("khalo", (128, 2, 2, 4, 128), BF16)
        vhalo = sb("vhalo", (128, 2, 4, 2, 128), BF16)
        Sst = sb("Sst", (128, 2, 8, 128), F32)
        lxhalo = sb("lxhalo", (128, 2, 8, 3), F32)
        hprev = sb("hprev", (128, 2, 8), F32)
        kstage = sb("kstage", (128, 4, 64), F32)
        vstage = sb("vstage", (128, 256), F32)
        ckd = [sb(f"ckd{i}", (128, 4, 2, 64), F32) for i in range(2)]
        Vc = [sb(f"Vc{i}", (128, 4, 2, 128), BF16) for i in range(2)]
        Ssm = sb("Ssm", (128, 8, 128), F32)
        h0s = sb("h0s", (128, NSS, 8), F32)
        cstage = sb("cstage", (128, 8, NSS, 3), F32)
        hstage = sb("hstage", (128, 8, NSS), F32)
        xhs = sb("xhs", (128, 8, NSS, 3), F32)
        if DBG:
            dbgst = sb("dbgst", (128, 8, TTS), F32)

        def dbg(name, l, kind, src, rnames):
            if DBG == name and l == 0 and kind == "s":
                T.op("dve", I("tensor_copy", out=dbgst[:], in_=src), reads=rnames, writes=["dbgst"])
                T.dma("sp", "dbg", [I("dma_start", out=D["dbg"], in_=dbgst[:])], reads=["dbgst"], is_output=True)

        bda32 = xin[0][:, :].rearrange("p (j d) -> p j d", d=128)
        ARENA = 41 * 1024
        arena = sb("arena", (128, ARENA), mybir.dt.uint8)
        pb = [st.enter_context(nc.psum_tensor(f"pb{i}", [128, 512], F32)) for i in range(8)]
        try:
            print("sbuf bytes remaining", nc.sbuf_bytes_remaining)
        except Exception as ex:
            print("sbuf remaining n/a", ex)

        def carve(name, off, shape, dt, sub=False, tname=None):
            esz = 4 if dt == F32 else 2
            n = int(np.prod(shape))
            tn = tname or name
            if CO["v"]:
                pass
            elif sub:
                rowb = (n // shape[0]) * esz
                for i_ in range(shape[0]):
                    T.set_range(f"{tn}{i_}", off + i_ * rowb, off + (i_ + 1) * rowb)
            elif tn != "-":
                T.set_range(tn, off, off + n * esz)
            v = arena[:, off:off + n * esz].bitcast(dt)
            if len(shape) > 1:
                names = "abcd"[:len(shape)]
                pat = "p (" + " ".join(names) + ") -> p " + " ".join(names)
                v = v.rearrange(pat, **{names[i]: shape[i] for i in range(1, len(shape))})
            return v, off + n * esz

        def reg(names, off, end):
            if CO["v"]:
                for nm in names:
                    T.set_range(nm, off, end)

        def cs(name, rows=slice(None)):
            a, b = CST[name]
            return cstb[rows, a:b]

        def cs32(name, rows=slice(None)):
            a, b = CST[name]
            return cst32[rows, a:b]

        rot = {"i": 0}

        def nextbank():
            rot["i"] = (rot["i"] + 1) % 4
            return rot["i"]

        stt = {"i": 0}

        def nextstat():
            stt["i"] ^= 1
            return (4 + stt["i"], rstd[0] if stt["i"] else rstd[2], "rstd0" if stt["i"] else "rstd2")

        def rsqrt_from(bank, r_, rn, TT, scale):
            T.op("act", I("activation", out=r_[:, 0:TT], in_=pb[bank][:, 0:TT], func=AF.Ln, scale=scale, bias=EPS), reads=[f"pb{bank}"], writes=[rn])
            T.op("act", I("activation", out=r_[:, 0:TT], in_=r_[:, 0:TT], func=AF.Exp, scale=-0.5), reads=[rn], writes=[rn])

        def seg(k):
            return stop is None or stop >= 0 or k <= -stop

        T.dma("sp", "cst", [I("dma_start", out=cst32[:], in_=D["cst"])], writes=["cst32"])
        T.dma("sp", "P", [I("dma_start", out=Psb[:], in_=D["P"])], writes=["Psb"])
        T.op("dve", I("tensor_copy", out=cstb[:], in_=cst32[:]), reads=["cst32"], writes=["cstb"])
        for j in (range(8) if seg(2) else ()):
            T.op("pe", I("transpose", out=pb[5][:, j * 32:(j + 1) * 32], in_=Psb[:, j * 128:(j + 1) * 128],
                                                 identity=cs32("ident", slice(0, NPR))[:, 0:NPR]),
                 reads=["Psb", "cst32"], writes=["pb5"])
        if seg(2):
            T.op("dve", I("tensor_copy", out=pcol[:], in_=pb[5][:, 0:8 * NPR].rearrange("p (j v) -> p j v", v=NPR)),
                 reads=["pb5"], writes=["pcol"])

        def pc(l, row, j):
            return pcol[:, j, 16 * l + row:16 * l + row + 1]

        def pcv(l, row):
            return pcol[:, :, 16 * l + row]

        for l in (range(2) if seg(3) else ()):
            T.op("act", I("activation", out=der[:, l, 0, :], in_=pcv(l, P_SINK), func=AF.Exp),
                 reads=["pcol"], writes=[f"der{l}0"])
            T.op("act", I("activation", out=der[:, l, 6, :], in_=pcv(l, P_LAM), func=AF.Exp, scale=-1.0),
                 reads=["pcol"], writes=[f"der{l}6"])
            T.op("act", I("activation", out=der[:, l, 7, :], in_=der[:, l, 6, :], func=AF.Ln, bias=1.0),
                 reads=[f"der{l}6"], writes=[f"der{l}7"])
            T.op("dve", I("tensor_scalar", out=der[:, l, 4, :], in0=der[:, l, 7, :], scalar1=-8.0, scalar2=None, op0=ALU.mult),
                 reads=[f"der{l}7"], writes=[f"der{l}4"])
            T.op("dve", I("tensor_scalar", out=der[:, l, 5, :], in0=der[:, l, 7, :], scalar1=-16.0, scalar2=None, op0=ALU.mult),
                 reads=[f"der{l}7"], writes=[f"der{l}5"])
        T.mute = not seg(4)
        L0, L1 = pcv(0, P_LB), pcv(1, P_LB)
        tA, tB, tC, tD = der[:, 0, 6, :], der[:, 0, 7, :], der[:, 1, 6, :], der[:, 1, 7, :]
        T.op("dve", I("tensor_tensor", out=tA, in0=L0, in1=L1, op=ALU.max), reads=["pcol", "der06", "der16"], writes=["lbA"])
        T.op("dve", I("tensor_tensor", out=tB, in0=L0, in1=tA, op=ALU.subtract), reads=["pcol", "lbA", "der07"], writes=["lbB"])
        T.op("dve", I("tensor_tensor", out=tC, in0=L1, in1=tA, op=ALU.subtract), reads=["pcol", "lbA", "der17"], writes=["lbC"])
        T.op("act", I("activation", out=tB, in_=tB, func=AF.Exp), reads=["lbB"], writes=["lbB"])
        T.op("act", I("activation", out=tC, in_=tC, func=AF.Exp), reads=["lbC"], writes=["lbC"])
        T.op("dve", I("tensor_tensor", out=tA, in0=tB, in1=tC, op=ALU.add), reads=["lbB", "lbC"], writes=["lbA"])
        T.op("dve", I("reciprocal", out=tA, in_=tA), reads=["lbA"], writes=["lbA"])
        T.op("dve", I("tensor_tensor", out=tB, in0=tB, in1=tA, op=ALU.mult), reads=["lbB", "lbA"], writes=["lbB"])
        T.op("dve", I("tensor_tensor", out=tC, in0=tC, in1=tA, op=ALU.mult), reads=["lbC", "lbA"], writes=["lbC"])
        T.op("dve", I("tensor_tensor", out=tD, in0=tB, in1=tC, op=ALU.add), reads=["lbB", "lbC"], writes=["lbD"])
        T.op("dve", I("tensor_tensor", out=der[:, 0, 1, :], in0=tB, in1=tB, op=ALU.subtract), reads=["lbB"], writes=["der01"])
        T.op("dve", I("tensor_tensor", out=der[:, 1, 1, :], in0=tD, in1=tB, op=ALU.subtract), reads=["lbD", "lbB"], writes=["der11"])
        for l in range(2):
            T.op("dve", I("tensor_scalar", out=der[:, l, 2, :], in0=der[:, l, 1, :], scalar1=-1.0, scalar2=1.0, op0=ALU.mult, op1=ALU.add),
                 reads=[f"der{l}1"], writes=[f"der{l}2"])
            T.op("dve", I("tensor_scalar", out=der[:, l, 3, :], in0=der[:, l, 1, :], scalar1=-1.0, scalar2=None, op0=ALU.add),
                 reads=[f"der{l}1"], writes=[f"der{l}3"])
        DER_ALL = [f"der{l}{k}" for l in range(2) for k in range(6)]

        def dcol(l, kind, j):
            return der[:, l, kind, j:j + 1]

        T.mute = not seg(5)
        for l in range(2):
            for gi, wn in enumerate(("lwa", "lwx")):
                T.op("dve", I("memset", bda32, 0.0), writes=["bda32"])
                src = D[wn][l].rearrange("(j two) c d -> two c j d", two=2)
                T.dma("sp", "bda32", [I("dma_start", out=bda32[0:64, :, 0:64], in_=src[0]),
                                      I("dma_start", out=bda32[64:128, :, 64:128], in_=src[1])],
                      writes=["bda32"])
                T.op("dve", I("tensor_copy", out=bda[:, l, gi, :, :], in_=bda32), reads=["bda32"], writes=["bda", "xin0"])
        T.mute = not seg(6)
        T.op("dve", I("memset", Sst[:], 0.0), writes=[f"Sst{l_}h{h_}" for l_ in range(2) for h_ in range(8)])
        T.op("dve", I("memset", lxhalo[:], 0.0), writes=["lxhalo0", "lxhalo1"])
        T.op("dve", I("memset", hprev[:], 0.0), writes=["hprev0", "hprev1"])
        T.op("dve", I("memset", khalo[:], 0.0), writes=["khalo0", "khalo1"])
        T.op("dve", I("memset", vhalo[:], 0.0), writes=["vhalo0", "vhalo1"])

        T.mute = False
        def layer_blocks(l):
            bl = []
            for f in ("1", "2"):
                pass
            def ffn(tag):
                r = [("up" + tag, l, b) for b in range(11)] + [("dn" + tag, l, j) for j in range(8)]
                return r
            bl += ffn("1")
            bl += [("in", l, C_AQ), ("in", l, C_AQ + 512), ("kdup", l, 0), ("in256", l, C_AV), ("in", l, C_GA), ("in", l, C_GA + 512),
                   ("sq", l, "wao", 0), ("sq", l, "wao", 512)]
            for half in range(2):
                bl += [("in", l, C_HQ + 512 * half), ("in", l, C_HF + 512 * half), ("in", l, C_HI + 512 * half), ("in", l, C_HG + 512 * half)]
            bl += [("in", l, C_GB), ("in", l, C_GB + 512), ("sq", l, "who", 0), ("sq", l, "who", 512)]
            for half in range(2):
                bl += [("in", l, C_LX + 512 * half), ("in", l, C_LG + 512 * half)]
            bl += [("in", l, C_GC), ("in", l, C_GC + 512), ("sq", l, "wlo", 0), ("sq", l, "wlo", 512),
                   ("sq", l, "wout", 0), ("sq", l, "wout", 512)]
            bl += ffn("2")
            return bl

        tiles = [("p", t) for t in range(ntile_p)] + ([("s", 0)] if do_sample else [])
        wsched = []
        for _ in tiles:
            for l in range(nlayer):
                wsched += layer_blocks(l)
        wstate = {"i": 0, "issued": 0}

        cast_done = set()

        def cast_weights(l):
            if l in cast_done:
                return
            cast_done.add(l)
            for wn_ in WNAMES:
                rows = D[wn_].shape[1]
                fns = [I("dma_start", out=WB[wn_][l, r0:r0 + 128, :], in_=D[wn_][l, r0:r0 + 128, :]) for r0 in range(0, rows, 128)]
                T.dma("pool", f"wb_{wn_}{l}", fns, writes=[f"wb_{wn_}{l}"])

        def w_src(desc):
            kind, l = desc[0], desc[1]
            if kind.startswith("up"):
                return ("w1u" if kind == "up1" else "w2u")
            if kind.startswith("dn"):
                return ("w1d" if kind == "dn1" else "w2d")
            if kind in ("in", "in256", "kdup"):
                return "win"
            return desc[2]

        def w_dma(desc, slot):
            s = slots[slot]
            kind = desc[0]
            l = desc[1]
            D = WB
            if kind.startswith("up"):
                W = D["w1u" if kind == "up1" else "w2u"][l].rearrange("(kc p) c -> p kc c", p=128)
                b = desc[2]
                v = s[:, 0:4096].rearrange("p (k two c) -> p k two c", two=2, c=256)
                return [I("dma_start", out=v[:, :, 0, :], in_=W[:, :, b * 256:(b + 1) * 256]),
                        I("dma_start", out=v[:, :, 1, :], in_=W[:, :, D_FF + b * 256:D_FF + (b + 1) * 256])]
            if kind.startswith("dn"):
                W = D["w1d" if kind == "dn1" else "w2d"][l].rearrange("(kc p) c -> p kc c", p=128)
                j = desc[2]
                v = s[:, 0:22 * 128].rearrange("p (k c) -> p k c", c=128)
                return [I("dma_start", out=v, in_=W[:, :, j * 128:(j + 1) * 128])]
            if kind == "in":
                W = D["win"][l].rearrange("(kc p) c -> p kc c", p=128)
                c0 = desc[2]
                v = s[:, 0:4096].rearrange("p (k c) -> p k c", c=512)
                return [I("dma_start", out=v, in_=W[:, :, c0:c0 + 512])]
            if kind == "in256":
                W = D["win"][l].rearrange("(kc p) c -> p kc c", p=128)
                c0 = desc[2]
                v = s[:, 0:2048].rearrange("p (k c) -> p k c", c=256)
                return [I("dma_start", out=v, in_=W[:, :, c0:c0 + 256])]
            if kind == "kdup":
                W = D["win"][l].rearrange("(kc p) c -> p kc c", p=128)
                v = s[:, 0:8 * 384].rearrange("p (k c) -> p k c", c=384)
                return [I("dma_start", out=v, in_=W[:, :, C_AK - 64:C_AK + 320])]
            if kind == "sq":
                W = D[desc[2]][l].rearrange("(kc p) c -> p kc c", p=128)
                c0 = desc[3]
                v = s[:, 0:4096].rearrange("p (k c) -> p k c", c=512)
                return [I("dma_start", out=v, in_=W[:, :, c0:c0 + 512])]
            raise ValueError(kind)

        def wnext(*tag):
            i = wstate["i"]
            assert wsched[i] == tuple(tag), (i, wsched[i], tag)
            while wstate["issued"] < min(len(wsched), i + NSLOT - 1):
                k = wstate["issued"]
                cast_weights(wsched[k][1])
                T.dma("sp" if WQ_SP else "pool", f"ws{k % NSLOT}", w_dma(wsched[k], k % NSLOT),
                      reads=[f"wb_{w_src(wsched[k])}{wsched[k][1]}"], writes=[f"ws{k % NSLOT}"])
                wstate["issued"] += 1
            wstate["i"] += 1
            return i % NSLOT

        cnt = {"n": 0}

        def alt():
            cnt["n"] += 1
            return cnt["n"] % 2

        def rmsnorm(l, prow, TT, xnames):
            bank, r_, rn = nextstat()
            fns = []
            for j in range(8):
                s_ = sq[j % 2]
                T.op("act", I("activation", out=s_[:, 0:TT], in_=xT[:, j, 0:TT], func=AF.Square),
                     reads=[f"xT{j}"], writes=[f"sq{j % 2}"])
                T.op("pe", mm(pb[bank][:, 0:TT], cs("ones"), s_[:, 0:TT], start=(j == 0), stop=(j == 7)),
                     reads=[f"sq{j % 2}", "cstb"], writes=[f"pb{bank}"])
            rsqrt_from(bank, r_, rn, TT, 1.0 / 1024)
            for j in range(8):
                T.op("dve", I("scalar_tensor_tensor", out=hT[:, j, 0:TT], in0=xT[:, j, 0:TT], scalar=pc(l, prow, j),
                                                                in1=r_[:, 0:TT], op0=ALU.mult, op1=ALU.mult),
                     reads=[f"xT{j}", rn, "pcol"], writes=[f"hT{j}"])

        HT = [f"hT{j}" for j in range(8)]

        def proj(slot, lhs_fn, TT, rhs_t, rhs_names, nk=8, bank=None):
            b = nextbank() if bank is None else bank
            fns = [mm(pb[b][:, 0:TT], lhs_fn(kc), rhs_t[:, kc, 0:TT], start=(kc == 0), stop=(kc == nk - 1)) for kc in range(nk)]
            T.op("pe", seq(fns), reads=[f"ws{slot}"] + rhs_names, writes=[f"pb{b}"])
            return b

        def wv512(slot):
            return slots[slot][:, 0:4096].rearrange("p (k c) -> p k c", c=512)

        def ffn(l, tag, prow, TT, aT, sgb):
            rmsnorm(l, prow, TT, None)
            for b in range(11):
                slot = wnext("up" + tag, l, b)
                v = slots[slot][:, 0:4096].rearrange("p (k two c) -> p k two c", two=2, c=256)
                for jj in range(2):
                    f = 2 * b + jj
                    bg = proj(slot, lambda kc: v[:, kc, 0, jj * 128:(jj + 1) * 128], TT, hT, HT)
                    bv = proj(slot, lambda kc: v[:, kc, 1, jj * 128:(jj + 1) * 128], TT, hT, HT)
                    s_ = sgb[f % 2]
                    T.op("act", I("activation", out=s_[:, 0:TT], in_=pb[bg][:, 0:TT], func=AF.Silu),
                         reads=[f"pb{bg}"], writes=[f"sg{f % 2}"])
                    T.op("dve", I("tensor_tensor", out=aT[:, f, 0:TT], in0=pb[bv][:, 0:TT], in1=s_[:, 0:TT], op=ALU.mult),
                         reads=[f"pb{bv}", f"sg{f % 2}"], writes=[f"aT{f}"])
            AT = [f"aT{f}" for f in range(22)]
            for j in range(8):
                slot = wnext("dn" + tag, l, j)
                v = slots[slot][:, 0:22 * 128].rearrange("p (k c) -> p k c", c=128)
                b = proj(slot, lambda kc: v[:, kc, :], TT, aT, AT, nk=22)
                T.op("dve", I("scalar_tensor_tensor", out=xT[:, j, 0:TT], in0=pb[b][:, 0:TT], scalar=0.5, in1=xT[:, j, 0:TT],
                                                                     op0=ALU.mult, op1=ALU.add),
                     reads=[f"pb{b}", f"xT{j}"], writes=[f"xT{j}"])

        def gate_and_out(l, gcol, wname, TT, first):
            for half in range(2):
                slot = wnext("in", l, gcol + 512 * half)
                v = wv512(slot)
                for jj in range(4):
                    j = 4 * half + jj
                    b = proj(slot, lambda kc: v[:, kc, jj * 128:(jj + 1) * 128], TT, hT, HT)
                    T.op("act", I("activation", out=gT[:, j, 0:TT], in_=pb[b][:, 0:TT], func=AF.Sigmoid),
                         reads=[f"pb{b}"], writes=[f"gT{j}"])
            YT = [f"yT{j}" for j in range(8)]
            for half in range(2):
                slot = wnext("sq", l, wname, 512 * half)
                v = wv512(slot)
                for jj in range(4):
                    j = 4 * half + jj
                    b = proj(slot, lambda kc: v[:, kc, jj * 128:(jj + 1) * 128], TT, yT, YT)
                    if first:
                        T.op("dve", I("tensor_tensor", out=merged[:, j, 0:TT], in0=pb[b][:, 0:TT], in1=gT[:, j, 0:TT], op=ALU.mult),
                             reads=[f"pb{b}", f"gT{j}"], writes=[f"mg{j}"])
                    else:
                        r_ = rstd[1]
                        T.op("dve", I("tensor_tensor", out=r_[:, 0:TT], in0=pb[b][:, 0:TT], in1=gT[:, j, 0:TT], op=ALU.mult),
                             reads=[f"pb{b}", f"gT{j}"], writes=["rstd1"])
                        T.op("dve", I("tensor_tensor", out=merged[:, j, 0:TT], in0=merged[:, j, 0:TT], in1=r_[:, 0:TT], op=ALU.add),
                             reads=["rstd1", f"mg{j}"], writes=[f"mg{j}"])

        def attention(l, kind, tix, TT):
            import os
            CO["v"] = (kind == "s") or ("att" not in FINEPH)
            off = 0
            qT, off = carve("qT", off, (8, TTP), BF16, sub=True)
            kz0 = off
            kZ, off = carve("kZ", off, (2, 4, 128 + TTP), BF16, tname="-")
            for g_ in range(4):
                if not CO["v"]:
                    T.set_range(f"kT{g_}", kz0, off)
                    T.set_range(f"kTh{g_}", kz0, off)
            Vpad, off = carve("Vpad", off, (5, 4, 2, 128), BF16, sub=True, tname="Vp")
            Eb = []
            for i in range(2):
                v_, off = carve("E", off, (2, 512), BF16, tname=f"E{i}")
                Eb.append(v_)
            dtmp = []
            for i in range(2):
                v_, off = carve("dtmp", off, (256,), F32, tname=f"dtmp{i}")
                dtmp.append(v_)
            kf32, off = carve("kf32", off, (4, 128), F32)
            kcT = []
            for i in range(2):
                v_, off = carve("kcT", off, (2, 4, 128), BF16, tname=f"kcT{i}"); kcT.append(v_)
            assert off <= ARENA, off
            names = ([f"qT{j}" for j in range(8)] + [f"kT{g}" for g in range(4)] + [f"kTh{g}" for g in range(4)] + [f"Vp{b}" for b in range(5)]
                     + ["E0", "E1", "dtmp0", "dtmp1", "kf32", "kcT0", "kcT1"])
            reg(names, 0, off)
            nblk = max(TT // 128, 1)
            is_last_p = (kind == "p" and tix == NTILE - 1)
            want_kout = ((kind == "s") or is_last_p) and not os.environ.get("NOKOUT")

            if kind == "p":
                T.op("dve", I("tensor_copy", out=kZ[:, :, :, 0:128], in_=khalo[:, l, :, :, :]), reads=[f"khalo{l}"], writes=[f"kTh{g}" for g in range(4)])
                T.op("dve", I("tensor_copy", out=Vpad[:, 0, :, :, :], in_=vhalo[:, l, :, :, :]), reads=[f"vhalo{l}"], writes=["Vp0"])
            else:
                T.op("dve", I("memset", Vpad[:, 0, :, :, :], 0.0), writes=["Vp0"])
                T.op("dve", I("memset", Vpad[:, 1, :, :, :], 0.0), writes=["Vp1"])
            if want_kout:
                T.op("dve", I("memset", kf32[:, :, :], 0.0), writes=["kf32"])
            T.op("dve", I("memset", kZ[64:128, 0, :, 128:128 + TT], 0.0), writes=[f"kT{g}" for g in range(4)])
            T.op("dve", I("memset", kZ[0:64, 1, :, 128:128 + TT], 0.0), writes=[f"kT{g}" for g in range(4)])
            if kind == "p" and tix == 0 and l == 0:
                pass
            for b_ in range(1, 5):
                if kind == "p":
                    T.op("dve", I("memset", Vpad[:, b_, :, 0, 64:128], 0.0), writes=[f"Vp{b_}"])
                    T.op("dve", I("memset", Vpad[:, b_, :, 1, 0:64], 0.0), writes=[f"Vp{b_}"])

            def qknorm(b, dst, gidx, dnames):
                s_ = sq[alt()]
                sn = "sq0" if s_ is sq[0] else "sq1"
                T.op("act", I("activation", out=s_[:, 0:TT], in_=pb[b][:, 0:TT], func=AF.Square), reads=[f"pb{b}"], writes=[sn])
                bank, r_, rn = nextstat()
                T.op("pe", mm(pb[bank][:, 0:TT], cs("bd64"), s_[:, 0:TT]), reads=[sn, "cstb"], writes=[f"pb{bank}"])
                rsqrt_from(bank, r_, rn, TT, 1.0 / 64)
                dl = dst if isinstance(dst, list) else [(dst, slice(0, 128))]
                for (d_ap, rows) in dl:
                    T.op("dve", I("scalar_tensor_tensor", out=d_ap, in0=pb[b][rows, 0:TT], scalar=pcol[rows, 0, 16 * l + gidx:16 * l + gidx + 1],
                                                                                   in1=r_[rows, 0:TT], op0=ALU.mult, op1=ALU.mult),
                         reads=[f"pb{b}", rn, "pcol"], writes=dnames)
                return r_, rn

            import os
            SUB = int(os.environ.get("SUB", "99"))

            def ck_(n):
                if n > SUB:
                    T.mute = True

            for half in range(2):
                slot = wnext("in", l, C_AQ + 512 * half)
                v = wv512(slot)
                for jj in range(4):
                    j = 4 * half + jj
                    b = proj(slot, lambda kc: v[:, kc, jj * 128:(jj + 1) * 128], TT, hT, HT)
                    qknorm(b, qT[:, j, 0:TT], P_GQ, [f"qT{j}"])
            ck_(1)
            slot = wnext("kdup", l, 0)
            v = slots[slot][:, 0:8 * 384].rearrange("p (k c) -> p k c", c=384)
            for g in range(4):
                blo = proj(slot, lambda kc: v[:, kc, 64 + g * 64:64 + g * 64 + 128], TT, hT, HT)
                qknorm(blo, [(kZ[0:64, 0, g, 128:128 + TT], slice(0, 64))], P_GK, [f"kT{g}"])
                b = proj(slot, lambda kc: v[:, kc, g * 64:g * 64 + 128], TT, hT, HT)
                r_, rn = qknorm(b, [(kZ[64:128, 1, g, 128:128 + TT], slice(64, 128))], P_GK, [f"kT{g}"])
                if want_kout:
                    n0 = TT - 128 if kind == "p" else 0
                    nn = 128 if kind == "p" else TT
                    T.op("dve", I("scalar_tensor_tensor",
                        out=kf32[64:128, g, 0:nn], in0=pb[b][64:128, n0:n0 + nn], scalar=pcol[64:128, 0, 16 * l + P_GK:16 * l + P_GK + 1],
                        in1=r_[64:128, n0:n0 + nn], op0=ALU.mult, op1=ALU.mult),
                        reads=[f"pb{b}", rn, "pcol"], writes=["kf32"])
                    bt = nextbank()
                    T.op("pe", I("transpose", out=pb[bt][0:nn, 0:128], in_=kf32[:, g, 0:nn], identity=cs32("ident")),
                         reads=["kf32", "cst32"], writes=[f"pb{bt}"])
                    T.op("act", I("activation", out=kstage[0:nn, g, :], in_=pb[bt][0:nn, 64:128], func=AF.Copy),
                         reads=[f"pb{bt}"], writes=["kstage"])
            if want_kout:
                nn = 128 if kind == "p" else TT
                dst = D["nkp"][l] if kind == "p" else D["nks"][l]
                T.dma("sp", "okst", [I("dma_start", out=dst.rearrange("t (g d) -> t g d", g=4), in_=kstage[0:nn, :, :])],
                      reads=["kstage"], is_output=True)
            ck_(2)
            slot = wnext("in256", l, C_AV)
            vv = slots[slot][:, 0:2048].rearrange("p (k c) -> p k c", c=256)
            for bk in range(nblk):
                nt = min(128, TT)
                b = nextbank()
                fns = [mm(pb[b][0:nt, 0:256], hT[:, kc, bk * 128:bk * 128 + nt], vv[:, kc, :], start=(kc == 0), stop=(kc == 7)) for kc in range(8)]
                T.op("pe", seq(fns), reads=[f"ws{slot}"] + HT, writes=[f"pb{b}"])
                src = pb[b][0:nt, 0:256].rearrange("p (g d) -> p g d", g=4)
                T.op("act", I("activation", out=Vpad[0:nt, bk + 1, :, 0, 0:64], in_=src, func=AF.Copy),
                     reads=[f"pb{b}"], writes=[f"Vp{bk + 1}"])
                T.op("dve", I("tensor_copy", out=Vpad[0:nt, bk + 1, :, 1, 64:128], in_=src),
                     reads=[f"pb{b}"], writes=[f"Vp{bk + 1}"])
                if kind == "s" or (is_last_p and bk == nblk - 1):
                    T.op("act", I("activation", out=vstage[0:nt, :], in_=pb[b][0:nt, 0:256], func=AF.Copy),
                         reads=[f"pb{b}"], writes=["vstage"])
                    dst = D["nvp"][l] if kind == "p" else D["nvs"][l]
                    T.dma("sp", "ovst", [I("dma_start", out=dst, in_=vstage[0:nt, :])], reads=["vstage"], is_output=True)

            ck_(3)
            esk = lambda j: dcol(l, 0, j)
            ones_lo = cs("onespad")[:, 0:128]
            ones_hi = cs("onespad")[:, 128:256]

            def finish_pair(g, bo, c0, n, ei):
                d_ = dtmp[ei]
                for jj in range(2):
                    T.op("act", I("activation", out=d_[:, jj * 128:jj * 128 + n], in_=pb[bo][:, 256 + jj * 128:256 + jj * 128 + n],
                                  func=AF.Ln, bias=esk(2 * g + jj)),
                         reads=[f"pb{bo}"] + DER_ALL, writes=[f"dtmp{ei}"])
                    T.op("act", I("activation", out=d_[:, jj * 128:jj * 128 + n], in_=d_[:, jj * 128:jj * 128 + n], func=AF.Exp, scale=-1.0),
                         reads=[f"dtmp{ei}"], writes=[f"dtmp{ei}"])
                for jj in range(2):
                    T.op("dve", I("tensor_tensor", out=yT[:, 2 * g + jj, c0:c0 + n], in0=pb[bo][:, jj * 128:jj * 128 + n],
                                                               in1=d_[:, jj * 128:jj * 128 + n], op=ALU.mult),
                         reads=[f"pb{bo}", f"dtmp{ei}"], writes=[f"yT{2 * g + jj}"])

            if kind == "p":
                its = [(g, pt) for g in range(4) for pt in range(TT // 128)]

                def emit_scores(i):
                    g, pt = its[i]
                    ei = i % 2
                    E = Eb[ei]
                    en = f"E{ei}"
                    hasA = not (tix == 0 and pt == 0)
                    acol = pt * 128
                    bcol = 128 + pt * 128
                    for which, kc0, use in ((0, acol, hasA), (1, bcol, True)):
                        if not use:
                            continue
                        bs = nextbank()
                        fns = []
                        for hh in range(4):
                            j = 2 * g + hh // 2
                            fns.append(mm(pb[bs][:, hh * 128:(hh + 1) * 128], kZ[:, hh % 2, g, kc0:kc0 + 128],
                                          qT[:, j, pt * 128:(pt + 1) * 128]))
                        T.op("pe", seq(fns), reads=[f"kT{g}", f"kTh{g}", f"qT{2 * g}", f"qT{2 * g + 1}"], writes=[f"pb{bs}"])
                        T.op("act", I("activation", out=E[:, which, :], in_=pb[bs][:, :], func=AF.Exp, scale=0.125),
                             reads=[f"pb{bs}"], writes=[en])
                        if which == 0:
                            T.op("dve", I("memset", E[0:64, 0, :].rearrange("p (h q) -> p h q", h=4)[:, :, 64:128], 0.0), writes=[en])
                        else:
                            T.op("dve", I("memset", E[64:128, 1, :].rearrange("p (h q) -> p h q", h=4)[:, :, 0:64], 0.0), writes=[en])

                def emit_pv(i):
                    g, pt = its[i]
                    ei = i % 2
                    E = Eb[ei]
                    en = f"E{ei}"
                    hasA = not (tix == 0 and pt == 0)
                    blkA = pt
                    blkB = pt + 1
                    bo = nextbank()
                    fns = []
                    for isden in (0, 1):
                        for jj in range(2):
                            oc = isden * 256 + jj * 128
                            terms = []
                            for hl in range(2):
                                hh = 2 * jj + hl
                                if hasA:
                                    lhs = (ones_lo if hl == 0 else ones_hi) if isden else Vpad[:, blkA, g, hl, :]
                                    terms.append((lhs, E[:, 0, hh * 128:(hh + 1) * 128]))
                                lhs = (ones_lo if hl == 0 else ones_hi) if isden else Vpad[:, blkB, g, hl, :]
                                terms.append((lhs, E[:, 1, hh * 128:(hh + 1) * 128]))
                            for ti, (lhs, rhs) in enumerate(terms):
                                fns.append(mm(pb[bo][:, oc:oc + 128], lhs, rhs, start=(ti == 0), stop=(ti == len(terms) - 1)))
                    T.op("pe", seq(fns), reads=[en, f"Vp{blkA}", f"Vp{blkB}", "cstb"], writes=[f"pb{bo}"])
                    finish_pair(g, bo, pt * 128, 128, ei)

                emit_scores(0)
                for i in range(len(its)):
                    if i + 1 < len(its):
                        emit_scores(i + 1)
                    emit_pv(i)
                T.op("dve", I("tensor_copy", out=khalo[:, l, :, :, :], in_=kZ[:, :, :, TT:TT + 128]), reads=[f"kT{g}" for g in range(4)], writes=[f"khalo{l}"])
                T.op("dve", I("tensor_copy", out=vhalo[:, l, :, :, :], in_=Vpad[:, 4, :, :, :]), reads=["Vp4"], writes=[f"vhalo{l}"])
            else:
                for s in range(NSS):
                    i2 = s % 2
                    T.dma("sp", f"ckd{i2}", [I("dma_start", out=ckd[s % 2][:, :, h, :], in_=D["ck"][l, s].rearrange("r (g d) -> r g d", g=4))
                                            for h in range(2)], writes=[f"ckd{i2}"])
                    SKIP = os.environ.get("SKIP", "")
                    if "vc" in SKIP:
                        T.mute = True
                    T.op("dve", I("memset", Vc[i2][:], 0.0), writes=[f"Vc{i2}"])
                    T.dma("pool", f"Vc{i2}", [I("dma_start", out=Vc[s % 2][:, :, 0, 0:64], in_=D["cv"][l, s].rearrange("r (g d) -> r g d", g=4)),
                                             I("dma_start", out=Vc[s % 2][:, :, 1, 64:128], in_=D["cv"][l, s].rearrange("r (g d) -> r g d", g=4))],
                          writes=[f"Vc{i2}"])
                    T.mute = ("tr" in SKIP)
                    T.op("dve", I("memset", kcT[i2][64:128, 0, :, :], 0.0), writes=[f"kcT{i2}"])
                    T.op("dve", I("memset", kcT[i2][0:64, 1, :, :], 0.0), writes=[f"kcT{i2}"])
                    for g in range(4):
                        bt = nextbank()
                        T.op("pe", I("transpose", out=pb[bt][:, 0:128], in_=ckd[i2][:, g, :, :].rearrange("p h d -> p (h d)"),
                                                                         identity=cs32("ident")),
                             reads=[f"ckd{i2}", "cst32"], writes=[f"pb{bt}"])
                        T.op("act", I("activation", out=kcT[i2][0:64, 0, g, :], in_=pb[bt][0:64, 0:128], func=AF.Copy),
                             reads=[f"pb{bt}"], writes=[f"kcT{i2}"])
                        T.op("act", I("activation", out=kcT[i2][64:128, 1, g, :], in_=pb[bt][64:128, 0:128], func=AF.Copy),
                             reads=[f"pb{bt}"], writes=[f"kcT{i2}"])
                    T.mute = False
                    ck_(4)
                    for g in range(4):
                        ei = alt()
                        E = Eb[ei]
                        en = f"E{ei}"
                        bs = nextbank()
                        fns = []
                        for hh in range(4):
                            j = 2 * g + hh // 2
                            fns.append(mm(pb[bs][:, hh * 16:(hh + 1) * 16], kcT[i2][:, hh % 2, g, :], qT[:, j, s * LS:(s + 1) * LS]))
                        for hh in range(4):
                            j = 2 * g + hh // 2
                            fns.append(mm(pb[bs][0:64, 64 + hh * 16:64 + (hh + 1) * 16], kZ[:, hh % 2, g, 128:128 + TT], qT[:, j, s * LS:(s + 1) * LS]))
                        T.op("pe", seq(fns), reads=[f"kcT{i2}", f"kT{g}", f"qT{2 * g}", f"qT{2 * g + 1}"], writes=[f"pb{bs}"])
                        T.op("act", I("activation", out=E[:, 0, 0:64], in_=pb[bs][:, 0:64], func=AF.Exp, scale=0.125),
                             reads=[f"pb{bs}"], writes=[en])
                        T.op("act", I("activation", out=E[0:64, 1, 0:64], in_=pb[bs][0:64, 64:128], func=AF.Exp, scale=0.125),
                             reads=[f"pb{bs}"], writes=[en])
                        T.op("dve", I("memset", E[64:128, 1, 0:64], 0.0), writes=[en])
                        T.op("dve", I("tensor_scalar", out=E[0:64, 1, 0:64], in0=E[0:64, 1, 0:64],
                                                                      scalar1=cst32[0:64, CST["rowm"][0] + s:CST["rowm"][0] + s + 1], scalar2=None, op0=ALU.mult),
                             reads=[en, "cst32"], writes=[en])
                        ck_(5)
                        bo = nextbank()
                        fns = []
                        for isden in (0, 1):
                            for jj in range(2):
                                oc = isden * 256 + jj * 128
                                terms = []
                                for hl in range(2):
                                    hh = 2 * jj + hl
                                    lhs = (ones_lo if hl == 0 else ones_hi) if isden else Vc[i2][:, g, hl, :]
                                    terms.append((lhs, E[:, 0, hh * 16:(hh + 1) * 16]))
                                    lhs = (ones_lo if hl == 0 else ones_hi) if isden else Vpad[:, 1, g, hl, :]
                                    terms.append((lhs, E[:, 1, hh * 16:(hh + 1) * 16]))
                                for ti, (lhs, rhs) in enumerate(terms):
                                    fns.append(mm(pb[bo][:, oc:oc + LS], lhs, rhs, start=(ti == 0), stop=(ti == len(terms) - 1)))
                        T.op("pe", seq(fns), reads=[en, f"Vc{i2}", "Vp1", "cstb"], writes=[f"pb{bo}"])
                        ck_(6)
                        finish_pair(g, bo, s * LS, LS, ei)
            ck_(7)
            dbg("ya", l, kind, yT[:, :, 0:TTS], [f"yT{j}" for j in range(8)])
            dbg("q", l, kind, qT[:, :, 0:TTS], [f"qT{j}" for j in range(8)])
            gate_and_out(l, C_GA, "wao", TT, True)
            dbg("m1", l, kind, merged[:, :, 0:TTS], [f"mg{j}" for j in range(8)])
            T.mute = False

        def hgrn(l, kind, tix, TT):
            CO["v"] = (kind == "s") or ("hgrn" not in FINEPH)
            off = 0
            tmp = []
            toffs = []
            for i in range(8):
                toffs.append(off)
                v_, off = carve("ht", off, (TTP,), F32, tname=f"ht{i}"); tmp.append(v_)
            qeT, off = carve("qeT", off, (4, TTP), BF16, sub=True)
            keT, off = carve("keT", off, (4, TTP), BF16, sub=True)
            kd32 = []; kdtok = []; attm = []
            for i in range(4):
                v_, off = carve("kd32", off, (128,), F32, tname=f"kd32{i}"); kd32.append(v_)
                v_, off = carve("kdtok", off, (2, 128), BF16, tname=f"kdtok{i}"); kdtok.append(v_)
                v_, off = carve("attm", off, (128,), BF16, tname=f"attm{i}"); attm.append(v_)
            eL, off = carve("eL", off, (4, 8), F32)
            Vh, off = carve("Vh", off, (4, 4, 128), BF16)
            VhM, off = carve("VhM", off, (4, 128), BF16)
            sgg, off = carve("sgg", off, (4, TTP), BF16, sub=True)
            oTs = []
            for i in range(4):
                v_, _o = carve("oTs", toffs[i], (TTP,), F32, tname=f"oTs{i}"); oTs.append(v_)
            Sbf, off = carve("Sbf", off, (8, 128), BF16, sub=True)
            assert off <= ARENA, off
            names = ([f"ht{i}" for i in range(8)] + [f"qeT{i}" for i in range(4)] + [f"keT{i}" for i in range(4)]
                     + [f"kd32{i}" for i in range(4)] + [f"kdtok{i}" for i in range(4)] + [f"attm{i}" for i in range(4)] + ["eL", "Vh", "VhM"]
                     + [f"sgg{i}" for i in range(4)] + [f"oTs{i}" for i in range(4)] + [f"Sbf{h}" for h in range(8)])
            reg(names, 0, off)
            L = 64 if kind == "p" else LS
            nch = TT // L
            scanm = cs32("scanp") if kind == "p" else cs32("scans")
            trim = cs("tri2") if kind == "p" else cs("tri4")
            nblk = max(TT // 128, 1)
            nt = min(128, TT)
            is_last_p = (kind == "p" and tix == NTILE - 1)

            def Sf(h):
                return Sst[:, l, h, :]

            for pi_ in range(4):
                T.op("dve", I("memset", kdtok[pi_][:, :, :], 0.0), writes=[f"kdtok{pi_}"])
                T.op("dve", I("memset", attm[pi_][:, :], 0.0), writes=[f"attm{pi_}"])
            T.op("dve", I("memset", VhM[:, :, :], 0.0), writes=["VhM"])
            T.op("dve", I("memset", Vh[:, :, :, :], 0.0), writes=["Vh"])
            if kind == "p":
                T.op("act", I("activation", out=Sbf[:, :, :], in_=Sst[:, l, :, :], func=AF.Copy), reads=[f"Sst{l}h{h_}" for h_ in range(8)], writes=[f"Sbf{h}" for h in range(8)])

            for half in range(2):
                s_q = wnext("in", l, C_HQ + 512 * half); vq = wv512(s_q)
                s_f = wnext("in", l, C_HF + 512 * half); vf = wv512(s_f)
                for hh in range(4):
                    h = 4 * half + hh
                    t1, t2, t3, t4 = tmp[4 * (hh % 2):4 * (hh % 2) + 4]
                    tn = [f"ht{4 * (hh % 2) + i}" for i in range(4)]
                    bf = proj(s_f, lambda kc: vf[:, kc, hh * 128:(hh + 1) * 128], TT, hT, HT)
                    T.op("act", I("activation", out=t1[:, 0:TT], in_=pb[bf][:, 0:TT], func=AF.Sigmoid), reads=[f"pb{bf}"], writes=[tn[0]])
                    T.op("act", I("activation", out=t2[:, 0:TT], in_=t1[:, 0:TT], func=AF.Ln, scale=dcol(l, 2, h), bias=dcol(l, 1, h)),
                         reads=[tn[0]] + DER_ALL, writes=[tn[1]])
                    T.op("dve", I("tensor_scalar", out=t1[:, 0:TT], in0=t1[:, 0:TT], scalar1=dcol(l, 3, h), scalar2=dcol(l, 2, h), op0=ALU.mult, op1=ALU.add),
                         reads=[tn[0]] + DER_ALL, writes=[tn[0]])
                    T.op("dve", I("tensor_scalar", out=t2[:, 0:TT], in0=t2[:, 0:TT], scalar1=-60.0, scalar2=None, op0=ALU.max),
                         reads=[tn[1]], writes=[tn[1]])
                    T.op("dve", I("tensor_tensor_scan", out=t3[:, 0:TT], data0=scanm[:, 0:TT], data1=t2[:, 0:TT], initial=0.0, op0=ALU.mult, op1=ALU.add),
                         reads=[tn[1], "cst32"], writes=[tn[2]])
                    T.op("dve", I("tensor_scalar", out=t2[:, 0:TT], in0=t3[:, 0:TT], scalar1=-1.0, scalar2=80.0, op0=ALU.mult, op1=ALU.min),
                         reads=[tn[2]], writes=[tn[1]])
                    T.op("act", I("activation", out=t3[:, 0:TT], in_=t3[:, 0:TT], func=AF.Exp), reads=[tn[2]], writes=[tn[2]])
                    T.op("act", I("activation", out=t2[:, 0:TT], in_=t2[:, 0:TT], func=AF.Exp), reads=[tn[1]], writes=[tn[1]])
                    bq = proj(s_q, lambda kc: vq[:, kc, hh * 128:(hh + 1) * 128], TT, hT, HT)
                    T.op("act", I("activation", out=t4[:, 0:TT], in_=pb[bq][:, 0:TT], func=AF.Silu), reads=[f"pb{bq}"], writes=[tn[3]])
                    T.op("dve", I("tensor_tensor", out=qeT[:, hh, 0:TT], in0=t4[:, 0:TT], in1=t3[:, 0:TT], op=ALU.mult),
                         reads=[tn[3], tn[2]], writes=[f"qeT{hh}"])
                    T.op("dve", I("tensor_tensor", out=keT[:, hh, 0:TT], in0=t1[:, 0:TT], in1=t2[:, 0:TT], op=ALU.mult),
                         reads=[tn[0], tn[1]], writes=[f"keT{hh}"])
                    T.op("dve", I("tensor_copy", out=eL[:, hh, 0:nch], in_=t3[:, 0:TT].rearrange("p (c l) -> p c l", l=L)[:, :, L - 1]),
                         reads=[tn[2]], writes=["eL"])
                SUBH = int(os.environ.get("SUBH", "99"))
                if SUBH < 1:
                    T.mute = True
                s_i = wnext("in", l, C_HI + 512 * half); vi = wv512(s_i)
                for bk in range(nblk):
                    b = nextbank()
                    fns = [mm(pb[b][0:nt, 0:512], hT[:, kc, bk * 128:bk * 128 + nt], vi[:, kc, :], start=(kc == 0), stop=(kc == 7)) for kc in range(8)]
                    T.op("pe", seq(fns), reads=[f"ws{s_i}"] + HT, writes=[f"pb{b}"])
                    T.op("act", I("activation", out=Vh[0:nt, bk, :, :], in_=pb[b][0:nt, 0:512].rearrange("p (h d) -> p h d", h=4), func=AF.Copy),
                         reads=[f"pb{b}"], writes=["Vh"])
                s_g = wnext("in", l, C_HG + 512 * half); vg = wv512(s_g)
                for hh in range(4):
                    bg = proj(s_g, lambda kc: vg[:, kc, hh * 128:(hh + 1) * 128], TT, hT, HT)
                    T.op("act", I("activation", out=sgg[:, hh, 0:TT], in_=pb[bg][:, 0:TT], func=AF.Silu), reads=[f"pb{bg}"], writes=[f"sgg{hh}"])
                if SUBH < 2:
                    T.mute = True
                def onorm(hh, h):
                    oT = oTs[hh]
                    on = f"oTs{hh}"
                    s_ = sq[alt()]
                    sn = "sq0" if s_ is sq[0] else "sq1"
                    bank, r_, rn = nextstat()
                    T.op("act", I("activation", out=s_[:, 0:TT], in_=oT[:, 0:TT], func=AF.Square), reads=[on], writes=[sn])
                    T.op("pe", mm(pb[bank][:, 0:TT], cs("ones"), s_[:, 0:TT]), reads=[sn, "cstb"], writes=[f"pb{bank}"])
                    rsqrt_from(bank, r_, rn, TT, 1.0 / 128)
                    T.op("dve", I("scalar_tensor_tensor", out=oT[:, 0:TT], in0=oT[:, 0:TT], scalar=pc(l, P_GO, 0), in1=r_[:, 0:TT], op0=ALU.mult, op1=ALU.mult),
                         reads=[on, rn, "pcol"], writes=[on])
                    T.op("dve", I("tensor_tensor", out=yT[:, h, 0:TT], in0=oT[:, 0:TT], in1=sgg[:, hh, 0:TT], op=ALU.mult),
                         reads=[on, f"sgg{hh}"], writes=[f"yT{h}"])

                if kind == "p":
                    HH = range(4)
                    for pr in range(TT // 128):
                        c0 = pr * 128
                        for hh in HH:
                            for cc in range(2):
                                T.op("dve", I("tensor_scalar", out=kd32[hh][:, cc * 64:(cc + 1) * 64], in0=keT[:, hh, c0 + cc * 64:c0 + (cc + 1) * 64],
                                              scalar1=eL[:, hh, 2 * pr + cc:2 * pr + cc + 1], scalar2=None, op0=ALU.mult),
                                     reads=[f"keT{hh}", "eL"], writes=[f"kd32{hh}"])
                        for hh in HH:
                            bk_ = 6 + hh % 2
                            T.op("pe", seq([I("transpose", out=pb[bk_][:, 0:128], in_=kd32[hh][:, :], identity=cs32("ident")),
                                            mm(pb[bk_][:, 128:256], keT[:, hh, c0:c0 + 128], qeT[:, hh, c0:c0 + 128])]),
                                 reads=[f"kd32{hh}", "cst32", f"keT{hh}", f"qeT{hh}"], writes=[f"pb{bk_}"])
                            T.op("act", I("activation", out=kdtok[hh][0:64, 0, :], in_=pb[bk_][0:64, 0:128], func=AF.Copy), reads=[f"pb{bk_}"], writes=[f"kdtok{hh}"])
                            T.op("act", I("activation", out=kdtok[hh][64:128, 1, :], in_=pb[bk_][64:128, 0:128], func=AF.Copy), reads=[f"pb{bk_}"], writes=[f"kdtok{hh}"])
                            T.op("dve", I("tensor_tensor", out=attm[hh][:, :], in0=pb[bk_][:, 128:256], in1=trim, op=ALU.mult),
                                 reads=[f"pb{bk_}", "cstb"], writes=[f"attm{hh}"])
                        for hh in HH:
                            h = 4 * half + hh
                            T.op("pe", seq([mm(pb[hh][:, 0:128], kdtok[hh][:, 0, :], Vh[:, pr, hh, :]),
                                            mm(pb[hh][:, 128:256], kdtok[hh][:, 1, :], Vh[:, pr, hh, :]),
                                            mm(pb[hh][:, 256:384], Vh[:, pr, hh, :], attm[hh][:, :], start=True, stop=False),
                                            mm(pb[hh][:, 256:320], Sbf[:, h, :], qeT[:, hh, c0:c0 + 64], start=False, stop=False)]),
                                 reads=[f"kdtok{hh}", "Vh", f"attm{hh}", f"Sbf{h}", f"qeT{hh}"], writes=[f"pb{hh}"])
                        for hh in HH:
                            h = 4 * half + hh
                            T.op("dve", I("scalar_tensor_tensor", out=Sf(h), in0=Sf(h), scalar=eL[:, hh, 2 * pr:2 * pr + 1], in1=pb[hh][:, 0:128],
                                          op0=ALU.mult, op1=ALU.add),
                                 reads=[f"pb{hh}", "eL", f"Sst{l}h{h}"], writes=[f"Sst{l}h{h}"])
                            T.op("act", I("activation", out=Sbf[:, h, :], in_=Sf(h), func=AF.Copy), reads=[f"Sst{l}h{h}"], writes=[f"Sbf{h}"])
                        for hh in HH:
                            h = 4 * half + hh
                            T.op("pe", mm(pb[hh][:, 320:384], Sbf[:, h, :], qeT[:, hh, c0 + 64:c0 + 128], start=False, stop=True),
                                 reads=[f"Sbf{h}", f"qeT{hh}"], writes=[f"pb{hh}"])
                        for hh in HH:
                            h = 4 * half + hh
                            T.op("dve", I("scalar_tensor_tensor", out=Sf(h), in0=Sf(h), scalar=eL[:, hh, 2 * pr + 1:2 * pr + 2], in1=pb[hh][:, 128:256],
                                          op0=ALU.mult, op1=ALU.add),
                                 reads=[f"pb{hh}", "eL", f"Sst{l}h{h}"], writes=[f"Sst{l}h{h}"])
                            T.op("act", I("activation", out=Sbf[:, h, :], in_=Sf(h), func=AF.Copy), reads=[f"Sst{l}h{h}"], writes=[f"Sbf{h}"])
                            T.op("act", I("activation", out=oTs[hh][:, c0:c0 + 128], in_=pb[hh][:, 256:384], func=AF.Copy), reads=[f"pb{hh}"], writes=[f"oTs{hh}"])
                    if SUBH < 3:
                        T.mute = True
                    for hh in HH:
                        onorm(hh, 4 * half + hh)
                else:
                  for hh in range(4):
                    h = 4 * half + hh
                    oT = oTs[hh]
                    on = f"oTs{hh}"
                    if True:
                        pi = hh
                        for s in range(NSS):
                            T.op("dve", I("tensor_scalar", out=kd32[pi][:, s * LS:(s + 1) * LS], in0=keT[:, hh, s * LS:(s + 1) * LS],
                                                                            scalar1=eL[:, hh, s:s + 1], scalar2=None, op0=ALU.mult),
                                 reads=[f"keT{hh}", "eL"], writes=[f"kd32{pi}"])
                        T.op("pe", seq([I("transpose", out=pb[6][0:64, 0:128], in_=kd32[pi][:, 0:64], identity=cs32("ident")),
                                        mm(pb[6][0:64, 128:192], keT[:, hh, 0:64], qeT[:, hh, 0:64])]),
                             reads=[f"kd32{pi}", "cst32", f"keT{hh}", f"qeT{hh}"], writes=["pb6"])
                        T.op("act", I("activation", out=kdtok[pi][0:64, 0, :], in_=pb[6][0:64, 0:128], func=AF.Copy), reads=["pb6"], writes=[f"kdtok{pi}"])
                        T.op("dve", I("tensor_tensor", out=attm[pi][0:64, 0:64], in0=pb[6][0:64, 128:192], in1=trim[0:64, :], op=ALU.mult),
                             reads=["pb6", "cstb"], writes=[f"attm{pi}"])
                        T.op("pe", mm(pb[7][:, 256:320], Vh[:, 0, hh, :], attm[pi][:, 0:64], start=True, stop=False),
                             reads=["Vh", f"attm{pi}"], writes=["pb7b"])
                        for s in range(NSS):
                            T.dma("sp", f"Ssm{h}", [I("dma_start", out=Ssm[:, h, :], in_=D["shg"][l, s, h])], writes=[f"Ssm{h}"])
                            T.op("act", I("activation", out=Sbf[:, h, :], in_=Ssm[:, h, :], func=AF.Copy), reads=[f"Ssm{h}"], writes=[f"Sbf{h}"])
                            T.op("pe", mm(pb[7][:, 256 + s * LS:256 + (s + 1) * LS], Sbf[:, h, :], qeT[:, hh, s * LS:(s + 1) * LS], start=False, stop=(s == NSS - 1)),
                                 reads=[f"Sbf{h}", f"qeT{hh}"], writes=["pb7b"])
                            T.op("dve", I("tensor_scalar", out=VhM[0:64, hh, :], in0=Vh[0:64, 0, hh, :],
                                                                     scalar1=cst32[0:64, CST["rowm"][0] + s:CST["rowm"][0] + s + 1], scalar2=None, op0=ALU.mult),
                                 reads=["Vh", "cst32"], writes=["VhM"])
                            T.op("pe", mm(pb[6][:, 256:384], kdtok[pi][:, 0, :], VhM[:, hh, :]), reads=[f"kdtok{pi}", "VhM"], writes=["pb6b"])
                            T.op("dve", I("scalar_tensor_tensor", out=Ssm[:, h, :], in0=Ssm[:, h, :], scalar=eL[:, hh, s:s + 1], in1=pb[6][:, 256:384],
                                                                              op0=ALU.mult, op1=ALU.add),
                                 reads=["pb6b", "eL", f"Ssm{h}"], writes=[f"Ssm{h}"])
                            T.dma("sp", f"ohs{h}", [I("dma_start", out=D["nhs"][l, s, h], in_=Ssm[:, h, :])], reads=[f"Ssm{h}"], is_output=True)
                        T.op("act", I("activation", out=oT[:, 0:64], in_=pb[7][:, 256:320], func=AF.Copy), reads=["pb7b"], writes=[on])
                    onorm(hh, h)
            if is_last_p:
                T.dma("sp", "ohp", [I("dma_start", out=D["nhp"][l].rearrange("h k v -> k h v"), in_=Sst[:, l, :, :])],
                      reads=[f"Sst{l}h{h}" for h in range(8)], is_output=True)
            T.mute = False
            dbg("yb", l, kind, yT[:, :, 0:TTS], [f"yT{j}" for j in range(8)])
            gate_and_out(l, C_GB, "who", TT, False)
            dbg("m2", l, kind, merged[:, :, 0:TTS], [f"mg{j}" for j in range(8)])

        def lru(l, kind, tix, TT):
            CO["v"] = (kind == "s") or ("lru" not in FINEPH)
            off = 0
            sets = []
            for i in range(2):
                d = {}
                d["X"], off = carve("X", off, (TTP + 12,), F32, tname=f"L{i}X")
                for nm in ("xc", "r", "ig", "a", "a2", "h", "gl"):
                    d[nm], off = carve(nm, off, (TTP,), F32, tname=f"L{i}{nm}")
                d["xcb"], off = carve("xcb", off, (TTP,), BF16, tname=f"L{i}xcb")
                sets.append(d)
            assert off <= ARENA, off
            keys = ("X", "xc", "r", "ig", "a", "a2", "h", "gl", "xcb")
            reg([f"L{i}{k}" for i in range(2) for k in keys], 0, off)
            nseq = 1 if kind == "p" else NSS
            Ls = TT // nseq
            is_last_p = (kind == "p" and tix == NTILE - 1)
            if kind == "s":
                T.dma("sp", "h0s", [I("dma_start", out=h0s[:, s, :], in_=D["slr"][l, s].rearrange("(j p) -> p j", p=128), allow_slow_non_contiguous=True)
                                    for s in range(NSS)], writes=["h0s"])
            for half in range(2):
                s_x = wnext("in", l, C_LX + 512 * half); vx = wv512(s_x)
                s_g = wnext("in", l, C_LG + 512 * half); vg = wv512(s_g)
                for jj in range(4):
                    j = 4 * half + jj
                    d = sets[j % 2]
                    n = lambda k, j=j: f"L{j % 2}{k}"
                    Xv = d["X"][:, 0:nseq * (Ls + 3)].rearrange("p (s t) -> p s t", s=nseq)
                    v3 = lambda ap: ap[:, 0:TT].rearrange("p (s t) -> p s t", s=nseq)
                    bx = proj(s_x, lambda kc: vx[:, kc, jj * 128:(jj + 1) * 128], TT, hT, HT)
                    if kind == "p":
                        T.op("dve", I("tensor_copy", out=Xv[:, 0, 0:3], in_=lxhalo[:, l, j, :]), reads=[f"lxhalo{l}"], writes=[n("X")])
                    else:
                        T.dma("sp", f"xhs{j}", [I("dma_start", out=xhs[:, j, s, :], in_=D["scv"][l, s, :, j * 128:(j + 1) * 128].rearrange("t p -> p t"),
                                                                             allow_slow_non_contiguous=True) for s in range(NSS)], writes=[f"xhs{j}"])
                        T.op("dve", I("tensor_copy", out=Xv[:, :, 0:3], in_=xhs[:, j, :, :]), reads=[f"xhs{j}"], writes=[n("X")])
                    T.op("act", I("activation", out=Xv[:, :, 3:3 + Ls], in_=pb[bx][:, 0:TT].rearrange("p (s t) -> p s t", s=nseq), func=AF.Copy),
                         reads=[f"pb{bx}"], writes=[n("X")])
                    if kind == "p":
                        T.op("dve", I("tensor_copy", out=lxhalo[:, l, j, :], in_=Xv[:, 0, Ls:Ls + 3]), reads=[n("X")], writes=[f"lxhalo{l}"])
                        if is_last_p:
                            T.op("dve", I("tensor_copy", out=cstage[:, j, 0, :], in_=Xv[:, 0, Ls:Ls + 3]), reads=[n("X")], writes=["cstage"])
                    else:
                        T.op("dve", I("tensor_copy", out=cstage[:, j, :, :], in_=Xv[:, :, Ls:Ls + 3]), reads=[n("X")], writes=["cstage"])
                    xc3 = v3(d["xc"])
                    T.op("dve", I("tensor_scalar", out=xc3, in0=Xv[:, :, 0:Ls], scalar1=pc(l, P_CW + 0, j), scalar2=pc(l, P_CB, j), op0=ALU.mult, op1=ALU.add),
                         reads=[n("X"), "pcol"], writes=[n("xc")])
                    for k in range(1, 4):
                        T.op("dve", I("scalar_tensor_tensor", out=xc3, in0=Xv[:, :, k:k + Ls], scalar=pc(l, P_CW + k, j), in1=xc3, op0=ALU.mult, op1=ALU.add),
                             reads=[n("X"), n("xc"), "pcol"], writes=[n("xc")])
                    T.op("act", I("activation", out=d["xcb"][:, 0:TT], in_=d["xc"][:, 0:TT], func=AF.Copy), reads=[n("xc")], writes=[n("xcb")])
                    ba = nextbank()
                    T.op("pe", mm(pb[ba][:, 0:TT], bda[:, l, 0, j, :], d["xcb"][:, 0:TT]), reads=["bda", n("xcb")], writes=[f"pb{ba}"])
                    bi = nextbank()
                    T.op("pe", mm(pb[bi][:, 0:TT], bda[:, l, 1, j, :], d["xcb"][:, 0:TT]), reads=["bda", n("xcb")], writes=[f"pb{bi}"])
                    T.op("act", I("activation", out=d["r"][:, 0:TT], in_=pb[ba][:, 0:TT], func=AF.Sigmoid, bias=pc(l, P_BA, j)), reads=[f"pb{ba}", "pcol"], writes=[n("r")])
                    T.op("act", I("activation", out=d["ig"][:, 0:TT], in_=pb[bi][:, 0:TT], func=AF.Sigmoid, bias=pc(l, P_BX, j)), reads=[f"pb{bi}", "pcol"], writes=[n("ig")])
                    T.op("act", I("activation", out=d["a"][:, 0:TT], in_=d["r"][:, 0:TT], func=AF.Exp, scale=dcol(l, 4, j)), reads=[n("r")] + DER_ALL, writes=[n("a")])
                    T.op("act", I("activation", out=d["a2"][:, 0:TT], in_=d["r"][:, 0:TT], func=AF.Exp, scale=dcol(l, 5, j)), reads=[n("r")] + DER_ALL, writes=[n("a2")])
                    T.op("dve", I("tensor_scalar", out=d["a2"][:, 0:TT], in0=d["a2"][:, 0:TT], scalar1=-1.0, scalar2=1.0, op0=ALU.mult, op1=ALU.add), reads=[n("a2")], writes=[n("a2")])
                    T.op("act", I("activation", out=d["a2"][:, 0:TT], in_=d["a2"][:, 0:TT], func=AF.Sqrt), reads=[n("a2")], writes=[n("a2")])
                    T.op("dve", I("tensor_tensor", out=d["ig"][:, 0:TT], in0=d["ig"][:, 0:TT], in1=d["xc"][:, 0:TT], op=ALU.mult), reads=[n("ig"), n("xc")], writes=[n("ig")])
                    T.op("dve", I("tensor_tensor", out=d["a2"][:, 0:TT], in0=d["a2"][:, 0:TT], in1=d["ig"][:, 0:TT], op=ALU.mult), reads=[n("a2"), n("ig")], writes=[n("a2")])
                    if kind == "p":
                        T.op("dve", I("tensor_tensor_scan", out=d["h"][:, 0:TT], data0=d["a"][:, 0:TT], data1=d["a2"][:, 0:TT], initial=hprev[:, l, j:j + 1], op0=ALU.mult, op1=ALU.add),
                             reads=[n("a"), n("a2"), f"hprev{l}"], writes=[n("h")])
                        T.op("dve", I("tensor_copy", out=hprev[:, l, j:j + 1], in_=d["h"][:, TT - 1:TT]), reads=[n("h")], writes=[f"hprev{l}"])
                    else:
                        for s in range(NSS):
                            T.op("dve", I("tensor_tensor_scan", out=d["h"][:, s * Ls:(s + 1) * Ls], data0=d["a"][:, s * Ls:(s + 1) * Ls], data1=d["a2"][:, s * Ls:(s + 1) * Ls],
                                                                                   initial=h0s[:, s, j:j + 1], op0=ALU.mult, op1=ALU.add),
                                 reads=[n("a"), n("a2"), "h0s"], writes=[n("h")])
                        T.op("dve", I("tensor_copy", out=hstage[:, j, :], in_=d["h"][:, 0:TT].rearrange("p (s t) -> p s t", s=NSS)[:, :, Ls - 1]), reads=[n("h")], writes=["hstage"])
                    bg = proj(s_g, lambda kc: vg[:, kc, jj * 128:(jj + 1) * 128], TT, hT, HT)
                    T.op("act", I("activation", out=d["gl"][:, 0:TT], in_=pb[bg][:, 0:TT], func=AF.Gelu_apprx_tanh), reads=[f"pb{bg}"], writes=[n("gl")])
                    T.op("dve", I("tensor_tensor", out=yT[:, j, 0:TT], in0=d["h"][:, 0:TT], in1=d["gl"][:, 0:TT], op=ALU.mult), reads=[n("h"), n("gl")], writes=[f"yT{j}"])
            if kind == "s":
                T.dma("sp", "ocs", [I("dma_start", out=D["ncs"][l, s, t].rearrange("(j p) -> p j", p=128), in_=cstage[:, :, s, t], allow_slow_non_contiguous=True)
                                    for s in range(NSS) for t in range(3)], reads=["cstage"], is_output=True)
                T.dma("sp", "ols", [I("dma_start", out=D["nls"][l, s].rearrange("(j p) -> p j", p=128), in_=hstage[:, :, s], allow_slow_non_contiguous=True)
                                    for s in range(NSS)], reads=["hstage"], is_output=True)
            elif is_last_p:
                T.dma("sp", "ocsP", [I("dma_start", out=D["ncp"][l, t].rearrange("(j p) -> p j", p=128), in_=cstage[:, :, 0, t], allow_slow_non_contiguous=True)
                                     for t in range(3)], reads=["cstage"], is_output=True)
                T.dma("sp", "olsP", [I("dma_start", out=D["nlp"][l].rearrange("(j p) -> p j", p=128), in_=hprev[:, l, :], allow_slow_non_contiguous=True)],
                      reads=[f"hprev{l}"], is_output=True)
            dbg("yc", l, kind, yT[:, :, 0:TTS], [f"yT{j}" for j in range(8)])
            gate_and_out(l, C_GC, "wlo", TT, False)
            dbg("m3", l, kind, merged[:, :, 0:TTS], [f"mg{j}" for j in range(8)])

        def load_x(kind, tix, TT):
            src = D["xp"] if kind == "p" else D["xs"]
            for bk in range(max(TT // 128, 1)):
                nt = min(128, TT)
                xi = xin[bk % 2]
                r0 = (tix * TTP if kind == "p" else 0) + bk * 128
                T.dma("sp", "xin0", [I("dma_start", out=xi[0:nt, :], in_=src[r0:r0 + nt, :])], writes=["xin0"])
                for j in range(8):
                    b = nextbank()
                    T.op("pe", I("transpose", out=pb[b][:, 0:nt], in_=xi[0:nt, j * 128:(j + 1) * 128], identity=cs32("ident", slice(0, nt))[:, 0:nt]),
                         reads=["xin0", "cst32"], writes=[f"pb{b}"])
                    eng = "act" if j % 2 else "dve"
                    if eng == "act":
                        T.op("act", I("activation", out=xT[:, j, bk * 128:bk * 128 + nt], in_=pb[b][:, 0:nt], func=AF.Copy), reads=[f"pb{b}"], writes=[f"xT{j}"])
                    else:
                        T.op("dve", I("tensor_copy", out=xT[:, j, bk * 128:bk * 128 + nt], in_=pb[b][:, 0:nt]), reads=[f"pb{b}"], writes=[f"xT{j}"])

        def store_x(kind, tix, TT):
            dst = D["yp"] if kind == "p" else D["ys"]
            for bk in range(max(TT // 128, 1)):
                nt = min(128, TT)
                xi = xin[bk % 2]
                r0 = (tix * TTP if kind == "p" else 0) + bk * 128
                for j in range(8):
                    b = nextbank()
                    T.op("pe", I("transpose", out=pb[b][0:nt, 0:128], in_=xT[:, j, bk * 128:bk * 128 + nt], identity=cs32("ident")),
                         reads=[f"xT{j}", "cst32"], writes=[f"pb{b}"])
                    if j % 2:
                        T.op("act", I("activation", out=xi[0:nt, j * 128:(j + 1) * 128], in_=pb[b][0:nt, 0:128], func=AF.Copy), reads=[f"pb{b}"], writes=["xin0"])
                    else:
                        T.op("dve", I("tensor_copy", out=xi[0:nt, j * 128:(j + 1) * 128], in_=pb[b][0:nt, 0:128]), reads=[f"pb{b}"], writes=["xin0"])
                T.dma("sp", "xin0", [I("dma_start", out=dst[r0:r0 + nt, :], in_=xi[0:nt, :])], reads=["xin0"], is_output=True)

        def ffn_phase(l, tag, prow, TT, kind="p"):
            CO["v"] = (kind == "s") or ("ffn" not in FINEPH)
            off = 0
            aT, off = carve("aT", off, (22, TTP), BF16, sub=True)
            sgb = []
            for i in range(2):
                v_, off = carve("sg", off, (TTP,), F32, tname=f"sg{i}"); sgb.append(v_)
            assert off <= ARENA
            reg([f"aT{f}" for f in range(22)] + ["sg0", "sg1"], 0, off)
            ffn(l, tag, prow, TT, aT, sgb)

        phase_ctr = {"n": 0}

        def phase(nm):
            phase_ctr["n"] += 1
            if stop is not None and phase_ctr["n"] > stop:
                raise _Stop(nm)

        def run_all():
          for kind, tix in tiles:
            TT = TTP if kind == "p" else TTS
            phase("load")
            load_x(kind, tix, TT)
            for l in range(nlayer):
                phase("ffn1")
                ffn_phase(l, "1", P_G1, TT, kind)
                phase("attn")
                dbg("x1", l, kind, xT[:, :, 0:TTS], [f"xT{j}" for j in range(8)])
                rmsnorm(l, P_GM, TT, None)
                attention(l, kind, tix, TT)
                phase("hgrn")
                hgrn(l, kind, tix, TT)
                phase("lru")
                lru(l, kind, tix, TT)
                phase("out")
                for j in range(8):
                    if j % 2:
                        T.op("act", I("activation", out=yT[:, j, 0:TT], in_=merged[:, j, 0:TT], func=AF.Copy), reads=[f"mg{j}"], writes=[f"yT{j}"])
                    else:
                        T.op("dve", I("tensor_copy", out=yT[:, j, 0:TT], in_=merged[:, j, 0:TT]), reads=[f"mg{j}"], writes=[f"yT{j}"])
                YT = [f"yT{j}" for j in range(8)]
                for half in range(2):
                    slot = wnext("sq", l, "wout", 512 * half)
                    v = wv512(slot)
                    for jj in range(4):
                        j = 4 * half + jj
                        b = proj(slot, lambda kc: v[:, kc, jj * 128:(jj + 1) * 128], TT, yT, YT)
                        T.op("dve", I("tensor_tensor", out=xT[:, j, 0:TT], in0=pb[b][:, 0:TT], in1=xT[:, j, 0:TT], op=ALU.add),
                             reads=[f"pb{b}", f"xT{j}"], writes=[f"xT{j}"])
                dbg("xm", l, kind, xT[:, :, 0:TTS], [f"xT{j}" for j in range(8)])
                ffn_phase(l, "2", P_G2, TT, kind)
            phase("store")
            store_x(kind, tix, TT)
        try:
            run_all()
            assert wstate["i"] == len(wsched)
        except _Stop as ex:
            print("build stopped before phase", ex)
        T.finish()
        T.emit()
    return nc


TT_DUMMY = None
_NC_CACHE = {}


def _get_nc(key=(NTILE, True, 2)):
    if key not in _NC_CACHE:
        _NC_CACHE[key] = build(*key)
    return _NC_CACHE[key]


PROMPT_CORES = (0, 1, 4, 5)


def make_in_maps(inp):
    f = lambda a: np.ascontiguousarray(np.asarray(a, dtype=np.float32))
    P = np.zeros((NPR, 1024), np.float32)
    for l in range(2):
        b = 16 * l
        P[b + P_G1] = inp["norm_ffn1"][l]; P[b + P_GM] = inp["norm_mix"][l]; P[b + P_G2] = inp["norm_ffn2"][l]
        P[b + P_LB] = inp["hgrn_lb_logits"][l]
        P[b + P_CW:b + P_CW + 4] = inp["conv_w"][l]
        P[b + P_CB] = inp["conv_b"][l]; P[b + P_BA] = inp["lru_b_a"][l]; P[b + P_BX] = inp["lru_b_x"][l]; P[b + P_LAM] = inp["lru_lambda"][l]
        P[b + P_GQ] = np.tile(inp["q_norm"][l], 16); P[b + P_GK] = np.tile(inp["k_norm"][l], 16)
        P[b + P_SINK] = np.repeat(inp["attn_sinks"][l], 64); P[b + P_GO] = np.tile(inp["hgrn_o_norm"][l], 8)
    cst = make_consts()
    shared = {"w1u": f(inp["w_ffn1_up"]), "w1d": f(inp["w_ffn1_down"]), "win": f(inp["w_in"]), "wao": f(inp["w_attn_o"]),
              "who": f(inp["w_hgrn_o"]), "wlo": f(inp["w_lru_o"]), "wout": f(inp["w_out"]), "w2u": f(inp["w_ffn2_up"]),
              "w2d": f(inp["w_ffn2_down"]), "lwa": f(inp["lru_w_a"]), "lwx": f(inp["lru_w_x"]), "P": P, "cst": cst}
    maps = []
    zero_xp = np.zeros((SEQ, 1024), np.float32)
    for c in range(8):
        s0 = NSS * c
        m = dict(shared)
        m["xp"] = f(inp["x_prompt"][PROMPT_CORES.index(c)]) if c in PROMPT_CORES else zero_xp
        m["xs"] = f(inp["x_sample"][s0:s0 + NSS]).reshape(TTS, 1024)
        m["ck"] = f(inp["cache_attn_k"][:, s0:s0 + NSS]).reshape(2, NSS, 128, 256)
        m["cv"] = f(inp["cache_attn_v"][:, s0:s0 + NSS]).reshape(2, NSS, 128, 256)
        m["shg"] = f(inp["state_hgrn"][:, s0:s0 + NSS])
        m["scv"] = f(inp["state_conv"][:, s0:s0 + NSS])
        m["slr"] = f(inp["state_lru"][:, s0:s0 + NSS])
        maps.append(m)
    return maps


def assemble(res):
    R = res
    cat = lambda k, ax, cores: np.concatenate([R[c][k] for c in cores], axis=ax)
    pc_ = PROMPT_CORES
    ac = range(8)
    yp = np.stack([R[c]["yp"] for c in pc_], 0)
    ys = np.concatenate([R[c]["ys"].reshape(NSS, LS, 1024) for c in ac], 0)
    nkp = np.stack([R[c]["nkp"].reshape(2, 128, 4, 64) for c in pc_], 1)
    nvp = np.stack([R[c]["nvp"].reshape(2, 128, 4, 64) for c in pc_], 1)
    nhp = np.stack([R[c]["nhp"] for c in pc_], 1)
    ncp = np.stack([R[c]["ncp"] for c in pc_], 1)
    nlp = np.stack([R[c]["nlp"] for c in pc_], 1)
    nks = np.concatenate([R[c]["nks"].reshape(2, NSS, LS, 4, 64) for c in ac], 1)
    nvs = np.concatenate([R[c]["nvs"].reshape(2, NSS, LS, 4, 64) for c in ac], 1)
    nhs = cat("nhs", 1, ac)
    ncs = cat("ncs", 1, ac)
    nls = cat("nls", 1, ac)
    outs = (yp, ys, nkp, nvp, nhp, ncp, nlp, nks, nvs, nhs, ncs, nls)
    return tuple(np.ascontiguousarray(o.astype(np.float32)) for o in outs)


def kernel(**inputs):
    nc = _get_nc()
    maps = make_in_maps(inputs)
    res = run_bass_kernel_spmd(nc, maps, core_ids=list(range(8)))
    return assemble(res.results)
```

```python
import contextlib
import numpy as np
import concourse.bass as bass
import concourse.mybir as mybir
from concourse.bass_utils import run_bass_kernel_spmd

F32 = mybir.dt.float32
BF16 = mybir.dt.bfloat16
AF = mybir.ActivationFunctionType
ALU = mybir.AluOpType
ENGS = ("pe", "act", "dve", "pool", "sp")

D_MODEL = 1024
D_FF = 2816
SEQ = 4096
TTP = 512
NTILE = SEQ // TTP
NSS = 4
LS = 16
TTS = NSS * LS
IN_COLS = 10752
EPS = 1e-6
C_AQ, C_AK, C_AV, C_HQ, C_HF, C_HI, C_HG, C_LX, C_LG, C_GA, C_GB, C_GC = (
    0, 1024, 1280, 1536, 2560, 3584, 4608, 5632, 6656, 7680, 8704, 9728)


class Tracker:
    def __init__(self, nc, stack):
        self.nc = nc
        self.stack = stack
        self.streams = {e: [] for e in ENGS}
        self.sems = {}
        self.val = {}
        self.seen = {e: {} for e in ENGS}
        self.bufs = {}
        self.out_deps = []
        self.ranges = {}
        self.overl = {}
        for e in ("pe", "act", "dve", "pool"):
            self._sem(e)

    def _sem(self, key):
        if key not in self.sems:
            nm = "s_" + key.replace(":", "_")
            self.sems[key] = self.stack.enter_context(self.nc.semaphore(nm))
            self.val[key] = 0
        return self.sems[key]

    def set_range(self, name, off, end):
        if self.ranges.get(name) == (off, end):
            return
        if name in self.ranges:
            o0, e0 = self.ranges[name]
            off, end = min(off, o0), max(end, e0)
            if (off, end) == (o0, e0):
                return
            for n2 in self.overl[name]:
                self.overl[n2].remove(name)
        self.ranges[name] = (off, end)
        ov = []
        for n2, (o2, e2) in self.ranges.items():
            if n2 != name and o2 < end and off < e2:
                ov.append(n2)
                self.overl[n2].append(name)
        self.overl[name] = ov

    def _deps(self, reads, writes):
        deps = {}

        def add(d):
            if d is not None and deps.get(d[0], 0) < d[1]:
                deps[d[0]] = d[1]

        def allacc(b):
            st = self.bufs.get(b)
            if st:
                add(st[0])
                for r in st[1]:
                    add(r)

        for b in reads:
            st = self.bufs.get(b)
            if st:
                add(st[0])
                if b.startswith("pb"):
                    for r in st[1]:
                        add(r)
            for o in self.overl.get(b, ()):
                allacc(o)
        for b in writes:
            allacc(b)
            for o in self.overl.get(b, ()):
                allacc(o)
        return deps

    def _emit_waits(self, eng, deps):
        for k, v in deps.items():
            if eng == "pe" and k == "pe":
                continue
            if self.seen[eng].get(k, 0) >= v:
                continue
            self.seen[eng][k] = v
            sem = self.sems[k]
            self.streams[eng].append(I("wait_ge", sem, v))

    def _record(self, dep, reads, writes):
        for b in reads:
            st = self.bufs.setdefault(b, [None, []])
            st[1].append(dep)
            if len(st[1]) > 12:
                m = {}
                for k, v in st[1]:
                    m[k] = max(m.get(k, 0), v)
                st[1] = list(m.items())
        for b in writes:
            self.bufs[b] = [dep, []]

    mute = False

    def op(self, eng, fn, reads=(), writes=()):
        if self.mute:
            return
        deps = self._deps(reads, writes)
        self._emit_waits(eng, deps)
        self.val[eng] += 1
        n = self.val[eng]
        sem = self.sems[eng]
        self.streams[eng].append(lambda e, fn=fn, sem=sem: fn(e).then_inc(sem, 1))
        self._record((eng, n), reads, writes)

    def dma(self, q, semkey, fns, reads=(), writes=(), is_output=False):
        if self.mute:
            return
        key = "dma:" + semkey
        sem = self._sem(key)
        deps = self._deps(reads, writes)
        self._emit_waits(q, deps)
        for fn in fns:
            self.val[key] += 16
            self.streams[q].append(lambda e, fn=fn, sem=sem: fn(e).then_inc(sem, 16))
        dep = (key, self.val[key])
        self._record(dep, reads, writes)
        if is_output:
            self.out_deps.append(dep)

    def finish(self, eng="sp"):
        deps = {}
        for k, v in self.out_deps:
            deps[k] = max(deps.get(k, 0), v)
        for k, v in deps.items():
            sem = self.sems[k]
            self.streams[eng].append(I("wait_ge", sem, v))

    def emit(self):
        S = self.streams
        with self.nc.Block() as block:
            @block.tensor
            def _(e):
                for f in S["pe"]:
                    f(e)

            @block.scalar
            def _(e):
                for f in S["act"]:
                    f(e)

            @block.vector
            def _(e):
                for f in S["dve"]:
                    f(e)

            @block.gpsimd
            def _(e):
                for f in S["pool"]:
                    f(e)

            @block.sync
            def _(e):
                for f in S["sp"]:
                    f(e)


def I(name, *a, **k):
    return lambda e: getattr(e, name)(*a, **k)


def seq(fns):
    def f(e):
        r = None
        for g in fns:
            r = g(e)
        return r
    return f


def mm(out, lhsT, rhs, start=True, stop=True):
    return I("matmul", out, lhsT=lhsT, rhs=rhs, start=start, stop=stop)


CST = {}
_c = 0
for _n, _w in (("ident", 128), ("bd64", 128), ("ones", 128), ("onespad", 256), ("tri2", 128), ("tri4", 64),
               ("ssame", 256), ("rowm", 4), ("scanp", 512), ("scans", 64)):
    CST[_n] = (_c, _c + _w)
    _c += _w
NCST = _c


def make_consts():
    c = np.zeros((128, NCST), np.float32)
    p = np.arange(128)[:, None]

    def put(name, fn):
        a, b = CST[name]
        cc = np.arange(b - a)[None, :]
        c[:, a:b] = fn(p, cc).astype(np.float32)

    put("ident", lambda p, c_: p == c_)
    put("bd64", lambda p, c_: (p // 64) == (c_ // 64))
    put("ones", lambda p, c_: (p >= 0) & (c_ >= 0))
    put("onespad", lambda p, c_: np.where(c_ < 128, c_ < 64, (c_ - 128) >= 64) & (p >= 0))
    put("tri2", lambda p, c_: ((p // 64) == (c_ // 64)) & (p <= c_))
    put("tri4", lambda p, c_: ((p // 16) == (c_ // 16)) & (p <= c_))
    put("ssame", lambda p, c_: (p // 16) == ((c_ % 64) // 16))
    put("rowm", lambda p, c_: (p // 16) == c_)
    put("scanp", lambda p, c_: ((c_ % 64) != 0) & (p >= 0))
    put("scans", lambda p, c_: ((c_ % 16) != 0) & (p >= 0))
    return c


P_G1, P_GM, P_G2, P_LB, P_CW, P_CB, P_BA, P_BX, P_LAM, P_GQ, P_GK, P_SINK, P_GO = 0, 1, 2, 3, 4, 8, 9, 10, 11, 12, 13, 14, 15
NPR = 32


class _Stop(Exception):
    pass


def build(ntile_p=NTILE, do_sample=True, nlayer=2, stop=None):
    nc = bass.Bass("TRN2", target_bir_lowering=False)
    D = {}

    def din(name, shape):
        D[name] = nc.dram_tensor(name, list(shape), F32, kind="ExternalInput").ap()

    def dout(name, shape):
        D[name] = nc.dram_tensor(name, list(shape), F32, kind="ExternalOutput").ap()

    din("xp", (SEQ, 1024)); din("xs", (TTS, 1024))
    din("ck", (2, NSS, 128, 256)); din("cv", (2, NSS, 128, 256))
    din("shg", (2, NSS, 8, 128, 128)); din("scv", (2, NSS, 3, 1024)); din("slr", (2, NSS, 1024))
    din("w1u", (2, 1024, 2 * D_FF)); din("w1d", (2, D_FF, 1024)); din("win", (2, 1024, IN_COLS))
    din("wao", (2, 1024, 1024)); din("who", (2, 1024, 1024)); din("wlo", (2, 1024, 1024)); din("wout", (2, 1024, 1024))
    din("w2u", (2, 1024, 2 * D_FF)); din("w2d", (2, D_FF, 1024))
    din("lwa", (2, 16, 64, 64)); din("lwx", (2, 16, 64, 64))
    din("P", (NPR, 1024)); din("cst", (128, NCST))
    WB = {}
    WNAMES = ("w1u", "w1d", "win", "wao", "who", "wlo", "wout", "w2u", "w2d")
    for wn_ in WNAMES:
        WB[wn_] = nc.dram_tensor("wb_" + wn_, list(D[wn_].shape), BF16, kind="Internal").ap()
    dout("yp", (SEQ, 1024)); dout("ys", (TTS, 1024))
    dout("nkp", (2, 128, 256)); dout("nvp", (2, 128, 256)); dout("nhp", (2, 8, 128, 128))
    dout("ncp", (2, 3, 1024)); dout("nlp", (2, 1024))
    dout("nks", (2, TTS, 256)); dout("nvs", (2, TTS, 256)); dout("nhs", (2, NSS, 8, 128, 128))
    dout("ncs", (2, NSS, 3, 1024)); dout("nls", (2, NSS, 1024))
    import os
    DBG = os.environ.get("DBG")
    if DBG:
        dout("dbg", (128, 8, TTS))

    WQ_SP = bool(int(os.environ.get("WQ_SP", "1")))
    FINEPH = os.environ.get("FINEPH", "ffn,att,hgrn,lru").split(",")
    FINE = False
    CO = {"v": True}
    with contextlib.ExitStack() as st:
        T = Tracker(nc, st)

        def sb(name, shape, dt):
            return st.enter_context(nc.sbuf_tensor(name, list(shape), dt))

        cst32 = sb("cst32", (128, NCST), F32)
        cstb = sb("cstb", (128, NCST), BF16)
        Psb = sb("Psb", (NPR, 1024), F32)
        pcol = sb("pcol", (128, 8, NPR), F32)
        der = sb("der", (128, 2, 8, 8), F32)
        bda = sb("bda", (128, 2, 2, 8, 128), BF16)
        NSLOT = 5
        slots = [sb(f"ws{i}", (128, 4096), BF16) for i in range(NSLOT)]
        xin = [sb("xin0", (128, 1024), F32)] * 2
        xT = sb("xT", (128, 8, TTP), F32)
        hT = sb("hT", (128, 8, TTP), BF16)
        sq = [sb(f"sq{i}", (128, TTP), BF16) for i in range(2)]
        rstd = [sb(f"rstd{i}", (128, TTP), F32) for i in range(3)]
        merged = sb("merged", (128, 8, TTP), F32)
        gT = sb("gT", (128, 8, TTP), BF16)
        yT = sb("yT", (128, 8, TTP), BF16)
        khalo = sb("khalo", (128, 2, 2, 4, 128), BF16)
        vhalo = sb("vhalo", (128, 2, 4, 2, 128), BF16)
        Sst = sb("Sst", (128, 2, 8, 128), F32)
        lxhalo = sb("lxhalo", (128, 2, 8, 3), F32)
        hprev = sb("hprev", (128, 2, 8), F32)
        kstage = sb("kstage", (128, 4, 64), F32)
        vstage = sb("vstage", (128, 256), F32)
        ckd = [sb(f"ckd{i}", (128, 4, 2, 64), F32) for i in range(2)]
        Vc = [sb(f"Vc{i}", (128, 4, 2, 128), BF16) for i in range(2)]
        Ssm = sb("Ssm", (128, 8, 128), F32)
        h0s = sb("h0s", (128, NSS, 8), F32)
        cstage = sb("cstage", (128, 8, NSS, 3), F32)
        hstage = sb("hstage", (128, 8, NSS), F32)
        xhs = sb("xhs", (128, 8, NSS, 3), F32)
        if DBG:
            dbgst = sb("dbgst", (128, 8, TTS), F32)

        def dbg(name, l, kind, src, rnames):
            if DBG == name and l == 0 and kind == "s":
                T.op("dve", I("tensor_copy", out=dbgst[:], in_=src), reads=rnames, writes=["dbgst"])
                T.dma("sp", "dbg", [I("dma_start", out=D["dbg"], in_=dbgst[:])], reads=["dbgst"], is_output=True)

        bda32 = xin[0][:, :].rearrange("p (j d) -> p j d", d=128)
        ARENA = 41 * 1024
        arena = sb("arena", (128, ARENA), mybir.dt.uint8)
        pb = [st.enter_context(nc.psum_tensor(f"pb{i}", [128, 512], F32)) for i in range(8)]
        try:
            print("sbuf bytes remaining", nc.sbuf_bytes_remaining)
        except Exception as ex:
            print("sbuf remaining n/a", ex)

        def carve(name, off, shape, dt, sub=False, tname=None):
            esz = 4 if dt == F32 else 2
            n = int(np.prod(shape))
            tn = tname or name
            if CO["v"]:
                pass
            elif sub:
                rowb = (n // shape[0]) * esz
                for i_ in range(shape[0]):
                    T.set_range(f"{tn}{i_}", off + i_ * rowb, off + (i_ + 1) * rowb)
            elif tn != "-":
                T.set_range(tn, off, off + n * esz)
            v = arena[:, off:off + n * esz].bitcast(dt)
            if len(shape) > 1:
                names = "abcd"[:len(shape)]
                pat = "p (" + " ".join(names) + ") -> p " + " ".join(names)
                v = v.rearrange(pat, **{names[i]: shape[i] for i in range(1, len(shape))})
            return v, off + n * esz

        def reg(names, off, end):
            if CO["v"]:
                for nm in names:
                    T.set_range(nm, off, end)

        def cs(name, rows=slice(None)):
            a, b = CST[name]
            return cstb[rows, a:b]

        def cs32(name, rows=slice(None)):
            a, b = CST[name]
            return cst32[rows, a:b]

        rot = {"i": 0}

        def nextbank():
            rot["i"] = (rot["i"] + 1) % 4
            return rot["i"]

        stt = {"i": 0}

        def nextstat():
            stt["i"] ^= 1
            return (4 + stt["i"], rstd[0] if stt["i"] else rstd[2], "rstd0" if stt["i"] else "rstd2")

        def rsqrt_from(bank, r_, rn, TT, scale):
            T.op("act", I("activation", out=r_[:, 0:TT], in_=pb[bank][:, 0:TT], func=AF.Ln, scale=scale, bias=EPS), reads=[f"pb{bank}"], writes=[rn])
            T.op("act", I("activation", out=r_[:, 0:TT], in_=r_[:, 0:TT], func=AF.Exp, scale=-0.5), reads=[rn], writes=[rn])

        def seg(k):
            return stop is None or stop >= 0 or k <= -stop

        T.dma("sp", "cst", [I("dma_start", out=cst32[:], in_=D["cst"])], writes=["cst32"])
        T.dma("sp", "P", [I("dma_start", out=Psb[:], in_=D["P"])], writes=["Psb"])
        T.op("dve", I("tensor_copy", out=cstb[:], in_=cst32[:]), reads=["cst32"], writes=["cstb"])
        for j in (range(8) if seg(2) else ()):
            T.op("pe", I("transpose", out=pb[5][:, j * 32:(j + 1) * 32], in_=Psb[:, j * 128:(j + 1) * 128],
                                                 identity=cs32("ident", slice(0, NPR))[:, 0:NPR]),
                 reads=["Psb", "cst32"], writes=["pb5"])
        if seg(2):
            T.op("dve", I("tensor_copy", out=pcol[:], in_=pb[5][:, 0:8 * NPR].rearrange("p (j v) -> p j v", v=NPR)),
                 reads=["pb5"], writes=["pcol"])

        def pc(l, row, j):
            return pcol[:, j, 16 * l + row:16 * l + row + 1]

        def pcv(l, row):
            return pcol[:, :, 16 * l + row]

        for l in (range(2) if seg(3) else ()):
            T.op("act", I("activation", out=der[:, l, 0, :], in_=pcv(l, P_SINK), func=AF.Exp),
                 reads=["pcol"], writes=[f"der{l}0"])
            T.op("act", I("activation", out=der[:, l, 6, :], in_=pcv(l, P_LAM), func=AF.Exp, scale=-1.0),
                 reads=["pcol"], writes=[f"der{l}6"])
            T.op("act", I("activation", out=der[:, l, 7, :], in_=der[:, l, 6, :], func=AF.Ln, bias=1.0),
                 reads=[f"der{l}6"], writes=[f"der{l}7"])
            T.op("dve", I("tensor_scalar", out=der[:, l, 4, :], in0=der[:, l, 7, :], scalar1=-8.0, scalar2=None, op0=ALU.mult),
                 reads=[f"der{l}7"], writes=[f"der{l}4"])
            T.op("dve", I("tensor_scalar", out=der[:, l, 5, :], in0=der[:, l, 7, :], scalar1=-16.0, scalar2=None, op0=ALU.mult),
                 reads=[f"der{l}7"], writes=[f"der{l}5"])
        T.mute = not seg(4)
        L0, L1 = pcv(0, P_LB), pcv(1, P_LB)
        tA, tB, tC, tD = der[:, 0, 6, :], der[:, 0, 7, :], der[:, 1, 6, :], der[:, 1, 7, :]
        T.op("dve", I("tensor_tensor", out=tA, in0=L0, in1=L1, op=ALU.max), reads=["pcol", "der06", "der16"], writes=["lbA"])
        T.op("dve", I("tensor_tensor", out=tB, in0=L0, in1=tA, op=ALU.subtract), reads=["pcol", "lbA", "der07"], writes=["lbB"])
        T.op("dve", I("tensor_tensor", out=tC, in0=L1, in1=tA, op=ALU.subtract), reads=["pcol", "lbA", "der17"], writes=["lbC"])
        T.op("act", I("activation", out=tB, in_=tB, func=AF.Exp), reads=["lbB"], writes=["lbB"])
        T.op("act", I("activation", out=tC, in_=tC, func=AF.Exp), reads=["lbC"], writes=["lbC"])
        T.op("dve", I("tensor_tensor", out=tA, in0=tB, in1=tC, op=ALU.add), reads=["lbB", "lbC"], writes=["lbA"])
        T.op("dve", I("reciprocal", out=tA, in_=tA), reads=["lbA"], writes=["lbA"])
        T.op("dve", I("tensor_tensor", out=tB, in0=tB, in1=tA, op=ALU.mult), reads=["lbB", "lbA"], writes=["lbB"])
        T.op("dve", I("tensor_tensor", out=tC, in0=tC, in1=tA, op=ALU.mult), reads=["lbC", "lbA"], writes=["lbC"])
        T.op("dve", I("tensor_tensor", out=tD, in0=tB, in1=tC, op=ALU.add), reads=["lbB", "lbC"], writes=["lbD"])
        T.op("dve", I("tensor_tensor", out=der[:, 0, 1, :], in0=tB, in1=tB, op=ALU.subtract), reads=["lbB"], writes=["der01"])
        T.op("dve", I("tensor_tensor", out=der[:, 1, 1, :], in0=tD, in1=tB, op=ALU.subtract), reads=["lbD", "lbB"], writes=["der11"])
        for l in range(2):
            T.op("dve", I("tensor_scalar", out=der[:, l, 2, :], in0=der[:, l, 1, :], scalar1=-1.0, scalar2=1.0, op0=ALU.mult, op1=ALU.add),
                 reads=[f"der{l}1"], writes=[f"der{l}2"])
            T.op("dve", I("tensor_scalar", out=der[:, l, 3, :], in0=der[:, l, 1, :], scalar1=-1.0, scalar2=None, op0=ALU.add),
                 reads=[f"der{l}1"], writes=[f"der{l}3"])
        DER_ALL = [f"der{l}{k}" for l in range(2) for k in range(6)]

        def dcol(l, kind, j):
            return der[:, l, kind, j:j + 1]

        T.mute = not seg(5)
        for l in range(2):
            for gi, wn in enumerate(("lwa", "lwx")):
                T.op("dve", I("memset", bda32, 0.0), writes=["bda32"])
                src = D[wn][l].rearrange("(j two) c d -> two c j d", two=2)
                T.dma("sp", "bda32", [I("dma_start", out=bda32[0:64, :, 0:64], in_=src[0]),
                                      I("dma_start", out=bda32[64:128, :, 64:128], in_=src[1])],
                      writes=["bda32"])
                T.op("dve", I("tensor_copy", out=bda[:, l, gi, :, :], in_=bda32), reads=["bda32"], writes=["bda", "xin0"])
        T.mute = not seg(6)
        T.op("dve", I("memset", Sst[:], 0.0), writes=[f"Sst{l_}h{h_}" for l_ in range(2) for h_ in range(8)])
        T.op("dve", I("memset", lxhalo[:], 0.0), writes=["lxhalo0", "lxhalo1"])
        T.op("dve", I("memset", hprev[:], 0.0), writes=["hprev0", "hprev1"])
        T.op("dve", I("memset", khalo[:], 0.0), writes=["khalo0", "khalo1"])
        T.op("dve", I("memset", vhalo[:], 0.0), writes=["vhalo0", "vhalo1"])

        T.mute = False
        def layer_blocks(l):
            bl = []
            for f in ("1", "2"):
                pass
            def ffn(tag):
                r = [("up" + tag, l, b) for b in range(11)] + [("dn" + tag, l, j) for j in range(8)]
                return r
            bl += ffn("1")
            bl += [("in", l, C_AQ), ("in", l, C_AQ + 512), ("kdup", l, 0), ("in256", l, C_AV), ("in", l, C_GA), ("in", l, C_GA + 512),
                   ("sq", l, "wao", 0), ("sq", l, "wao", 512)]
            for half in range(2):
                bl += [("in", l, C_HQ + 512 * half), ("in", l, C_HF + 512 * half), ("in", l, C_HI + 512 * half), ("in", l, C_HG + 512 * half)]
            bl += [("in", l, C_GB), ("in", l, C_GB + 512), ("sq", l, "who", 0), ("sq", l, "who", 512)]
            for half in range(2):
                bl += [("in", l, C_LX + 512 * half), ("in", l, C_LG + 512 * half)]
            bl += [("in", l, C_GC), ("in", l, C_GC + 512), ("sq", l, "wlo", 0), ("sq", l, "wlo", 512),
                   ("sq", l, "wout", 0), ("sq", l, "wout", 512)]
            bl += ffn("2")
            return bl

        tiles = [("p", t) for t in range(ntile_p)] + ([("s", 0)] if do_sample else [])
        wsched = []
        for _ in tiles:
            for l in range(nlayer):
                wsched += layer_blocks(l)
        wstate = {"i": 0, "issued": 0}

        cast_done = set()

        def cast_weights(l):
            if l in cast_done:
                return
            cast_done.add(l)
            for wn_ in WNAMES:
                rows = D[wn_].shape[1]
                fns = [I("dma_start", out=WB[wn_][l, r0:r0 + 128, :], in_=D[wn_][l, r0:r0 + 128, :]) for r0 in range(0, rows, 128)]
                T.dma("pool", f"wb_{wn_}{l}", fns, writes=[f"wb_{wn_}{l}"])

        def w_src(desc):
            kind, l = desc[0], desc[1]
            if kind.startswith("up"):
                return ("w1u" if kind == "up1" else "w2u")
            if kind.startswith("dn"):
                return ("w1d" if kind == "dn1" else "w2d")
            if kind in ("in", "in256", "kdup"):
                return "win"
            return desc[2]

        def w_dma(desc, slot):
            s = slots[slot]
            kind = desc[0]
            l = desc[1]
            D = WB
            if kind.startswith("up"):
                W = D["w1u" if kind == "up1" else "w2u"][l].rearrange("(kc p) c -> p kc c", p=128)
                b = desc[2]
                v = s[:, 0:4096].rearrange("p (k two c) -> p k two c", two=2, c=256)
                return [I("dma_start", out=v[:, :, 0, :], in_=W[:, :, b * 256:(b + 1) * 256]),
                        I("dma_start", out=v[:, :, 1, :], in_=W[:, :, D_FF + b * 256:D_FF + (b + 1) * 256])]
            if kind.startswith("dn"):
                W = D["w1d" if kind == "dn1" else "w2d"][l].rearrange("(kc p) c -> p kc c", p=128)
                j = desc[2]
                v = s[:, 0:22 * 128].rearrange("p (k c) -> p k c", c=128)
                return [I("dma_start", out=v, in_=W[:, :, j * 128:(j + 1) * 128])]
            if kind == "in":
                W = D["win"][l].rearrange("(kc p) c -> p kc c", p=128)
                c0 = desc[2]
                v = s[:, 0:4096].rearrange("p (k c) -> p k c", c=512)
                return [I("dma_start", out=v, in_=W[:, :, c0:c0 + 512])]
            if kind == "in256":
                W = D["win"][l].rearrange("(kc p) c -> p kc c", p=128)
                c0 = desc[2]
                v = s[:, 0:2048].rearrange("p (k c) -> p k c", c=256)
                return [I("dma_start", out=v, in_=W[:, :, c0:c0 + 256])]
            if kind == "kdup":
                W = D["win"][l].rearrange("(kc p) c -> p kc c", p=128)
                v = s[:, 0:8 * 384].rearrange("p (k c) -> p k c", c=384)
                return [I("dma_start", out=v, in_=W[:, :, C_AK - 64:C_AK + 320])]
            if kind == "sq":
                W = D[desc[2]][l].rearrange("(kc p) c -> p kc c", p=128)
                c0 = desc[3]
                v = s[:, 0:4096].rearrange("p (k c) -> p k c", c=512)
                return [I("dma_start", out=v, in_=W[:, :, c0:c0 + 512])]
            raise ValueError(kind)

        def wnext(*tag):
            i = wstate["i"]
            assert wsched[i] == tuple(tag), (i, wsched[i], tag)
            while wstate["issued"] < min(len(wsched), i + NSLOT - 1):
                k = wstate["issued"]
                cast_weights(wsched[k][1])
                T.dma("sp" if WQ_SP else "pool", f"ws{k % NSLOT}", w_dma(wsched[k], k % NSLOT),
                      reads=[f"wb_{w_src(wsched[k])}{wsched[k][1]}"], writes=[f"ws{k % NSLOT}"])
                wstate["issued"] += 1
            wstate["i"] += 1
            return i % NSLOT

        cnt = {"n": 0}

        def alt():
            cnt["n"] += 1
            return cnt["n"] % 2

        def rmsnorm(l, prow, TT, xnames):
            bank, r_, rn = nextstat()
            fns = []
            for j in range(8):
                s_ = sq[j % 2]
                T.op("act", I("activation", out=s_[:, 0:TT], in_=xT[:, j, 0:TT], func=AF.Square),
                     reads=[f"xT{j}"], writes=[f"sq{j % 2}"])
                T.op("pe", mm(pb[bank][:, 0:TT], cs("ones"), s_[:, 0:TT], start=(j == 0), stop=(j == 7)),
                     reads=[f"sq{j % 2}", "cstb"], writes=[f"pb{bank}"])
            rsqrt_from(bank, r_, rn, TT, 1.0 / 1024)
            for j in range(8):
                T.op("dve", I("scalar_tensor_tensor", out=hT[:, j, 0:TT], in0=xT[:, j, 0:TT], scalar=pc(l, prow, j),
                                                                in1=r_[:, 0:TT], op0=ALU.mult, op1=ALU.mult),
                     reads=[f"xT{j}", rn, "pcol"], writes=[f"hT{j}"])

        HT = [f"hT{j}" for j in range(8)]

        def proj(slot, lhs_fn, TT, rhs_t, rhs_names, nk=8, bank=None):
            b = nextbank() if bank is None else bank
            fns = [mm(pb[b][:, 0:TT], lhs_fn(kc), rhs_t[:, kc, 0:TT], start=(kc == 0), stop=(kc == nk - 1)) for kc in range(nk)]
            T.op("pe", seq(fns), reads=[f"ws{slot}"] + rhs_names, writes=[f"pb{b}"])
            return b

        def wv512(slot):
            return slots[slot][:, 0:4096].rearrange("p (k c) -> p k c", c=512)

        def ffn(l, tag, prow, TT, aT, sgb):
            rmsnorm(l, prow, TT, None)
            for b in range(11):
                slot = wnext("up" + tag, l, b)
                v = slots[slot][:, 0:4096].rearrange("p (k two c) -> p k two c", two=2, c=256)
                for jj in range(2):
                    f = 2 * b + jj
                    bg = proj(slot, lambda kc: v[:, kc, 0, jj * 128:(jj + 1) * 128], TT, hT, HT)
                    bv = proj(slot, lambda kc: v[:, kc, 1, jj * 128:(jj + 1) * 128], TT, hT, HT)
                    s_ = sgb[f % 2]
                    T.op("act", I("activation", out=s_[:, 0:TT], in_=pb[bg][:, 0:TT], func=AF.Silu),
                         reads=[f"pb{bg}"], writes=[f"sg{f % 2}"])
                    T.op("dve", I("tensor_tensor", out=aT[:, f, 0:TT], in0=pb[bv][:, 0:TT], in1=s_[:, 0:TT], op=ALU.mult),
                         reads=[f"pb{bv}", f"sg{f % 2}"], writes=[f"aT{f}"])
            AT = [f"aT{f}" for f in range(22)]
            for j in range(8):
                slot = wnext("dn" + tag, l, j)
                v = slots[slot][:, 0:22 * 128].rearrange("p (k c) -> p k c", c=128)
                b = proj(slot, lambda kc: v[:, kc, :], TT, aT, AT, nk=22)
                T.op("dve", I("scalar_tensor_tensor", out=xT[:, j, 0:TT], in0=pb[b][:, 0:TT], scalar=0.5, in1=xT[:, j, 0:TT],
                                                                     op0=ALU.mult, op1=ALU.add),
                     reads=[f"pb{b}", f"xT{j}"], writes=[f"xT{j}"])

        def gate_and_out(l, gcol, wname, TT, first):
            for half in range(2):
                slot = wnext("in", l, gcol + 512 * half)
                v = wv512(slot)
                for jj in range(4):
                    j = 4 * half + jj
                    b = proj(slot, lambda kc: v[:, kc, jj * 128:(jj + 1) * 128], TT, hT, HT)
                    T.op("act", I("activation", out=gT[:, j, 0:TT], in_=pb[b][:, 0:TT], func=AF.Sigmoid),
                         reads=[f"pb{b}"], writes=[f"gT{j}"])
            YT = [f"yT{j}" for j in range(8)]
            for half in range(2):
                slot = wnext("sq", l, wname, 512 * half)
                v = wv512(slot)
                for jj in range(4):
                    j = 4 * half + jj
                    b = proj(slot, lambda kc: v[:, kc, jj * 128:(jj + 1) * 128], TT, yT, YT)
                    if first:
                        T.op("dve", I("tensor_tensor", out=merged[:, j, 0:TT], in0=pb[b][:, 0:TT], in1=gT[:, j, 0:TT], op=ALU.mult),
                             reads=[f"pb{b}", f"gT{j}"], writes=[f"mg{j}"])
                    else:
                        r_ = rstd[1]
                        T.op("dve", I("tensor_tensor", out=r_[:, 0:TT], in0=pb[b][:, 0:TT], in1=gT[:, j, 0:TT], op=ALU.mult),
                             reads=[f"pb{b}", f"gT{j}"], writes=["rstd1"])
                        T.op("dve", I("tensor_tensor", out=merged[:, j, 0:TT], in0=merged[:, j, 0:TT], in1=r_[:, 0:TT], op=ALU.add),
                             reads=["rstd1", f"mg{j}"], writes=[f"mg{j}"])

        def attention(l, kind, tix, TT):
            import os
            CO["v"] = (kind == "s") or ("att" not in FINEPH)
            off = 0
            qT, off = carve("qT", off, (8, TTP), BF16, sub=True)
            kz0 = off
            kZ, off = carve("kZ", off, (2, 4, 128 + TTP), BF16, tname="-")
            for g_ in range(4):
                if not CO["v"]:
                    T.set_range(f"kT{g_}", kz0, off)
                    T.set_range(f"kTh{g_}", kz0, off)
            Vpad, off = carve("Vpad", off, (5, 4, 2, 128), BF16, sub=True, tname="Vp")
            Eb = []
            for i in range(2):
                v_, off = carve("E", off, (2, 512), BF16, tname=f"E{i}")
                Eb.append(v_)
            dtmp = []
            for i in range(2):
                v_, off = carve("dtmp", off, (256,), F32, tname=f"dtmp{i}")
                dtmp.append(v_)
            kf32, off = carve("kf32", off, (4, 128), F32)
            kcT = []
            for i in range(2):
                v_, off = carve("kcT", off, (2, 4, 128), BF16, tname=f"kcT{i}"); kcT.append(v_)
            assert off <= ARENA, off
            names = ([f"qT{j}" for j in range(8)] + [f"kT{g}" for g in range(4)] + [f"kTh{g}" for g in range(4)] + [f"Vp{b}" for b in range(5)]
                     + ["E0", "E1", "dtmp0", "dtmp1", "kf32", "kcT0", "kcT1"])
            reg(names, 0, off)
            nblk = max(TT // 128, 1)
            is_last_p = (kind == "p" and tix == NTILE - 1)
            want_kout = ((kind == "s") or is_last_p) and not os.environ.get("NOKOUT")

            if kind == "p":
                T.op("dve", I("tensor_copy", out=kZ[:, :, :, 0:128], in_=khalo[:, l, :, :, :]), reads=[f"khalo{l}"], writes=[f"kTh{g}" for g in range(4)])
                T.op("dve", I("tensor_copy", out=Vpad[:, 0, :, :, :], in_=vhalo[:, l, :, :, :]), reads=[f"vhalo{l}"], writes=["Vp0"])
            else:
                T.op("dve", I("memset", Vpad[:, 0, :, :, :], 0.0), writes=["Vp0"])
                T.op("dve", I("memset", Vpad[:, 1, :, :, :], 0.0), writes=["Vp1"])
            if want_kout:
                T.op("dve", I("memset", kf32[:, :, :], 0.0), writes=["kf32"])
            T.op("dve", I("memset", kZ[64:128, 0, :, 128:128 + TT], 0.0), writes=[f"kT{g}" for g in range(4)])
            T.op("dve", I("memset", kZ[0:64, 1, :, 128:128 + TT], 0.0), writes=[f"kT{g}" for g in range(4)])
            if kind == "p" and tix == 0 and l == 0:
                pass
            for b_ in range(1, 5):
                if kind == "p":
                    T.op("dve", I("memset", Vpad[:, b_, :, 0, 64:128], 0.0), writes=[f"Vp{b_}"])
                    T.op("dve", I("memset", Vpad[:, b_, :, 1, 0:64], 0.0), writes=[f"Vp{b_}"])

            def qknorm(b, dst, gidx, dnames):
                s_ = sq[alt()]
                sn = "sq0" if s_ is sq[0] else "sq1"
                T.op("act", I("activation", out=s_[:, 0:TT], in_=pb[b][:, 0:TT], func=AF.Square), reads=[f"pb{b}"], writes=[sn])
                bank, r_, rn = nextstat()
                T.op("pe", mm(pb[bank][:, 0:TT], cs("bd64"), s_[:, 0:TT]), reads=[sn, "cstb"], writes=[f"pb{bank}"])
                rsqrt_from(bank, r_, rn, TT, 1.0 / 64)
                dl = dst if isinstance(dst, list) else [(dst, slice(0, 128))]
                for (d_ap, rows) in dl:
                    T.op("dve", I("scalar_tensor_tensor", out=d_ap, in0=pb[b][rows, 0:TT], scalar=pcol[rows, 0, 16 * l + gidx:16 * l + gidx + 1],
                                                                                   in1=r_[rows, 0:TT], op0=ALU.mult, op1=ALU.mult),
                         reads=[f"pb{b}", rn, "pcol"], writes=dnames)
                return r_, rn

            import os
            SUB = int(os.environ.get("SUB", "99"))

            def ck_(n):
                if n > SUB:
                    T.mute = True

            for half in range(2):
                slot = wnext("in", l, C_AQ + 512 * half)
                v = wv512(slot)
                for jj in range(4):
                    j = 4 * half + jj
                    b = proj(slot, lambda kc: v[:, kc, jj * 128:(jj + 1) * 128], TT, hT, HT)
                    qknorm(b, qT[:, j, 0:TT], P_GQ, [f"qT{j}"])
            ck_(1)
            slot = wnext("kdup", l, 0)
            v = slots[slot][:, 0:8 * 384].rearrange("p (k c) -> p k c", c=384)
            for g in range(4):
                blo = proj(slot, lambda kc: v[:, kc, 64 + g * 64:64 + g * 64 + 128], TT, hT, HT)
                qknorm(blo, [(kZ[0:64, 0, g, 128:128 + TT], slice(0, 64))], P_GK, [f"kT{g}"])
                b = proj(slot, lambda kc: v[:, kc, g * 64:g * 64 + 128], TT, hT, HT)
                r_, rn = qknorm(b, [(kZ[64:128, 1, g, 128:128 + TT], slice(64, 128))], P_GK, [f"kT{g}"])
                if want_kout:
                    n0 = TT - 128 if kind == "p" else 0
                    nn = 128 if kind == "p" else TT
                    T.op("dve", I("scalar_tensor_tensor",
                        out=kf32[64:128, g, 0:nn], in0=pb[b][64:128, n0:n0 + nn], scalar=pcol[64:128, 0, 16 * l + P_GK:16 * l + P_GK + 1],
                        in1=r_[64:128, n0:n0 + nn], op0=ALU.mult, op1=ALU.mult),
                        reads=[f"pb{b}", rn, "pcol"], writes=["kf32"])
                    bt = nextbank()
                    T.op("pe", I("transpose", out=pb[bt][0:nn, 0:128], in_=kf32[:, g, 0:nn], identity=cs32("ident")),
                         reads=["kf32", "cst32"], writes=[f"pb{bt}"])
                    T.op("act", I("activation", out=kstage[0:nn, g, :], in_=pb[bt][0:nn, 64:128], func=AF.Copy),
                         reads=[f"pb{bt}"], writes=["kstage"])
            if want_kout:
                nn = 128 if kind == "p" else TT
                dst = D["nkp"][l] if kind == "p" else D["nks"][l]
                T.dma("sp", "okst", [I("dma_start", out=dst.rearrange("t (g d) -> t g d", g=4), in_=kstage[0:nn, :, :])],
                      reads=["kstage"], is_output=True)
            ck_(2)
            slot = wnext("in256", l, C_AV)
            vv = slots[slot][:, 0:2048].rearrange("p (k c) -> p k c", c=256)
            for bk in range(nblk):
                nt = min(128, TT)
                b = nextbank()
                fns = [mm(pb[b][0:nt, 0:256], hT[:, kc, bk * 128:bk * 128 + nt], vv[:, kc, :], start=(kc == 0), stop=(kc == 7)) for kc in range(8)]
                T.op("pe", seq(fns), reads=[f"ws{slot}"] + HT, writes=[f"pb{b}"])
                src = pb[b][0:nt, 0:256].rearrange("p (g d) -> p g d", g=4)
                T.op("act", I("activation", out=Vpad[0:nt, bk + 1, :, 0, 0:64], in_=src, func=AF.Copy),
                     reads=[f"pb{b}"], writes=[f"Vp{bk + 1}"])
                T.op("dve", I("tensor_copy", out=Vpad[0:nt, bk + 1, :, 1, 64:128], in_=src),
                     reads=[f"pb{b}"], writes=[f"Vp{bk + 1}"])
                if kind == "s" or (is_last_p and bk == nblk - 1):
                    T.op("act", I("activation", out=vstage[0:nt, :], in_=pb[b][0:nt, 0:256], func=AF.Copy),
                         reads=[f"pb{b}"], writes=["vstage"])
                    dst = D["nvp"][l] if kind == "p" else D["nvs"][l]
                    T.dma("sp", "ovst", [I("dma_start", out=dst, in_=vstage[0:nt, :])], reads=["vstage"], is_output=True)

            ck_(3)
            esk = lambda j: dcol(l, 0, j)
            ones_lo = cs("onespad")[:, 0:128]
            ones_hi = cs("onespad")[:, 128:256]

            def finish_pair(g, bo, c0, n, ei):
                d_ = dtmp[ei]
                for jj in range(2):
                    T.op("act", I("activation", out=d_[:, jj * 128:jj * 128 + n], in_=pb[bo][:, 256 + jj * 128:256 + jj * 128 + n],
                                  func=AF.Ln, bias=esk(2 * g + jj)),
                         reads=[f"pb{bo}"] + DER_ALL, writes=[f"dtmp{ei}"])
                    T.op("act", I("activation", out=d_[:, jj * 128:jj * 128 + n], in_=d_[:, jj * 128:jj * 128 + n], func=AF.Exp, scale=-1.0),
                         reads=[f"dtmp{ei}"], writes=[f"dtmp{ei}"])
                for jj in range(2):
                    T.op("dve", I("tensor_tensor", out=yT[:, 2 * g + jj, c0:c0 + n], in0=pb[bo][:, jj * 128:jj * 128 + n],
                                                               in1=d_[:, jj * 128:jj * 128 + n], op=ALU.mult),
                         reads=[f"pb{bo}", f"dtmp{ei}"], writes=[f"yT{2 * g + jj}"])

            if kind == "p":
                its = [(g, pt) for g in range(4) for pt in range(TT // 128)]

                def emit_scores(i):
                    g, pt = its[i]
                    ei = i % 2
                    E = Eb[ei]
                    en = f"E{ei}"
                    hasA = not (tix == 0 and pt == 0)
                    acol = pt * 128
                    bcol = 128 + pt * 128
                    for which, kc0, use in ((0, acol, hasA), (1, bcol, True)):
                        if not use:
                            continue
                        bs = nextbank()
                        fns = []
                        for hh in range(4):
                            j = 2 * g + hh // 2
                            fns.append(mm(pb[bs][:, hh * 128:(hh + 1) * 128], kZ[:, hh % 2, g, kc0:kc0 + 128],
                                          qT[:, j, pt * 128:(pt + 1) * 128]))
                        T.op("pe", seq(fns), reads=[f"kT{g}", f"kTh{g}", f"qT{2 * g}", f"qT{2 * g + 1}"], writes=[f"pb{bs}"])
                        T.op("act", I("activation", out=E[:, which, :], in_=pb[bs][:, :], func=AF.Exp, scale=0.125),
                             reads=[f"pb{bs}"], writes=[en])
                        if which == 0:
                            T.op("dve", I("memset", E[0:64, 0, :].rearrange("p (h q) -> p h q", h=4)[:, :, 64:128], 0.0), writes=[en])
                        else:
                            T.op("dve", I("memset", E[64:128, 1, :].rearrange("p (h q) -> p h q", h=4)[:, :, 0:64], 0.0), writes=[en])

                def emit_pv(i):
                    g, pt = its[i]
                    ei = i % 2
                    E = Eb[ei]
                    en = f"E{ei}"
                    hasA = not (tix == 0 and pt == 0)
                    blkA = pt
                    blkB = pt + 1
                    bo = nextbank()
                    fns = []
                    for isden in (0, 1):
                        for jj in range(2):
                            oc = isden * 256 + jj * 128
                            terms = []
                            for hl in range(2):
                                hh = 2 * jj + hl
                                if hasA:
                                    lhs = (ones_lo if hl == 0 else ones_hi) if isden else Vpad[:, blkA, g, hl, :]
                                    terms.append((lhs, E[:, 0, hh * 128:(hh + 1) * 128]))
                                lhs = (ones_lo if hl == 0 else ones_hi) if isden else Vpad[:, blkB, g, hl, :]
                                terms.append((lhs, E[:, 1, hh * 128:(hh + 1) * 128]))
                            for ti, (lhs, rhs) in enumerate(terms):
                                fns.append(mm(pb[bo][:, oc:oc + 128], lhs, rhs, start=(ti == 0), stop=(ti == len(terms) - 1)))
                    T.op("pe", seq(fns), reads=[en, f"Vp{blkA}", f"Vp{blkB}", "cstb"], writes=[f"pb{bo}"])
                    finish_pair(g, bo, pt * 128, 128, ei)

                emit_scores(0)
                for i in range(len(its)):
                    if i + 1 < len(its):
                        emit_scores(i + 1)
                    emit_pv(i)
                T.op("dve", I("tensor_copy", out=khalo[:, l, :, :, :], in_=kZ[:, :, :, TT:TT + 128]), reads=[f"kT{g}" for g in range(4)], writes=[f"khalo{l}"])
                T.op("dve", I("tensor_copy", out=vhalo[:, l, :, :, :], in_=Vpad[:, 4, :, :, :]), reads=["Vp4"], writes=[f"vhalo{l}"])
            else:
                for s in range(NSS):
                    i2 = s % 2
                    T.dma("sp", f"ckd{i2}", [I("dma_start", out=ckd[s % 2][:, :, h, :], in_=D["ck"][l, s].rearrange("r (g d) -> r g d", g=4))
                                            for h in range(2)], writes=[f"ckd{i2}"])
                    SKIP = os.environ.get("SKIP", "")
                    if "vc" in SKIP:
                        T.mute = True
                    T.op("dve", I("memset", Vc[i2][:], 0.0), writes=[f"Vc{i2}"])
                    T.dma("pool", f"Vc{i2}", [I("dma_start", out=Vc[s % 2][:, :, 0, 0:64], in_=D["cv"][l, s].rearrange("r (g d) -> r g d", g=4)),
                                             I("dma_start", out=Vc[s % 2][:, :, 1, 64:128], in_=D["cv"][l, s].rearrange("r (g d) -> r g d", g=4))],
                          writes=[f"Vc{i2}"])
                    T.mute = ("tr" in SKIP)
                    T.op("dve", I("memset", kcT[i2][64:128, 0, :, :], 0.0), writes=[f"kcT{i2}"])
                    T.op("dve", I("memset", kcT[i2][0:64, 1, :, :], 0.0), writes=[f"kcT{i2}"])
                    for g in range(4):
                        bt = nextbank()
                        T.op("pe", I("transpose", out=pb[bt][:, 0:128], in_=ckd[i2][:, g, :, :].rearrange("p h d -> p (h d)"),
                                                                         identity=cs32("ident")),
                             reads=[f"ckd{i2}", "cst32"], writes=[f"pb{bt}"])
                        T.op("act", I("activation", out=kcT[i2][0:64, 0, g, :], in_=pb[bt][0:64, 0:128], func=AF.Copy),
                             reads=[f"pb{bt}"], writes=[f"kcT{i2}"])
                        T.op("act", I("activation", out=kcT[i2][64:128, 1, g, :], in_=pb[bt][64:128, 0:128], func=AF.Copy),
                             reads=[f"pb{bt}"], writes=[f"kcT{i2}"])
                    T.mute = False
                    ck_(4)
                    for g in range(4):
                        ei = alt()
                        E = Eb[ei]
                        en = f"E{ei}"
                        bs = nextbank()
                        fns = []
                        for hh in range(4):
                            j = 2 * g + hh // 2
                            fns.append(mm(pb[bs][:, hh * 16:(hh + 1) * 16], kcT[i2][:, hh % 2, g, :], qT[:, j, s * LS:(s + 1) * LS]))
                        for hh in range(4):
                            j = 2 * g + hh // 2
                            fns.append(mm(pb[bs][0:64, 64 + hh * 16:64 + (hh + 1) * 16], kZ[:, hh % 2, g, 128:128 + TT], qT[:, j, s * LS:(s + 1) * LS]))
                        T.op("pe", seq(fns), reads=[f"kcT{i2}", f"kT{g}", f"qT{2 * g}", f"qT{2 * g + 1}"], writes=[f"pb{bs}"])
                        T.op("act", I("activation", out=E[:, 0, 0:64], in_=pb[bs][:, 0:64], func=AF.Exp, scale=0.125),
                             reads=[f"pb{bs}"], writes=[en])
                        T.op("act", I("activation", out=E[0:64, 1, 0:64], in_=pb[bs][0:64, 64:128], func=AF.Exp, scale=0.125),
                             reads=[f"pb{bs}"], writes=[en])
                        T.op("dve", I("memset", E[64:128, 1, 0:64], 0.0), writes=[en])
                        T.op("dve", I("tensor_scalar", out=E[0:64, 1, 0:64], in0=E[0:64, 1, 0:64],
                                                                      scalar1=cst32[0:64, CST["rowm"][0] + s:CST["rowm"][0] + s + 1], scalar2=None, op0=ALU.mult),
                             reads=[en, "cst32"], writes=[en])
                        ck_(5)
                        bo = nextbank()
                        fns = []
                        for isden in (0, 1):
                            for jj in range(2):
                                oc = isden * 256 + jj * 128
                                terms = []
                                for hl in range(2):
                                    hh = 2 * jj + hl
                                    lhs = (ones_lo if hl == 0 else ones_hi) if isden else Vc[i2][:, g, hl, :]
                                    terms.append((lhs, E[:, 0, hh * 16:(hh + 1) * 16]))
                                    lhs = (ones_lo if hl == 0 else ones_hi) if isden else Vpad[:, 1, g, hl, :]
                                    terms.append((lhs, E[:, 1, hh * 16:(hh + 1) * 16]))
                                for ti, (lhs, rhs) in enumerate(terms):
                                    fns.append(mm(pb[bo][:, oc:oc + LS], lhs, rhs, start=(ti == 0), stop=(ti == len(terms) - 1)))
                        T.op("pe", seq(fns), reads=[en, f"Vc{i2}", "Vp1", "cstb"], writes=[f"pb{bo}"])
                        ck_(6)
                        finish_pair(g, bo, s * LS, LS, ei)
            ck_(7)
            dbg("ya", l, kind, yT[:, :, 0:TTS], [f"yT{j}" for j in range(8)])
            dbg("q", l, kind, qT[:, :, 0:TTS], [f"qT{j}" for j in range(8)])
            gate_and_out(l, C_GA, "wao", TT, True)
            dbg("m1", l, kind, merged[:, :, 0:TTS], [f"mg{j}" for j in range(8)])
            T.mute = False

        def hgrn(l, kind, tix, TT):
            CO["v"] = (kind == "s") or ("hgrn" not in FINEPH)
            off = 0
            tmp = []
            toffs = []
            for i in range(16):
                toffs.append(off)
                v_, off = carve("ht", off, (TTP // 2,), F32, tname=f"ht{i}"); tmp.append(v_)
            qeT, off = carve("qeT", off, (4, TTP), BF16, sub=True)
            keT, off = carve("keT", off, (4, TTP), BF16, sub=True)
            kd32 = []; kdtok = []; attm = []
            for i in range(4):
                v_, off = carve("kd32", off, (128,), F32, tname=f"kd32{i}"); kd32.append(v_)
                v_, off = carve("kdtok", off, (2, 128), BF16, tname=f"kdtok{i}"); kdtok.append(v_)
                v_, off = carve("attm", off, (128,), BF16, tname=f"attm{i}"); attm.append(v_)
            eL, off = carve("eL", off, (4, 8), F32)
            Vh, off = carve("Vh", off, (4, 4, 128), BF16)
            VhM, off = carve("VhM", off, (4, 128), BF16)
            sgg, off = carve("sgg", off, (4, TTP), BF16, sub=True)
            oTs = []
            for i in range(4):
                v_, _o = carve("oTs", toffs[2 * i], (TTP,), F32, tname=f"oTs{i}"); oTs.append(v_)
            Sbf, off = carve("Sbf", off, (8, 128), BF16, sub=True)
            assert off <= ARENA, off
            names = ([f"ht{i}" for i in range(16)] + [f"qeT{i}" for i in range(4)] + [f"keT{i}" for i in range(4)]
                     + [f"kd32{i}" for i in range(4)] + [f"kdtok{i}" for i in range(4)] + [f"attm{i}" for i in range(4)] + ["eL", "Vh", "VhM"]
                     + [f"sgg{i}" for i in range(4)] + [f"oTs{i}" for i in range(4)] + [f"Sbf{h}" for h in range(8)])
            reg(names, 0, off)
            L = 64 if kind == "p" else LS
            nch = TT // L
            scanm = cs32("scanp") if kind == "p" else cs32("scans")
            trim = cs("tri2") if kind == "p" else cs("tri4")
            nblk = max(TT // 128, 1)
            nt = min(128, TT)
            is_last_p = (kind == "p" and tix == NTILE - 1)

            def Sf(h):
                return Sst[:, l, h, :]

            for pi_ in range(4):
                T.op("dve", I("memset", kdtok[pi_][:, :, :], 0.0), writes=[f"kdtok{pi_}"])
                T.op("dve", I("memset", attm[pi_][:, :], 0.0), writes=[f"attm{pi_}"])
            T.op("dve", I("memset", VhM[:, :, :], 0.0), writes=["VhM"])
            T.op("dve", I("memset", Vh[:, :, :, :], 0.0), writes=["Vh"])
            if kind == "p":
                T.op("act", I("activation", out=Sbf[:, :, :], in_=Sst[:, l, :, :], func=AF.Copy), reads=[f"Sst{l}h{h_}" for h_ in range(8)], writes=[f"Sbf{h}" for h in range(8)])

            for half in range(2):
                s_q = wnext("in", l, C_HQ + 512 * half); vq = wv512(s_q)
                s_f = wnext("in", l, C_HF + 512 * half); vf = wv512(s_f)
                nth = 2 if kind == "p" else 1
                Wd = TT // nth
                HH4 = range(4)
                for th in range(nth):
                    c0 = th * Wd
                    cs_ = slice(c0, c0 + Wd)
                    tt = {hh: tmp[4 * hh:4 * hh + 4] for hh in HH4}
                    tnn = {hh: [f"ht{4 * hh + i}" for i in range(4)] for hh in HH4}
                    bfs, bqs = {}, {}
                    for hh in HH4:
                        for which, vv_, ss_, dd in ((0, vf, s_f, bfs), (1, vq, s_q, bqs)):
                            b = hh
                            o0 = which * 256
                            fns = [mm(pb[b][:, o0:o0 + Wd], vv_[:, kc, hh * 128:(hh + 1) * 128], hT[:, kc, cs_], start=(kc == 0), stop=(kc == 7)) for kc in range(8)]
                            T.op("pe", seq(fns), reads=[f"ws{ss_}"] + HT, writes=[f"pb{b}"])
                            dd[hh] = (b, o0)
                    for hh in HH4:
                        t1, t2, t3, t4 = tt[hh]; tn = tnn[hh]
                        b, o0 = bfs[hh]
                        T.op("act", I("activation", out=t1[:, 0:Wd], in_=pb[b][:, o0:o0 + Wd], func=AF.Sigmoid), reads=[f"pb{b}"], writes=[tn[0]])
                    for hh in HH4:
                        t1, t2, t3, t4 = tt[hh]; tn = tnn[hh]
                        b, o0 = bqs[hh]
                        T.op("act", I("activation", out=t4[:, 0:Wd], in_=pb[b][:, o0:o0 + Wd], func=AF.Silu), reads=[f"pb{b}"], writes=[tn[3]])
                    for hh in HH4:
                        h = 4 * half + hh
                        t1, t2, t3, t4 = tt[hh]; tn = tnn[hh]
                        T.op("act", I("activation", out=t2[:, 0:Wd], in_=t1[:, 0:Wd], func=AF.Ln, scale=dcol(l, 2, h), bias=dcol(l, 1, h)),
                             reads=[tn[0]] + DER_ALL, writes=[tn[1]])
                    for hh in HH4:
                        h = 4 * half + hh
                        t1, t2, t3, t4 = tt[hh]; tn = tnn[hh]
                        T.op("dve", I("tensor_scalar", out=t1[:, 0:Wd], in0=t1[:, 0:Wd], scalar1=dcol(l, 3, h), scalar2=dcol(l, 2, h), op0=ALU.mult, op1=ALU.add),
                             reads=[tn[0]] + DER_ALL, writes=[tn[0]])
                        T.op("dve", I("tensor_scalar", out=t2[:, 0:Wd], in0=t2[:, 0:Wd], scalar1=-60.0, scalar2=None, op0=ALU.max),
                             reads=[tn[1]], writes=[tn[1]])
                        T.op("dve", I("tensor_tensor_scan", out=t3[:, 0:Wd], data0=scanm[:, 0:Wd], data1=t2[:, 0:Wd], initial=0.0, op0=ALU.mult, op1=ALU.add),
                             reads=[tn[1], "cst32"], writes=[tn[2]])
                        T.op("dve", I("tensor_scalar", out=t2[:, 0:Wd], in0=t3[:, 0:Wd], scalar1=-1.0, scalar2=80.0, op0=ALU.mult, op1=ALU.min),
                             reads=[tn[2]], writes=[tn[1]])
                    for hh in HH4:
                        t1, t2, t3, t4 = tt[hh]; tn = tnn[hh]
                        T.op("act", I("activation", out=t3[:, 0:Wd], in_=t3[:, 0:Wd], func=AF.Exp), reads=[tn[2]], writes=[tn[2]])
                        T.op("act", I("activation", out=t2[:, 0:Wd], in_=t2[:, 0:Wd], func=AF.Exp), reads=[tn[1]], writes=[tn[1]])
                    for hh in HH4:
                        t1, t2, t3, t4 = tt[hh]; tn = tnn[hh]
                        T.op("dve", I("tensor_tensor", out=qeT[:, hh, cs_], in0=t4[:, 0:Wd], in1=t3[:, 0:Wd], op=ALU.mult),
                             reads=[tn[3], tn[2]], writes=[f"qeT{hh}"])
                        T.op("dve", I("tensor_tensor", out=keT[:, hh, cs_], in0=t1[:, 0:Wd], in1=t2[:, 0:Wd], op=ALU.mult),
                             reads=[tn[0], tn[1]], writes=[f"keT{hh}"])
                        nchh = Wd // L
                        T.op("dve", I("tensor_copy", out=eL[:, hh, th * nchh:(th + 1) * nchh], in_=t3[:, 0:Wd].rearrange("p (c l) -> p c l", l=L)[:, :, L - 1]),
                             reads=[tn[2]], writes=["eL"])
                SUBH = int(os.environ.get("SUBH", "99"))
                if SUBH < 1:
                    T.mute = True
                s_i = wnext("in", l, C_HI + 512 * half); vi = wv512(s_i)
                for bk in range(nblk):
                    b = nextbank()
                    fns = [mm(pb[b][0:nt, 0:512], hT[:, kc, bk * 128:bk * 128 + nt], vi[:, kc, :], start=(kc == 0), stop=(kc == 7)) for kc in range(8)]
                    T.op("pe", seq(fns), reads=[f"ws{s_i}"] + HT, writes=[f"pb{b}"])
                    T.op("act", I("activation", out=Vh[0:nt, bk, :, :], in_=pb[b][0:nt, 0:512].rearrange("p (h d) -> p h d", h=4), func=AF.Copy),
                         reads=[f"pb{b}"], writes=["Vh"])
                s_g = wnext("in", l, C_HG + 512 * half); vg = wv512(s_g)
                for hh in range(4):
                    bg = proj(s_g, lambda kc: vg[:, kc, hh * 128:(hh + 1) * 128], TT, hT, HT)
                    T.op("act", I("activation", out=sgg[:, hh, 0:TT], in_=pb[bg][:, 0:TT], func=AF.Silu), reads=[f"pb{bg}"], writes=[f"sgg{hh}"])
                if SUBH < 2:
                    T.mute = True
                def onorm(hh, h):
                    oT = oTs[hh]
                    on = f"oTs{hh}"
                    s_ = sq[alt()]
                    sn = "sq0" if s_ is sq[0] else "sq1"
                    bank, r_, rn = nextstat()
                    T.op("act", I("activation", out=s_[:, 0:TT], in_=oT[:, 0:TT], func=AF.Square), reads=[on], writes=[sn])
                    T.op("pe", mm(pb[bank][:, 0:TT], cs("ones"), s_[:, 0:TT]), reads=[sn, "cstb"], writes=[f"pb{bank}"])
                    rsqrt_from(bank, r_, rn, TT, 1.0 / 128)
                    T.op("dve", I("scalar_tensor_tensor", out=oT[:, 0:TT], in0=oT[:, 0:TT], scalar=pc(l, P_GO, 0), in1=r_[:, 0:TT], op0=ALU.mult, op1=ALU.mult),
                         reads=[on, rn, "pcol"], writes=[on])
                    T.op("dve", I("tensor_tensor", out=yT[:, h, 0:TT], in0=oT[:, 0:TT], in1=sgg[:, hh, 0:TT], op=ALU.mult),
                         reads=[on, f"sgg{hh}"], writes=[f"yT{h}"])

                if kind == "p":
                    HH = range(4)
                    for pr in range(TT // 128):
                        c0 = pr * 128
                        for hh in HH:
                            for cc in range(2):
                                T.op("dve", I("tensor_scalar", out=kd32[hh][:, cc * 64:(cc + 1) * 64], in0=keT[:, hh, c0 + cc * 64:c0 + (cc + 1) * 64],
                                              scalar1=eL[:, hh, 2 * pr + cc:2 * pr + cc + 1], scalar2=None, op0=ALU.mult),
                                     reads=[f"keT{hh}", "eL"], writes=[f"kd32{hh}"])
                        for hh in HH:
                            bk_ = 6 + hh % 2
                            T.op("pe", seq([I("transpose", out=pb[bk_][:, 0:128], in_=kd32[hh][:, :], identity=cs32("ident")),
                                            mm(pb[bk_][:, 128:256], keT[:, hh, c0:c0 + 128], qeT[:, hh, c0:c0 + 128])]),
                                 reads=[f"kd32{hh}", "cst32", f"keT{hh}", f"qeT{hh}"], writes=[f"pb{bk_}"])
                            T.op("act", I("activation", out=kdtok[hh][0:64, 0, :], in_=pb[bk_][0:64, 0:128], func=AF.Copy), reads=[f"pb{bk_}"], writes=[f"kdtok{hh}"])
                            T.op("act", I("activation", out=kdtok[hh][64:128, 1, :], in_=pb[bk_][64:128, 0:128], func=AF.Copy), reads=[f"pb{bk_}"], writes=[f"kdtok{hh}"])
                            T.op("dve", I("tensor_tensor", out=attm[hh][:, :], in0=pb[bk_][:, 128:256], in1=trim, op=ALU.mult),
                                 reads=[f"pb{bk_}", "cstb"], writes=[f"attm{hh}"])
                        for hh in HH:
                            h = 4 * half + hh
                            T.op("pe", seq([mm(pb[hh][:, 0:128], kdtok[hh][:, 0, :], Vh[:, pr, hh, :]),
                                            mm(pb[hh][:, 128:256], kdtok[hh][:, 1, :], Vh[:, pr, hh, :]),
                                            mm(pb[hh][:, 256:384], Vh[:, pr, hh, :], attm[hh][:, :], start=True, stop=False),
                                            mm(pb[hh][:, 256:320], Sbf[:, h, :], qeT[:, hh, c0:c0 + 64], start=False, stop=False)]),
                                 reads=[f"kdtok{hh}", "Vh", f"attm{hh}", f"Sbf{h}", f"qeT{hh}"], writes=[f"pb{hh}"])
                        for hh in HH:
                            h = 4 * half + hh
                            T.op("dve", I("scalar_tensor_tensor", out=Sf(h), in0=Sf(h), scalar=eL[:, hh, 2 * pr:2 * pr + 1], in1=pb[hh][:, 0:128],
                                          op0=ALU.mult, op1=ALU.add),
                                 reads=[f"pb{hh}", "eL", f"Sst{l}h{h}"], writes=[f"Sst{l}h{h}"])
                            T.op("act", I("activation", out=Sbf[:, h, :], in_=Sf(h), func=AF.Copy), reads=[f"Sst{l}h{h}"], writes=[f"Sbf{h}"])
                        for hh in HH:
                            h = 4 * half + hh
                            T.op("pe", mm(pb[hh][:, 320:384], Sbf[:, h, :], qeT[:, hh, c0 + 64:c0 + 128], start=False, stop=True),
                                 reads=[f"Sbf{h}", f"qeT{hh}"], writes=[f"pb{hh}"])
                        for hh in HH:
                            h = 4 * half + hh
                            T.op("dve", I("scalar_tensor_tensor", out=Sf(h), in0=Sf(h), scalar=eL[:, hh, 2 * pr + 1:2 * pr + 2], in1=pb[hh][:, 128:256],
                                          op0=ALU.mult, op1=ALU.add),
                                 reads=[f"pb{hh}", "eL", f"Sst{l}h{h}"], writes=[f"Sst{l}h{h}"])
                            T.op("act", I("activation", out=Sbf[:, h, :], in_=Sf(h), func=AF.Copy), reads=[f"Sst{l}h{h}"], writes=[f"Sbf{h}"])
                            T.op("act", I("activation", out=oTs[hh][:, c0:c0 + 128], in_=pb[hh][:, 256:384], func=AF.Copy), reads=[f"pb{hh}"], writes=[f"oTs{hh}"])
                    if SUBH < 3:
                        T.mute = True
                    for hh in HH:
                        onorm(hh, 4 * half + hh)
                else:
                  for hh in range(4):
                    h = 4 * half + hh
                    oT = oTs[hh]
                    on = f"oTs{hh}"
                    if True:
                        pi = hh
                        for s in range(NSS):
                            T.op("dve", I("tensor_scalar", out=kd32[pi][:, s * LS:(s + 1) * LS], in0=keT[:, hh, s * LS:(s + 1) * LS],
                                                                            scalar1=eL[:, hh, s:s + 1], scalar2=None, op0=ALU.mult),
                                 reads=[f"keT{hh}", "eL"], writes=[f"kd32{pi}"])
                        T.op("pe", seq([I("transpose", out=pb[6][0:64, 0:128], in_=kd32[pi][:, 0:64], identity=cs32("ident")),
                                        mm(pb[6][0:64, 128:192], keT[:, hh, 0:64], qeT[:, hh, 0:64])]),
                             reads=[f"kd32{pi}", "cst32", f"keT{hh}", f"qeT{hh}"], writes=["pb6"])
                        T.op("act", I("activation", out=kdtok[pi][0:64, 0, :], in_=pb[6][0:64, 0:128], func=AF.Copy), reads=["pb6"], writes=[f"kdtok{pi}"])
                        T.op("dve", I("tensor_tensor", out=attm[pi][0:64, 0:64], in0=pb[6][0:64, 128:192], in1=trim[0:64, :], op=ALU.mult),
                             reads=["pb6", "cstb"], writes=[f"attm{pi}"])
                        T.op("pe", mm(pb[7][:, 256:320], Vh[:, 0, hh, :], attm[pi][:, 0:64], start=True, stop=False),
                             reads=["Vh", f"attm{pi}"], writes=["pb7b"])
                        for s in range(NSS):
                            T.dma("sp", f"Ssm{h}", [I("dma_start", out=Ssm[:, h, :], in_=D["shg"][l, s, h])], writes=[f"Ssm{h}"])
                            T.op("act", I("activation", out=Sbf[:, h, :], in_=Ssm[:, h, :], func=AF.Copy), reads=[f"Ssm{h}"], writes=[f"Sbf{h}"])
                            T.op("pe", mm(pb[7][:, 256 + s * LS:256 + (s + 1) * LS], Sbf[:, h, :], qeT[:, hh, s * LS:(s + 1) * LS], start=False, stop=(s == NSS - 1)),
                                 reads=[f"Sbf{h}", f"qeT{hh}"], writes=["pb7b"])
                            T.op("dve", I("tensor_scalar", out=VhM[0:64, hh, :], in0=Vh[0:64, 0, hh, :],
                                                                     scalar1=cst32[0:64, CST["rowm"][0] + s:CST["rowm"][0] + s + 1], scalar2=None, op0=ALU.mult),
                                 reads=["Vh", "cst32"], writes=["VhM"])
                            T.op("pe", mm(pb[6][:, 256:384], kdtok[pi][:, 0, :], VhM[:, hh, :]), reads=[f"kdtok{pi}", "VhM"], writes=["pb6b"])
                            T.op("dve", I("scalar_tensor_tensor", out=Ssm[:, h, :], in0=Ssm[:, h, :], scalar=eL[:, hh, s:s + 1], in1=pb[6][:, 256:384],
                                                                              op0=ALU.mult, op1=ALU.add),
                                 reads=["pb6b", "eL", f"Ssm{h}"], writes=[f"Ssm{h}"])
                            T.dma("sp", f"ohs{h}", [I("dma_start", out=D["nhs"][l, s, h], in_=Ssm[:, h, :])], reads=[f"Ssm{h}"], is_output=True)
                        T.op("act", I("activation", out=oT[:, 0:64], in_=pb[7][:, 256:320], func=AF.Copy), reads=["pb7b"], writes=[on])
                    onorm(hh, h)
            if is_last_p:
                T.dma("sp", "ohp", [I("dma_start", out=D["nhp"][l].rearrange("h k v -> k h v"), in_=Sst[:, l, :, :])],
                      reads=[f"Sst{l}h{h}" for h in range(8)], is_output=True)
            T.mute = False
            dbg("yb", l, kind, yT[:, :, 0:TTS], [f"yT{j}" for j in range(8)])
            gate_and_out(l, C_GB, "who", TT, False)
            dbg("m2", l, kind, merged[:, :, 0:TTS], [f"mg{j}" for j in range(8)])

        def lru(l, kind, tix, TT):
            CO["v"] = (kind == "s") or ("lru" not in FINEPH)
            off = 0
            sets = []
            for i in range(2):
                d = {}
                d["X"], off = carve("X", off, (TTP + 12,), F32, tname=f"L{i}X")
                for nm in ("xc", "r", "ig", "a", "a2", "h", "gl"):
                    d[nm], off = carve(nm, off, (TTP,), F32, tname=f"L{i}{nm}")
                d["xcb"], off = carve("xcb", off, (TTP,), BF16, tname=f"L{i}xcb")
                sets.append(d)
            assert off <= ARENA, off
            keys = ("X", "xc", "r", "ig", "a", "a2", "h", "gl", "xcb")
            reg([f"L{i}{k}" for i in range(2) for k in keys], 0, off)
            nseq = 1 if kind == "p" else NSS
            Ls = TT // nseq
            is_last_p = (kind == "p" and tix == NTILE - 1)
            if kind == "s":
                T.dma("sp", "h0s", [I("dma_start", out=h0s[:, s, :], in_=D["slr"][l, s].rearrange("(j p) -> p j", p=128), allow_slow_non_contiguous=True)
                                    for s in range(NSS)], writes=["h0s"])
            for half in range(2):
                s_x = wnext("in", l, C_LX + 512 * half); vx = wv512(s_x)
                s_g = wnext("in", l, C_LG + 512 * half); vg = wv512(s_g)
                for jj in range(4):
                    j = 4 * half + jj
                    d = sets[j % 2]
                    n = lambda k, j=j: f"L{j % 2}{k}"
                    Xv = d["X"][:, 0:nseq * (Ls + 3)].rearrange("p (s t) -> p s t", s=nseq)
                    v3 = lambda ap: ap[:, 0:TT].rearrange("p (s t) -> p s t", s=nseq)
                    bx = proj(s_x, lambda kc: vx[:, kc, jj * 128:(jj + 1) * 128], TT, hT, HT)
                    if kind == "p":
                        T.op("dve", I("tensor_copy", out=Xv[:, 0, 0:3], in_=lxhalo[:, l, j, :]), reads=[f"lxhalo{l}"], writes=[n("X")])
                    else:
                        T.dma("sp", f"xhs{j}", [I("dma_start", out=xhs[:, j, s, :], in_=D["scv"][l, s, :, j * 128:(j + 1) * 128].rearrange("t p -> p t"),
                                                                             allow_slow_non_contiguous=True) for s in range(NSS)], writes=[f"xhs{j}"])
                        T.op("dve", I("tensor_copy", out=Xv[:, :, 0:3], in_=xhs[:, j, :, :]), reads=[f"xhs{j}"], writes=[n("X")])
                    T.op("act", I("activation", out=Xv[:, :, 3:3 + Ls], in_=pb[bx][:, 0:TT].rearrange("p (s t) -> p s t", s=nseq), func=AF.Copy),
                         reads=[f"pb{bx}"], writes=[n("X")])
                    if kind == "p":
                        T.op("dve", I("tensor_copy", out=lxhalo[:, l, j, :], in_=Xv[:, 0, Ls:Ls + 3]), reads=[n("X")], writes=[f"lxhalo{l}"])
                        if is_last_p:
                            T.op("dve", I("tensor_copy", out=cstage[:, j, 0, :], in_=Xv[:, 0, Ls:Ls + 3]), reads=[n("X")], writes=["cstage"])
                    else:
                        T.op("dve", I("tensor_copy", out=cstage[:, j, :, :], in_=Xv[:, :, Ls:Ls + 3]), reads=[n("X")], writes=["cstage"])
                    xc3 = v3(d["xc"])
                    T.op("dve", I("tensor_scalar", out=xc3, in0=Xv[:, :, 0:Ls], scalar1=pc(l, P_CW + 0, j), scalar2=pc(l, P_CB, j), op0=ALU.mult, op1=ALU.add),
                         reads=[n("X"), "pcol"], writes=[n("xc")])
                    for k in range(1, 4):
                        T.op("dve", I("scalar_tensor_tensor", out=xc3, in0=Xv[:, :, k:k + Ls], scalar=pc(l, P_CW + k, j), in1=xc3, op0=ALU.mult, op1=ALU.add),
                             reads=[n("X"), n("xc"), "pcol"], writes=[n("xc")])
                    T.op("act", I("activation", out=d["xcb"][:, 0:TT], in_=d["xc"][:, 0:TT], func=AF.Copy), reads=[n("xc")], writes=[n("xcb")])
                    ba = nextbank()
                    T.op("pe", mm(pb[ba][:, 0:TT], bda[:, l, 0, j, :], d["xcb"][:, 0:TT]), reads=["bda", n("xcb")], writes=[f"pb{ba}"])
                    bi = nextbank()
                    T.op("pe", mm(pb[bi][:, 0:TT], bda[:, l, 1, j, :], d["xcb"][:, 0:TT]), reads=["bda", n("xcb")], writes=[f"pb{bi}"])
                    T.op("act", I("activation", out=d["r"][:, 0:TT], in_=pb[ba][:, 0:TT], func=AF.Sigmoid, bias=pc(l, P_BA, j)), reads=[f"pb{ba}", "pcol"], writes=[n("r")])
                    T.op("act", I("activation", out=d["ig"][:, 0:TT], in_=pb[bi][:, 0:TT], func=AF.Sigmoid, bias=pc(l, P_BX, j)), reads=[f"pb{bi}", "pcol"], writes=[n("ig")])
                    T.op("act", I("activation", out=d["a"][:, 0:TT], in_=d["r"][:, 0:TT], func=AF.Exp, scale=dcol(l, 4, j)), reads=[n("r")] + DER_ALL, writes=[n("a")])
                    T.op("act", I("activation", out=d["a2"][:, 0:TT], in_=d["r"][:, 0:TT], func=AF.Exp, scale=dcol(l, 5, j)), reads=[n("r")] + DER_ALL, writes=[n("a2")])
                    T.op("dve", I("tensor_scalar", out=d["a2"][:, 0:TT], in0=d["a2"][:, 0:TT], scalar1=-1.0, scalar2=1.0, op0=ALU.mult, op1=ALU.add), reads=[n("a2")], writes=[n("a2")])
                    T.op("act", I("activation", out=d["a2"][:, 0:TT], in_=d["a2"][:, 0:TT], func=AF.Sqrt), reads=[n("a2")], writes=[n("a2")])
                    T.op("dve", I("tensor_tensor", out=d["ig"][:, 0:TT], in0=d["ig"][:, 0:TT], in1=d["xc"][:, 0:TT], op=ALU.mult), reads=[n("ig"), n("xc")], writes=[n("ig")])
                    T.op("dve", I("tensor_tensor", out=d["a2"][:, 0:TT], in0=d["a2"][:, 0:TT], in1=d["ig"][:, 0:TT], op=ALU.mult), reads=[n("a2"), n("ig")], writes=[n("a2")])
                    if kind == "p":
                        T.op("dve", I("tensor_tensor_scan", out=d["h"][:, 0:TT], data0=d["a"][:, 0:TT], data1=d["a2"][:, 0:TT], initial=hprev[:, l, j:j + 1], op0=ALU.mult, op1=ALU.add),
                             reads=[n("a"), n("a2"), f"hprev{l}"], writes=[n("h")])
                        T.op("dve", I("tensor_copy", out=hprev[:, l, j:j + 1], in_=d["h"][:, TT - 1:TT]), reads=[n("h")], writes=[f"hprev{l}"])
                    else:
                        for s in range(NSS):
                            T.op("dve", I("tensor_tensor_scan", out=d["h"][:, s * Ls:(s + 1) * Ls], data0=d["a"][:, s * Ls:(s + 1) * Ls], data1=d["a2"][:, s * Ls:(s + 1) * Ls],
                                                                                   initial=h0s[:, s, j:j + 1], op0=ALU.mult, op1=ALU.add),
                                 reads=[n("a"), n("a2"), "h0s"], writes=[n("h")])
                        T.op("dve", I("tensor_copy", out=hstage[:, j, :], in_=d["h"][:, 0:TT].rearrange("p (s t) -> p s t", s=NSS)[:, :, Ls - 1]), reads=[n("h")], writes=["hstage"])
                    bg = proj(s_g, lambda kc: vg[:, kc, jj * 128:(jj + 1) * 128], TT, hT, HT)
                    T.op("act", I("activation", out=d["gl"][:, 0:TT], in_=pb[bg][:, 0:TT], func=AF.Gelu_apprx_tanh), reads=[f"pb{bg}"], writes=[n("gl")])
                    T.op("dve", I("tensor_tensor", out=yT[:, j, 0:TT], in0=d["h"][:, 0:TT], in1=d["gl"][:, 0:TT], op=ALU.mult), reads=[n("h"), n("gl")], writes=[f"yT{j}"])
            if kind == "s":
                T.dma("sp", "ocs", [I("dma_start", out=D["ncs"][l, s, t].rearrange("(j p) -> p j", p=128), in_=cstage[:, :, s, t], allow_slow_non_contiguous=True)
                                    for s in range(NSS) for t in range(3)], reads=["cstage"], is_output=True)
                T.dma("sp", "ols", [I("dma_start", out=D["nls"][l, s].rearrange("(j p) -> p j", p=128), in_=hstage[:, :, s], allow_slow_non_contiguous=True)
                                    for s in range(NSS)], reads=["hstage"], is_output=True)
            elif is_last_p:
                T.dma("sp", "ocsP", [I("dma_start", out=D["ncp"][l, t].rearrange("(j p) -> p j", p=128), in_=cstage[:, :, 0, t], allow_slow_non_contiguous=True)
                                     for t in range(3)], reads=["cstage"], is_output=True)
                T.dma("sp", "olsP", [I("dma_start", out=D["nlp"][l].rearrange("(j p) -> p j", p=128), in_=hprev[:, l, :], allow_slow_non_contiguous=True)],
                      reads=[f"hprev{l}"], is_output=True)
            dbg("yc", l, kind, yT[:, :, 0:TTS], [f"yT{j}" for j in range(8)])
            gate_and_out(l, C_GC, "wlo", TT, False)
            dbg("m3", l, kind, merged[:, :, 0:TTS], [f"mg{j}" for j in range(8)])

        def load_x(kind, tix, TT):
            src = D["xp"] if kind == "p" else D["xs"]
            for bk in range(max(TT // 128, 1)):
                nt = min(128, TT)
                xi = xin[bk % 2]
                r0 = (tix * TTP if kind == "p" else 0) + bk * 128
                T.dma("sp", "xin0", [I("dma_start", out=xi[0:nt, :], in_=src[r0:r0 + nt, :])], writes=["xin0"])
                for j in range(8):
                    b = nextbank()
                    T.op("pe", I("transpose", out=pb[b][:, 0:nt], in_=xi[0:nt, j * 128:(j + 1) * 128], identity=cs32("ident", slice(0, nt))[:, 0:nt]),
                         reads=["xin0", "cst32"], writes=[f"pb{b}"])
                    eng = "act" if j % 2 else "dve"
                    if eng == "act":
                        T.op("act", I("activation", out=xT[:, j, bk * 128:bk * 128 + nt], in_=pb[b][:, 0:nt], func=AF.Copy), reads=[f"pb{b}"], writes=[f"xT{j}"])
                    else:
                        T.op("dve", I("tensor_copy", out=xT[:, j, bk * 128:bk * 128 + nt], in_=pb[b][:, 0:nt]), reads=[f"pb{b}"], writes=[f"xT{j}"])

        def store_x(kind, tix, TT):
            dst = D["yp"] if kind == "p" else D["ys"]
            for bk in range(max(TT // 128, 1)):
                nt = min(128, TT)
                xi = xin[bk % 2]
                r0 = (tix * TTP if kind == "p" else 0) + bk * 128
                for j in range(8):
                    b = nextbank()
                    T.op("pe", I("transpose", out=pb[b][0:nt, 0:128], in_=xT[:, j, bk * 128:bk * 128 + nt], identity=cs32("ident")),
                         reads=[f"xT{j}", "cst32"], writes=[f"pb{b}"])
                    if j % 2:
                        T.op("act", I("activation", out=xi[0:nt, j * 128:(j + 1) * 128], in_=pb[b][0:nt, 0:128], func=AF.Copy), reads=[f"pb{b}"], writes=["xin0"])
                    else:
                        T.op("dve", I("tensor_copy", out=xi[0:nt, j * 128:(j + 1) * 128], in_=pb[b][0:nt, 0:128]), reads=[f"pb{b}"], writes=["xin0"])
                T.dma("sp", "xin0", [I("dma_start", out=dst[r0:r0 + nt, :], in_=xi[0:nt, :])], reads=["xin0"], is_output=True)

        def ffn_phase(l, tag, prow, TT, kind="p"):
            CO["v"] = (kind == "s") or ("ffn" not in FINEPH)
            off = 0
            aT, off = carve("aT", off, (22, TTP), BF16, sub=True)
            sgb = []
            for i in range(2):
                v_, off = carve("sg", off, (TTP,), F32, tname=f"sg{i}"); sgb.append(v_)
            assert off <= ARENA
            reg([f"aT{f}" for f in range(22)] + ["sg0", "sg1"], 0, off)
            ffn(l, tag, prow, TT, aT, sgb)

        phase_ctr = {"n": 0}

        def phase(nm):
            phase_ctr["n"] += 1
            if stop is not None and phase_ctr["n"] > stop:
                raise _Stop(nm)

        def run_all():
          for kind, tix in tiles:
            TT = TTP if kind == "p" else TTS
            phase("load")
            load_x(kind, tix, TT)
            for l in range(nlayer):
                phase("ffn1")
                ffn_phase(l, "1", P_G1, TT, kind)
                phase("attn")
                dbg("x1", l, kind, xT[:, :, 0:TTS], [f"xT{j}" for j in range(8)])
                rmsnorm(l, P_GM, TT, None)
                attention(l, kind, tix, TT)
                phase("hgrn")
                hgrn(l, kind, tix, TT)
                phase("lru")
                lru(l, kind, tix, TT)
                phase("out")
                for j in range(8):
                    if j % 2:
                        T.op("act", I("activation", out=yT[:, j, 0:TT], in_=merged[:, j, 0:TT], func=AF.Copy), reads=[f"mg{j}"], writes=[f"yT{j}"])
                    else:
                        T.op("dve", I("tensor_copy", out=yT[:, j, 0:TT], in_=merged[:, j, 0:TT]), reads=[f"mg{j}"], writes=[f"yT{j}"])
                YT = [f"yT{j}" for j in range(8)]
                for half in range(2):
                    slot = wnext("sq", l, "wout", 512 * half)
                    v = wv512(slot)
                    for jj in range(4):
                        j = 4 * half + jj
                        b = proj(slot, lambda kc: v[:, kc, jj * 128:(jj + 1) * 128], TT, yT, YT)
                        T.op("dve", I("tensor_tensor", out=xT[:, j, 0:TT], in0=pb[b][:, 0:TT], in1=xT[:, j, 0:TT], op=ALU.add),
                             reads=[f"pb{b}", f"xT{j}"], writes=[f"xT{j}"])
                dbg("xm", l, kind, xT[:, :, 0:TTS], [f"xT{j}" for j in range(8)])
                ffn_phase(l, "2", P_G2, TT, kind)
            phase("store")
            store_x(kind, tix, TT)
        try:
            run_all()
            assert wstate["i"] == len(wsched)
        except _Stop as ex:
            print("build stopped before phase", ex)
        T.finish()
        T.emit()
    return nc


TT_DUMMY = None
_NC_CACHE = {}


def _get_nc(key=(NTILE, True, 2)):
    if key not in _NC_CACHE:
        _NC_CACHE[key] = build(*key)
    return _NC_CACHE[key]


PROMPT_CORES = (0, 1, 4, 5)


def make_in_maps(inp):
    f = lambda a: np.ascontiguousarray(np.asarray(a, dtype=np.float32))
    P = np.zeros((NPR, 1024), np.float32)
    for l in range(2):
        b = 16 * l
        P[b + P_G1] = inp["norm_ffn1"][l]; P[b + P_GM] = inp["norm_mix"][l]; P[b + P_G2] = inp["norm_ffn2"][l]
        P[b + P_LB] = inp["hgrn_lb_logits"][l]
        P[b + P_CW:b + P_CW + 4] = inp["conv_w"][l]
        P[b + P_CB] = inp["conv_b"][l]; P[b + P_BA] = inp["lru_b_a"][l]; P[b + P_BX] = inp["lru_b_x"][l]; P[b + P_LAM] = inp["lru_lambda"][l]
        P[b + P_GQ] = np.tile(inp["q_norm"][l], 16); P[b + P_GK] = np.tile(inp["k_norm"][l], 16)
        P[b + P_SINK] = np.repeat(inp["attn_sinks"][l], 64); P[b + P_GO] = np.tile(inp["hgrn_o_norm"][l], 8)
    cst = make_consts()
    shared = {"w1u": f(inp["w_ffn1_up"]), "w1d": f(inp["w_ffn1_down"]), "win": f(inp["w_in"]), "wao": f(inp["w_attn_o"]),
              "who": f(inp["w_hgrn_o"]), "wlo": f(inp["w_lru_o"]), "wout": f(inp["w_out"]), "w2u": f(inp["w_ffn2_up"]),
              "w2d": f(inp["w_ffn2_down"]), "lwa": f(inp["lru_w_a"]), "lwx": f(inp["lru_w_x"]), "P": P, "cst": cst}
    maps = []
    zero_xp = np.zeros((SEQ, 1024), np.float32)
    for c in range(8):
        s0 = NSS * c
        m = dict(shared)
        m["xp"] = f(inp["x_prompt"][PROMPT_CORES.index(c)]) if c in PROMPT_CORES else zero_xp
        m["xs"] = f(inp["x_sample"][s0:s0 + NSS]).reshape(TTS, 1024)
        m["ck"] = f(inp["cache_attn_k"][:, s0:s0 + NSS]).reshape(2, NSS, 128, 256)
        m["cv"] = f(inp["cache_attn_v"][:, s0:s0 + NSS]).reshape(2, NSS, 128, 256)
        m["shg"] = f(inp["state_hgrn"][:, s0:s0 + NSS])
        m["scv"] = f(inp["state_conv"][:, s0:s0 + NSS])
        m["slr"] = f(inp["state_lru"][:, s0:s0 + NSS])
        maps.append(m)
    return maps


def assemble(res):
    R = res
    cat = lambda k, ax, cores: np.concatenate([R[c][k] for c in cores], axis=ax)
    pc_ = PROMPT_CORES
    ac = range(8)
    yp = np.stack([R[c]["yp"] for c in pc_], 0)
    ys = np.concatenate([R[c]["ys"].reshape(NSS, LS, 1024) for c in ac], 0)
    nkp = np.stack([R[c]["nkp"].reshape(2, 128, 4, 64) for c in pc_], 1)
    nvp = np.stack([R[c]["nvp"].reshape(2, 128, 4, 64) for c in pc_], 1)
    nhp = np.stack([R[c]["nhp"] for c in pc_], 1)
    ncp = np.stack([R[c]["ncp"] for c in pc_], 1)
    nlp = np.stack([R[c]["nlp"] for c in pc_], 1)
    nks = np.concatenate([R[c]["nks"].reshape(2, NSS, LS, 4, 64) for c in ac], 1)
    nvs = np.concatenate([R[c]["nvs"].reshape(2, NSS, LS, 4, 64) for c in ac], 1)
    nhs = cat("nhs", 1, ac)
    ncs = cat("ncs", 1, ac)
    nls = cat("nls", 1, ac)
    outs = (yp, ys, nkp, nvp, nhp, ncp, nlp, nks, nvs, nhs, ncs, nls)
    return tuple(np.ascontiguousarray(o.astype(np.float32)) for o in outs)


def kernel(**inputs):
    nc = _get_nc()
    maps = make_in_maps(inputs)
    res = run_bass_kernel_spmd(nc, maps, core_ids=list(range(8)))
    return assemble(res.results)
```

```python
import contextlib
import numpy as np
import concourse.bass as bass
import concourse.mybir as mybir
from concourse.bass_utils import run_bass_kernel_spmd

F32 = mybir.dt.float32
BF16 = mybir.dt.bfloat16
AF = mybir.ActivationFunctionType
ALU = mybir.AluOpType
ENGS = ("pe", "act", "dve", "pool", "sp")

D_MODEL = 1024
D_FF = 2816
SEQ = 4096
TTP = 512
NTILE = SEQ // TTP
NSS = 4
LS = 16
TTS = NSS * LS
IN_COLS = 10752
EPS = 1e-6
C_AQ, C_AK, C_AV, C_HQ, C_HF, C_HI, C_HG, C_LX, C_LG, C_GA, C_GB, C_GC = (
    0, 1024, 1280, 1536, 2560, 3584, 4608, 5632, 6656, 7680, 8704, 9728)


class Tracker:
    def __init__(self, nc, stack):
        self.nc = nc
        self.stack = stack
        self.streams = {e: [] for e in ENGS}
        self.sems = {}
        self.val = {}
        self.seen = {e: {} for e in ENGS}
        self.bufs = {}
        self.out_deps = []
        self.ranges = {}
        self.overl = {}
        for e in ("pe", "act", "dve", "pool"):
            self._sem(e)

    def _sem(self, key):
        if key not in self.sems:
            nm = "s_" + key.replace(":", "_")
            self.sems[key] = self.stack.enter_context(self.nc.semaphore(nm))
            self.val[key] = 0
        return self.sems[key]

    def set_range(self, name, off, end):
        if self.ranges.get(name) == (off, end):
            return
        if name in self.ranges:
            o0, e0 = self.ranges[name]
            off, end = min(off, o0), max(end, e0)
            if (off, end) == (o0, e0):
                return
            for n2 in self.overl[name]:
                self.overl[n2].remove(name)
        self.ranges[name] = (off, end)
        ov = []
        for n2, (o2, e2) in self.ranges.items():
            if n2 != name and o2 < end and off < e2:
                ov.append(n2)
                self.overl[n2].append(name)
        self.overl[name] = ov

    def _deps(self, reads, writes):
        deps = {}

        def add(d):
            if d is not None and deps.get(d[0], 0) < d[1]:
                deps[d[0]] = d[1]

        def allacc(b):
            st = self.bufs.get(b)
            if st:
                add(st[0])
                for r in st[1]:
                    add(r)

        for b in reads:
            st = self.bufs.get(b)
            if st:
                add(st[0])
                if b.startswith("pb"):
                    for r in st[1]:
                        add(r)
            for o in self.overl.get(b, ()):
                allacc(o)
        for b in writes:
            allacc(b)
            for o in self.overl.get(b, ()):
                allacc(o)
        return deps

    def _emit_waits(self, eng, deps):
        for k, v in deps.items():
            if eng == "pe" and k == "pe":
                continue
            if self.seen[eng].get(k, 0) >= v:
                continue
            self.seen[eng][k] = v
            sem = self.sems[k]
            self.streams[eng].append(I("wait_ge", sem, v))

    def _record(self, dep, reads, writes):
        for b in reads:
            st = self.bufs.setdefault(b, [None, []])
            st[1].append(dep)
            if len(st[1]) > 12:
                m = {}
                for k, v in st[1]:
                    m[k] = max(m.get(k, 0), v)
                st[1] = list(m.items())
        for b in writes:
            self.bufs[b] = [dep, []]

    mute = False

    def op(self, eng, fn, reads=(), writes=()):
        if self.mute:
            return
        deps = self._deps(reads, writes)
        self._emit_waits(eng, deps)
        self.val[eng] += 1
        n = self.val[eng]
        sem = self.sems[eng]
        self.streams[eng].append(lambda e, fn=fn, sem=sem: fn(e).then_inc(sem, 1))
        self._record((eng, n), reads, writes)

    def dma(self, q, semkey, fns, reads=(), writes=(), is_output=False):
        if self.mute:
            return
        key = "dma:" + semkey
        sem = self._sem(key)
        deps = self._deps(reads, writes)
        self._emit_waits(q, deps)
        for fn in fns:
            self.val[key] += 16
            self.streams[q].append(lambda e, fn=fn, sem=sem: fn(e).then_inc(sem, 16))
        dep = (key, self.val[key])
        self._record(dep, reads, writes)
        if is_output:
            self.out_deps.append(dep)

    def finish(self, eng="sp"):
        deps = {}
        for k, v in self.out_deps:
            deps[k] = max(deps.get(k, 0), v)
        for k, v in deps.items():
            sem = self.sems[k]
            self.streams[eng].append(I("wait_ge", sem, v))

    def emit(self):
        S = self.streams
        with self.nc.Block() as block:
            @block.tensor
            def _(e):
                for f in S["pe"]:
                    f(e)

            @block.scalar
            def _(e):
                for f in S["act"]:
                    f(e)

            @block.vector
            def _(e):
                for f in S["dve"]:
                    f(e)

            @block.gpsimd
            def _(e):
                for f in S["pool"]:
                    f(e)

            @block.sync
            def _(e):
                for f in S["sp"]:
                    f(e)


def I(name, *a, **k):
    return lambda e: getattr(e, name)(*a, **k)


def seq(fns):
    def f(e):
        r = None
        for g in fns:
            r = g(e)
        return r
    return f


def mm(out, lhsT, rhs, start=True, stop=True):
    return I("matmul", out, lhsT=lhsT, rhs=rhs, start=start, stop=stop)


CST = {}
_c = 0
for _n, _w in (("ident", 128), ("bd64", 128), ("ones", 128), ("onespad", 256), ("tri2", 128), ("tri4", 64),
               ("ssame", 256), ("rowm", 4), ("scanp", 512), ("scans", 64)):
    CST[_n] = (_c, _c + _w)
    _c += _w
NCST = _c


def make_consts():
    c = np.zeros((128, NCST), np.float32)
    p = np.arange(128)[:, None]

    def put(name, fn):
        a, b = CST[name]
        cc = np.arange(b - a)[None, :]
        c[:, a:b] = fn(p, cc).astype(np.float32)

    put("ident", lambda p, c_: p == c_)
    put("bd64", lambda p, c_: (p // 64) == (c_ // 64))
    put("ones", lambda p, c_: (p >= 0) & (c_ >= 0))
    put("onespad", lambda p, c_: np.where(c_ < 128, c_ < 64, (c_ - 128) >= 64) & (p >= 0))
    put("tri2", lambda p, c_: ((p // 64) == (c_ // 64)) & (p <= c_))
    put("tri4", lambda p, c_: ((p // 16) == (c_ // 16)) & (p <= c_))
    put("ssame", lambda p, c_: (p // 16) == ((c_ % 64) // 16))
    put("rowm", lambda p, c_: (p // 16) == c_)
    put("scanp", lambda p, c_: ((c_ % 64) != 0) & (p >= 0))
    put("scans", lambda p, c_: ((c_ % 16) != 0) & (p >= 0))
    return c


P_G1, P_GM, P_G2, P_LB, P_CW, P_CB, P_BA, P_BX, P_LAM, P_GQ, P_GK, P_SINK, P_GO = 0, 1, 2, 3, 4, 8, 9, 10, 11, 12, 13, 14, 15
NPR = 32


class _Stop(Exception):
    pass


def build(ntile_p=NTILE, do_sample=True, nlayer=2, stop=None):
    nc = bass.Bass("TRN2", target_bir_lowering=False)
    D = {}

    def din(name, shape):
        D[name] = nc.dram_tensor(name, list(shape), F32, kind="ExternalInput").ap()

    def dout(name, shape):
        D[name] = nc.dram_tensor(name, list(shape), F32, kind="ExternalOutput").ap()

    din("xp", (SEQ, 1024)); din("xs", (TTS, 1024))
    din("ck", (2, NSS, 128, 256)); din("cv", (2, NSS, 128, 256))
    din("shg", (2, NSS, 8, 128, 128)); din("scv", (2, NSS, 3, 1024)); din("slr", (2, NSS, 1024))
    din("w1u", (2, 1024, 2 * D_FF)); din("w1d", (2, D_FF, 1024)); din("win", (2, 1024, IN_COLS))
    din("wao", (2, 1024, 1024)); din("who", (2, 1024, 1024)); din("wlo", (2, 1024, 1024)); din("wout", (2, 1024, 1024))
    din("w2u", (2, 1024, 2 * D_FF)); din("w2d", (2, D_FF, 1024))
    din("lwa", (2, 16, 64, 64)); din("lwx", (2, 16, 64, 64))
    din("P", (NPR, 1024)); din("cst", (128, NCST))
    WB = {}
    WNAMES = ("w1u", "w1d", "win", "wao", "who", "wlo", "wout", "w2u", "w2d")
    for wn_ in WNAMES:
        WB[wn_] = nc.dram_tensor("wb_" + wn_, list(D[wn_].shape), BF16, kind="Internal").ap()
    dout("yp", (SEQ, 1024)); dout("ys", (TTS, 1024))
    dout("nkp", (2, 128, 256)); dout("nvp", (2, 128, 256)); dout("nhp", (2, 8, 128, 128))
    dout("ncp", (2, 3, 1024)); dout("nlp", (2, 1024))
    dout("nks", (2, TTS, 256)); dout("nvs", (2, TTS, 256)); dout("nhs", (2, NSS, 8, 128, 128))
    dout("ncs", (2, NSS, 3, 1024)); dout("nls", (2, NSS, 1024))
    import os
    DBG = os.environ.get("DBG")
    if DBG:
        dout("dbg", (128, 8, TTS))

    WQ_SP = bool(int(os.environ.get("WQ_SP", "1")))
    SCOARSE = bool(int(os.environ.get("SCOARSE", "0")))
    FINEPH = os.environ.get("FINEPH", "ffn,att,hgrn,lru").split(",")
    FINE = False
    CO = {"v": True}
    with contextlib.ExitStack() as st:
        T = Tracker(nc, st)

        def sb(name, shape, dt):
            return st.enter_context(nc.sbuf_tensor(name, list(shape), dt))

        cst32 = sb("cst32", (128, NCST), F32)
        cstb = sb("cstb", (128, NCST), BF16)
        Psb = sb("Psb", (NPR, 1024), F32)
        pcol = sb("pcol", (128, 8, NPR), F32)
        der = sb("der", (128, 2, 8, 8), F32)
        bda = sb("bda", (128, 2, 2, 8, 128), BF16)
        NSLOT = 5
        slots = [sb(f"ws{i}", (128, 4096), BF16) for i in range(NSLOT)]
        xin = [sb("xin0", (128, 1024), F32)] * 2
        xT = sb("xT", (128, 8, TTP), F32)
        hT = sb("hT", (128, 8, TTP), BF16)
        sq = [sb(f"sq{i}", (128, TTP), BF16) for i in range(2)]
        rstd = [sb(f"rstd{i}", (128, TTP), F32) for i in range(3)]
        merged = sb("merged", (128, 8, TTP), F32)
        gT = sb("gT", (128, 8, TTP), BF16)
        yT = sb("yT", (128, 8, TTP), BF16)
        khalo = sb("khalo", (128, 2, 2, 4, 128), BF16)
        vhalo = sb("vhalo", (128, 2, 4, 2, 128), BF16)
        Sst = sb("Sst", (128, 2, 8, 128), F32)
        lxhalo = sb("lxhalo", (128, 2, 8, 3), F32)
        hprev = sb("hprev", (128, 2, 8), F32)
        kstage = sb("kstage", (128, 4, 64), F32)
        vstage = sb("vstage", (128, 256), F32)
        ckd = [sb(f"ckd{i}", (128, 4, 2, 64), F32) for i in range(2)]
        Vc = [sb(f"Vc{i}", (128, 4, 2, 128), BF16) for i in range(2)]
        Ssm = sb("Ssm", (128, 8, 128), F32)
        h0s = sb("h0s", (128, NSS, 8), F32)
        cstage = sb("cstage", (128, 8, NSS, 3), F32)
        hstage = sb("hstage", (128, 8, NSS), F32)
        xhs = sb("xhs", (128, 8, NSS, 3), F32)
        if DBG:
            dbgst = sb("dbgst", (128, 8, TTS), F32)

        def dbg(name, l, kind, src, rnames):
            if DBG == name and l == 0 and kind == "s":
                T.op("dve", I("tensor_copy", out=dbgst[:], in_=src), reads=rnames, writes=["dbgst"])
                T.dma("sp", "dbg", [I("dma_start", out=D["dbg"], in_=dbgst[:])], reads=["dbgst"], is_output=True)

        bda32 = xin[0][:, :].rearrange("p (j d) -> p j d", d=128)
        ARENA = 41 * 1024
        arena = sb("arena", (128, ARENA), mybir.dt.uint8)
        pb = [st.enter_context(nc.psum_tensor(f"pb{i}", [128, 512], F32)) for i in range(8)]
        try:
            print("sbuf bytes remaining", nc.sbuf_bytes_remaining)
        except Exception as ex:
            print("sbuf remaining n/a", ex)

        def carve(name, off, shape, dt, sub=False, tname=None):
            esz = 4 if dt == F32 else 2
            n = int(np.prod(shape))
            tn = tname or name
            if CO["v"]:
                pass
            elif sub:
                rowb = (n // shape[0]) * esz
                for i_ in range(shape[0]):
                    T.set_range(f"{tn}{i_}", off + i_ * rowb, off + (i_ + 1) * rowb)
            elif tn != "-":
                T.set_range(tn, off, off + n * esz)
            v = arena[:, off:off + n * esz].bitcast(dt)
            if len(shape) > 1:
                names = "abcd"[:len(shape)]
                pat = "p (" + " ".join(names) + ") -> p " + " ".join(names)
                v = v.rearrange(pat, **{names[i]: shape[i] for i in range(1, len(shape))})
            return v, off + n * esz

        def reg(names, off, end):
            if CO["v"]:
                for nm in names:
                    T.set_range(nm, off, end)

        def cs(name, rows=slice(None)):
            a, b = CST[name]
            return cstb[rows, a:b]

        def cs32(name, rows=slice(None)):
            a, b = CST[name]
            return cst32[rows, a:b]

        rot = {"i": 0}

        def nextbank():
            rot["i"] = (rot["i"] + 1) % 4
            return rot["i"]

        stt = {"i": 0}

        def nextstat():
            stt["i"] ^= 1
            return (4 + stt["i"], rstd[0] if stt["i"] else rstd[2], "rstd0" if stt["i"] else "rstd2")

        def rsqrt_from(bank, r_, rn, TT, scale):
            T.op("act", I("activation", out=r_[:, 0:TT], in_=pb[bank][:, 0:TT], func=AF.Ln, scale=scale, bias=EPS), reads=[f"pb{bank}"], writes=[rn])
            T.op("act", I("activation", out=r_[:, 0:TT], in_=r_[:, 0:TT], func=AF.Exp, scale=-0.5), reads=[rn], writes=[rn])

        def seg(k):
            return stop is None or stop >= 0 or k <= -stop

        T.dma("sp", "cst", [I("dma_start", out=cst32[:], in_=D["cst"])], writes=["cst32"])
        T.dma("sp", "P", [I("dma_start", out=Psb[:], in_=D["P"])], writes=["Psb"])
        T.op("dve", I("tensor_copy", out=cstb[:], in_=cst32[:]), reads=["cst32"], writes=["cstb"])
        for j in (range(8) if seg(2) else ()):
            T.op("pe", I("transpose", out=pb[5][:, j * 32:(j + 1) * 32], in_=Psb[:, j * 128:(j + 1) * 128],
                                                 identity=cs32("ident", slice(0, NPR))[:, 0:NPR]),
                 reads=["Psb", "cst32"], writes=["pb5"])
        if seg(2):
            T.op("dve", I("tensor_copy", out=pcol[:], in_=pb[5][:, 0:8 * NPR].rearrange("p (j v) -> p j v", v=NPR)),
                 reads=["pb5"], writes=["pcol"])

        def pc(l, row, j):
            return pcol[:, j, 16 * l + row:16 * l + row + 1]

        def pcv(l, row):
            return pcol[:, :, 16 * l + row]

        for l in (range(2) if seg(3) else ()):
            T.op("act", I("activation", out=der[:, l, 0, :], in_=pcv(l, P_SINK), func=AF.Exp),
                 reads=["pcol"], writes=[f"der{l}0"])
            T.op("act", I("activation", out=der[:, l, 6, :], in_=pcv(l, P_LAM), func=AF.Exp, scale=-1.0),
                 reads=["pcol"], writes=[f"der{l}6"])
            T.op("act", I("activation", out=der[:, l, 7, :], in_=der[:, l, 6, :], func=AF.Ln, bias=1.0),
                 reads=[f"der{l}6"], writes=[f"der{l}7"])
            T.op("dve", I("tensor_scalar", out=der[:, l, 4, :], in0=der[:, l, 7, :], scalar1=-8.0, scalar2=None, op0=ALU.mult),
                 reads=[f"der{l}7"], writes=[f"der{l}4"])
            T.op("dve", I("tensor_scalar", out=der[:, l, 5, :], in0=der[:, l, 7, :], scalar1=-16.0, scalar2=None, op0=ALU.mult),
                 reads=[f"der{l}7"], writes=[f"der{l}5"])
        T.mute = not seg(4)
        L0, L1 = pcv(0, P_LB), pcv(1, P_LB)
        tA, tB, tC, tD = der[:, 0, 6, :], der[:, 0, 7, :], der[:, 1, 6, :], der[:, 1, 7, :]
        T.op("dve", I("tensor_tensor", out=tA, in0=L0, in1=L1, op=ALU.max), reads=["pcol", "der06", "der16"], writes=["lbA"])
        T.op("dve", I("tensor_tensor", out=tB, in0=L0, in1=tA, op=ALU.subtract), reads=["pcol", "lbA", "der07"], writes=["lbB"])
        T.op("dve", I("tensor_tensor", out=tC, in0=L1, in1=tA, op=ALU.subtract), reads=["pcol", "lbA", "der17"], writes=["lbC"])
        T.op("act", I("activation", out=tB, in_=tB, func=AF.Exp), reads=["lbB"], writes=["lbB"])
        T.op("act", I("activation", out=tC, in_=tC, func=AF.Exp), reads=["lbC"], writes=["lbC"])
        T.op("dve", I("tensor_tensor", out=tA, in0=tB, in1=tC, op=ALU.add), reads=["lbB", "lbC"], writes=["lbA"])
        T.op("dve", I("reciprocal", out=tA, in_=tA), reads=["lbA"], writes=["lbA"])
        T.op("dve", I("tensor_tensor", out=tB, in0=tB, in1=tA, op=ALU.mult), reads=["lbB", "lbA"], writes=["lbB"])
        T.op("dve", I("tensor_tensor", out=tC, in0=tC, in1=tA, op=ALU.mult), reads=["lbC", "lbA"], writes=["lbC"])
        T.op("dve", I("tensor_tensor", out=tD, in0=tB, in1=tC, op=ALU.add), reads=["lbB", "lbC"], writes=["lbD"])
        T.op("dve", I("tensor_tensor", out=der[:, 0, 1, :], in0=tB, in1=tB, op=ALU.subtract), reads=["lbB"], writes=["der01"])
        T.op("dve", I("tensor_tensor", out=der[:, 1, 1, :], in0=tD, in1=tB, op=ALU.subtract), reads=["lbD", "lbB"], writes=["der11"])
        for l in range(2):
            T.op("dve", I("tensor_scalar", out=der[:, l, 2, :], in0=der[:, l, 1, :], scalar1=-1.0, scalar2=1.0, op0=ALU.mult, op1=ALU.add),
                 reads=[f"der{l}1"], writes=[f"der{l}2"])
            T.op("dve", I("tensor_scalar", out=der[:, l, 3, :], in0=der[:, l, 1, :], scalar1=-1.0, scalar2=None, op0=ALU.add),
                 reads=[f"der{l}1"], writes=[f"der{l}3"])
        DER_ALL = [f"der{l}{k}" for l in range(2) for k in range(6)]

        def dcol(l, kind, j):
            return der[:, l, kind, j:j + 1]

        T.mute = not seg(5)
        for l in range(2):
            for gi, wn in enumerate(("lwa", "lwx")):
                T.op("dve", I("memset", bda32, 0.0), writes=["bda32"])
                src = D[wn][l].rearrange("(j two) c d -> two c j d", two=2)
                T.dma("sp", "bda32", [I("dma_start", out=bda32[0:64, :, 0:64], in_=src[0]),
                                      I("dma_start", out=bda32[64:128, :, 64:128], in_=src[1])],
                      writes=["bda32"])
                T.op("dve", I("tensor_copy", out=bda[:, l, gi, :, :], in_=bda32), reads=["bda32"], writes=["bda", "xin0"])
        T.mute = not seg(6)
        T.op("dve", I("memset", Sst[:], 0.0), writes=[f"Sst{l_}h{h_}" for l_ in range(2) for h_ in range(8)])
        T.op("dve", I("memset", lxhalo[:], 0.0), writes=["lxhalo0", "lxhalo1"])
        T.op("dve", I("memset", hprev[:], 0.0), writes=["hprev0", "hprev1"])
        T.op("dve", I("memset", khalo[:], 0.0), writes=["khalo0", "khalo1"])
        T.op("dve", I("memset", vhalo[:], 0.0), writes=["vhalo0", "vhalo1"])

        T.mute = False
        def layer_blocks(l):
            bl = []
            for f in ("1", "2"):
                pass
            def ffn(tag):
                r = [("up" + tag, l, b) for b in range(11)] + [("dn" + tag, l, j) for j in range(8)]
                return r
            bl += ffn("1")
            bl += [("in", l, C_AQ), ("in", l, C_AQ + 512), ("kdup", l, 0), ("in256", l, C_AV), ("in", l, C_GA), ("in", l, C_GA + 512),
                   ("sq", l, "wao", 0), ("sq", l, "wao", 512)]
            for half in range(2):
                bl += [("in", l, C_HQ + 512 * half), ("in", l, C_HF + 512 * half), ("in", l, C_HI + 512 * half), ("in", l, C_HG + 512 * half)]
            bl += [("in", l, C_GB), ("in", l, C_GB + 512), ("sq", l, "who", 0), ("sq", l, "who", 512)]
            for half in range(2):
                bl += [("in", l, C_LX + 512 * half), ("in", l, C_LG + 512 * half)]
            bl += [("in", l, C_GC), ("in", l, C_GC + 512), ("sq", l, "wlo", 0), ("sq", l, "wlo", 512),
                   ("sq", l, "wout", 0), ("sq", l, "wout", 512)]
            bl += ffn("2")
            return bl

        tiles = [("p", t) for t in range(ntile_p)] + ([("s", 0)] if do_sample else [])
        wsched = []
        for _ in tiles:
            for l in range(nlayer):
                wsched += layer_blocks(l)
        wstate = {"i": 0, "issued": 0}
        blocks_per_tile = len(wsched) // max(len(tiles), 1)
        D_fp32 = D

        cast_list = []
        cast_names = {}
        for l_ in range(nlayer):
            for wn_ in WNAMES:
                rows = D[wn_].shape[1]
                cast_names[(wn_, l_)] = []
                for r0 in range(0, rows, 128):
                    nm_ = f"wb_{wn_}{l_}_{r0}"
                    cast_names[(wn_, l_)].append(nm_)
                    cast_list.append((f"wb_{wn_}{l_}", nm_, I("dma_start", out=WB[wn_][l_, r0:r0 + 128, :], in_=D[wn_][l_, r0:r0 + 128, :])))
        cast_state = {"i": 0}

        def issue_casts(n):
            while n > 0 and cast_state["i"] < len(cast_list):
                semk, nm_, fn = cast_list[cast_state["i"]]
                T.dma("pool", semk, [fn], writes=[nm_])
                cast_state["i"] += 1
                n -= 1

        def w_src(desc):
            kind, l = desc[0], desc[1]
            if kind.startswith("up"):
                return ("w1u" if kind == "up1" else "w2u")
            if kind.startswith("dn"):
                return ("w1d" if kind == "dn1" else "w2d")
            if kind in ("in", "in256", "kdup"):
                return "win"
            return desc[2]

        def w_dma(desc, slot, D):
            s = slots[slot]
            kind = desc[0]
            l = desc[1]
            if kind.startswith("up"):
                W = D["w1u" if kind == "up1" else "w2u"][l].rearrange("(kc p) c -> p kc c", p=128)
                b = desc[2]
                v = s[:, 0:4096].rearrange("p (k two c) -> p k two c", two=2, c=256)
                return [I("dma_start", out=v[:, :, 0, :], in_=W[:, :, b * 256:(b + 1) * 256]),
                        I("dma_start", out=v[:, :, 1, :], in_=W[:, :, D_FF + b * 256:D_FF + (b + 1) * 256])]
            if kind.startswith("dn"):
                W = D["w1d" if kind == "dn1" else "w2d"][l].rearrange("(kc p) c -> p kc c", p=128)
                j = desc[2]
                v = s[:, 0:22 * 128].rearrange("p (k c) -> p k c", c=128)
                return [I("dma_start", out=v, in_=W[:, :, j * 128:(j + 1) * 128])]
            if kind == "in":
                W = D["win"][l].rearrange("(kc p) c -> p kc c", p=128)
                c0 = desc[2]
                v = s[:, 0:4096].rearrange("p (k c) -> p k c", c=512)
                return [I("dma_start", out=v, in_=W[:, :, c0:c0 + 512])]
            if kind == "in256":
                W = D["win"][l].rearrange("(kc p) c -> p kc c", p=128)
                c0 = desc[2]
                v = s[:, 0:2048].rearrange("p (k c) -> p k c", c=256)
                return [I("dma_start", out=v, in_=W[:, :, c0:c0 + 256])]
            if kind == "kdup":
                W = D["win"][l].rearrange("(kc p) c -> p kc c", p=128)
                v = s[:, 0:8 * 384].rearrange("p (k c) -> p k c", c=384)
                return [I("dma_start", out=v, in_=W[:, :, C_AK - 64:C_AK + 320])]
            if kind == "sq":
                W = D[desc[2]][l].rearrange("(kc p) c -> p kc c", p=128)
                c0 = desc[3]
                v = s[:, 0:4096].rearrange("p (k c) -> p k c", c=512)
                return [I("dma_start", out=v, in_=W[:, :, c0:c0 + 512])]
            raise ValueError(kind)

        def wnext(*tag):
            i = wstate["i"]
            assert wsched[i] == tuple(tag), (i, wsched[i], tag)
            while wstate["issued"] < min(len(wsched), i + NSLOT - 1):
                k = wstate["issued"]
                if k < blocks_per_tile and len(tiles) > 1:
                    T.dma("pool", f"ws{k % NSLOT}", w_dma(wsched[k], k % NSLOT, D_fp32), writes=[f"ws{k % NSLOT}"])
                    rem_blocks = blocks_per_tile - k
                    rem_casts = len(cast_list) - cast_state["i"]
                    issue_casts(-(-rem_casts // rem_blocks))
                else:
                    issue_casts(len(cast_list))
                    T.dma("sp" if WQ_SP else "pool", f"ws{k % NSLOT}", w_dma(wsched[k], k % NSLOT, WB),
                          reads=cast_names[(w_src(wsched[k]), wsched[k][1])], writes=[f"ws{k % NSLOT}"])
                wstate["issued"] += 1
            wstate["i"] += 1
            return i % NSLOT

        cnt = {"n": 0}

        def alt():
            cnt["n"] += 1
            return cnt["n"] % 2

        def rmsnorm(l, prow, TT, xnames):
            bank, r_, rn = nextstat()
            fns = []
            for j in range(8):
                s_ = sq[j % 2]
                T.op("act", I("activation", out=s_[:, 0:TT], in_=xT[:, j, 0:TT], func=AF.Square),
                     reads=[f"xT{j}"], writes=[f"sq{j % 2}"])
                T.op("pe", mm(pb[bank][:, 0:TT], cs("ones"), s_[:, 0:TT], start=(j == 0), stop=(j == 7)),
                     reads=[f"sq{j % 2}", "cstb"], writes=[f"pb{bank}"])
            rsqrt_from(bank, r_, rn, TT, 1.0 / 1024)
            for j in range(8):
                T.op("dve", I("scalar_tensor_tensor", out=hT[:, j, 0:TT], in0=xT[:, j, 0:TT], scalar=pc(l, prow, j),
                                                                in1=r_[:, 0:TT], op0=ALU.mult, op1=ALU.mult),
                     reads=[f"xT{j}", rn, "pcol"], writes=[f"hT{j}"])

        HT = [f"hT{j}" for j in range(8)]

        def proj(slot, lhs_fn, TT, rhs_t, rhs_names, nk=8, bank=None):
            b = nextbank() if bank is None else bank
            fns = [mm(pb[b][:, 0:TT], lhs_fn(kc), rhs_t[:, kc, 0:TT], start=(kc == 0), stop=(kc == nk - 1)) for kc in range(nk)]
            T.op("pe", seq(fns), reads=[f"ws{slot}"] + rhs_names, writes=[f"pb{b}"])
            return b

        def wv512(slot):
            return slots[slot][:, 0:4096].rearrange("p (k c) -> p k c", c=512)

        def ffn(l, tag, prow, TT, aT, sgb):
            rmsnorm(l, prow, TT, None)
            for b in range(11):
                slot = wnext("up" + tag, l, b)
                v = slots[slot][:, 0:4096].rearrange("p (k two c) -> p k two c", two=2, c=256)
                for jj in range(2):
                    f = 2 * b + jj
                    bg = proj(slot, lambda kc: v[:, kc, 0, jj * 128:(jj + 1) * 128], TT, hT, HT)
                    bv = proj(slot, lambda kc: v[:, kc, 1, jj * 128:(jj + 1) * 128], TT, hT, HT)
                    s_ = sgb[f % 2]
                    T.op("act", I("activation", out=s_[:, 0:TT], in_=pb[bg][:, 0:TT], func=AF.Silu),
                         reads=[f"pb{bg}"], writes=[f"sg{f % 2}"])
                    T.op("dve", I("tensor_tensor", out=aT[:, f, 0:TT], in0=pb[bv][:, 0:TT], in1=s_[:, 0:TT], op=ALU.mult),
                         reads=[f"pb{bv}", f"sg{f % 2}"], writes=[f"aT{f}"])
            AT = [f"aT{f}" for f in range(22)]
            for j in range(8):
                slot = wnext("dn" + tag, l, j)
                v = slots[slot][:, 0:22 * 128].rearrange("p (k c) -> p k c", c=128)
                b = proj(slot, lambda kc: v[:, kc, :], TT, aT, AT, nk=22)
                T.op("dve", I("scalar_tensor_tensor", out=xT[:, j, 0:TT], in0=pb[b][:, 0:TT], scalar=0.5, in1=xT[:, j, 0:TT],
                                                                     op0=ALU.mult, op1=ALU.add),
                     reads=[f"pb{b}", f"xT{j}"], writes=[f"xT{j}"])

        def gate_and_out(l, gcol, wname, TT, first):
            for half in range(2):
                slot = wnext("in", l, gcol + 512 * half)
                v = wv512(slot)
                for jj in range(4):
                    j = 4 * half + jj
                    b = proj(slot, lambda kc: v[:, kc, jj * 128:(jj + 1) * 128], TT, hT, HT)
                    T.op("act", I("activation", out=gT[:, j, 0:TT], in_=pb[b][:, 0:TT], func=AF.Sigmoid),
                         reads=[f"pb{b}"], writes=[f"gT{j}"])
            YT = [f"yT{j}" for j in range(8)]
            for half in range(2):
                slot = wnext("sq", l, wname, 512 * half)
                v = wv512(slot)
                for jj in range(4):
                    j = 4 * half + jj
                    b = proj(slot, lambda kc: v[:, kc, jj * 128:(jj + 1) * 128], TT, yT, YT)
                    if first:
                        T.op("dve", I("tensor_tensor", out=merged[:, j, 0:TT], in0=pb[b][:, 0:TT], in1=gT[:, j, 0:TT], op=ALU.mult),
                             reads=[f"pb{b}", f"gT{j}"], writes=[f"mg{j}"])
                    else:
                        r_ = rstd[1]
                        T.op("dve", I("tensor_tensor", out=r_[:, 0:TT], in0=pb[b][:, 0:TT], in1=gT[:, j, 0:TT], op=ALU.mult),
                             reads=[f"pb{b}", f"gT{j}"], writes=["rstd1"])
                        T.op("dve", I("tensor_tensor", out=merged[:, j, 0:TT], in0=merged[:, j, 0:TT], in1=r_[:, 0:TT], op=ALU.add),
                             reads=["rstd1", f"mg{j}"], writes=[f"mg{j}"])

        def attention(l, kind, tix, TT):
            import os
            CO["v"] = (kind == "s" and SCOARSE) or ("att" not in FINEPH)
            off = 0
            qT, off = carve("qT", off, (8, TTP), BF16, sub=True)
            kz0 = off
            kZ, off = carve("kZ", off, (2, 4, 128 + TTP), BF16, tname="-")
            for g_ in range(4):
                if not CO["v"]:
                    T.set_range(f"kT{g_}", kz0, off)
                    T.set_range(f"kTh{g_}", kz0, off)
            Vpad, off = carve("Vpad", off, (5, 4, 2, 128), BF16, sub=True, tname="Vp")
            Eb = []
            for i in range(2):
                v_, off = carve("E", off, (2, 512), BF16, tname=f"E{i}")
                Eb.append(v_)
            dtmp = []
            for i in range(2):
                v_, off = carve("dtmp", off, (256,), F32, tname=f"dtmp{i}")
                dtmp.append(v_)
            kf32, off = carve("kf32", off, (4, 128), F32)
            kcT = []
            for i in range(2):
                v_, off = carve("kcT", off, (2, 4, 128), BF16, tname=f"kcT{i}"); kcT.append(v_)
            assert off <= ARENA, off
            names = ([f"qT{j}" for j in range(8)] + [f"kT{g}" for g in range(4)] + [f"kTh{g}" for g in range(4)] + [f"Vp{b}" for b in range(5)]
                     + ["E0", "E1", "dtmp0", "dtmp1", "kf32", "kcT0", "kcT1"])
            reg(names, 0, off)
            nblk = max(TT // 128, 1)
            is_last_p = (kind == "p" and tix == NTILE - 1)
            want_kout = ((kind == "s") or is_last_p) and not os.environ.get("NOKOUT")

            if kind == "p":
                T.op("dve", I("tensor_copy", out=kZ[:, :, :, 0:128], in_=khalo[:, l, :, :, :]), reads=[f"khalo{l}"], writes=[f"kTh{g}" for g in range(4)])
                T.op("dve", I("tensor_copy", out=Vpad[:, 0, :, :, :], in_=vhalo[:, l, :, :, :]), reads=[f"vhalo{l}"], writes=["Vp0"])
            else:
                T.op("dve", I("memset", Vpad[:, 0, :, :, :], 0.0), writes=["Vp0"])
                T.op("dve", I("memset", Vpad[:, 1, :, :, :], 0.0), writes=["Vp1"])
            if want_kout:
                T.op("dve", I("memset", kf32[:, :, :], 0.0), writes=["kf32"])
            T.op("dve", I("memset", kZ[64:128, 0, :, 128:128 + TT], 0.0), writes=[f"kT{g}" for g in range(4)])
            T.op("dve", I("memset", kZ[0:64, 1, :, 128:128 + TT], 0.0), writes=[f"kT{g}" for g in range(4)])
            if kind == "p" and tix == 0 and l == 0:
                pass
            for b_ in range(1, 5):
                if kind == "p":
                    T.op("dve", I("memset", Vpad[:, b_, :, 0, 64:128], 0.0), writes=[f"Vp{b_}"])
                    T.op("dve", I("memset", Vpad[:, b_, :, 1, 0:64], 0.0), writes=[f"Vp{b_}"])

            def qknorm(b, dst, gidx, dnames):
                s_ = sq[alt()]
                sn = "sq0" if s_ is sq[0] else "sq1"
                T.op("act", I("activation", out=s_[:, 0:TT], in_=pb[b][:, 0:TT], func=AF.Square), reads=[f"pb{b}"], writes=[sn])
                bank, r_, rn = nextstat()
                T.op("pe", mm(pb[bank][:, 0:TT], cs("bd64"), s_[:, 0:TT]), reads=[sn, "cstb"], writes=[f"pb{bank}"])
                rsqrt_from(bank, r_, rn, TT, 1.0 / 64)
                dl = dst if isinstance(dst, list) else [(dst, slice(0, 128))]
                for (d_ap, rows) in dl:
                    T.op("dve", I("scalar_tensor_tensor", out=d_ap, in0=pb[b][rows, 0:TT], scalar=pcol[rows, 0, 16 * l + gidx:16 * l + gidx + 1],
                                                                                   in1=r_[rows, 0:TT], op0=ALU.mult, op1=ALU.mult),
                         reads=[f"pb{b}", rn, "pcol"], writes=dnames)
                return r_, rn

            import os
            SUB = int(os.environ.get("SUB", "99"))

            def ck_(n):
                if n > SUB:
                    T.mute = True

            for half in range(2):
                slot = wnext("in", l, C_AQ + 512 * half)
                v = wv512(slot)
                for jj in range(4):
                    j = 4 * half + jj
                    b = proj(slot, lambda kc: v[:, kc, jj * 128:(jj + 1) * 128], TT, hT, HT)
                    qknorm(b, qT[:, j, 0:TT], P_GQ, [f"qT{j}"])
            ck_(1)
            slot = wnext("kdup", l, 0)
            v = slots[slot][:, 0:8 * 384].rearrange("p (k c) -> p k c", c=384)
            for g in range(4):
                blo = proj(slot, lambda kc: v[:, kc, 64 + g * 64:64 + g * 64 + 128], TT, hT, HT)
                qknorm(blo, [(kZ[0:64, 0, g, 128:128 + TT], slice(0, 64))], P_GK, [f"kT{g}"])
                b = proj(slot, lambda kc: v[:, kc, g * 64:g * 64 + 128], TT, hT, HT)
                r_, rn = qknorm(b, [(kZ[64:128, 1, g, 128:128 + TT], slice(64, 128))], P_GK, [f"kT{g}"])
                if want_kout:
                    n0 = TT - 128 if kind == "p" else 0
                    nn = 128 if kind == "p" else TT
                    T.op("dve", I("scalar_tensor_tensor",
                        out=kf32[64:128, g, 0:nn], in0=pb[b][64:128, n0:n0 + nn], scalar=pcol[64:128, 0, 16 * l + P_GK:16 * l + P_GK + 1],
                        in1=r_[64:128, n0:n0 + nn], op0=ALU.mult, op1=ALU.mult),
                        reads=[f"pb{b}", rn, "pcol"], writes=["kf32"])
                    bt = nextbank()
                    T.op("pe", I("transpose", out=pb[bt][0:nn, 0:128], in_=kf32[:, g, 0:nn], identity=cs32("ident")),
                         reads=["kf32", "cst32"], writes=[f"pb{bt}"])
                    T.op("act", I("activation", out=kstage[0:nn, g, :], in_=pb[bt][0:nn, 64:128], func=AF.Copy),
                         reads=[f"pb{bt}"], writes=["kstage"])
            if want_kout:
                nn = 128 if kind == "p" else TT
                dst = D["nkp"][l] if kind == "p" else D["nks"][l]
                T.dma("sp", "okst", [I("dma_start", out=dst.rearrange("t (g d) -> t g d", g=4), in_=kstage[0:nn, :, :])],
                      reads=["kstage"], is_output=True)
            ck_(2)
            slot = wnext("in256", l, C_AV)
            vv = slots[slot][:, 0:2048].rearrange("p (k c) -> p k c", c=256)
            for bk in range(nblk):
                nt = min(128, TT)
                b = nextbank()
                fns = [mm(pb[b][0:nt, 0:256], hT[:, kc, bk * 128:bk * 128 + nt], vv[:, kc, :], start=(kc == 0), stop=(kc == 7)) for kc in range(8)]
                T.op("pe", seq(fns), reads=[f"ws{slot}"] + HT, writes=[f"pb{b}"])
                src = pb[b][0:nt, 0:256].rearrange("p (g d) -> p g d", g=4)
                T.op("act", I("activation", out=Vpad[0:nt, bk + 1, :, 0, 0:64], in_=src, func=AF.Copy),
                     reads=[f"pb{b}"], writes=[f"Vp{bk + 1}"])
                T.op("dve", I("tensor_copy", out=Vpad[0:nt, bk + 1, :, 1, 64:128], in_=src),
                     reads=[f"pb{b}"], writes=[f"Vp{bk + 1}"])
                if kind == "s" or (is_last_p and bk == nblk - 1):
                    T.op("act", I("activation", out=vstage[0:nt, :], in_=pb[b][0:nt, 0:256], func=AF.Copy),
                         reads=[f"pb{b}"], writes=["vstage"])
                    dst = D["nvp"][l] if kind == "p" else D["nvs"][l]
                    T.dma("sp", "ovst", [I("dma_start", out=dst, in_=vstage[0:nt, :])], reads=["vstage"], is_output=True)

            ck_(3)
            esk = lambda j: dcol(l, 0, j)
            ones_lo = cs("onespad")[:, 0:128]
            ones_hi = cs("onespad")[:, 128:256]

            def finish_pair(g, bo, c0, n, ei):
                d_ = dtmp[ei]
                for jj in range(2):
                    T.op("act", I("activation", out=d_[:, jj * 128:jj * 128 + n], in_=pb[bo][:, 256 + jj * 128:256 + jj * 128 + n],
                                  func=AF.Ln, bias=esk(2 * g + jj)),
                         reads=[f"pb{bo}"] + DER_ALL, writes=[f"dtmp{ei}"])
                    T.op("act", I("activation", out=d_[:, jj * 128:jj * 128 + n], in_=d_[:, jj * 128:jj * 128 + n], func=AF.Exp, scale=-1.0),
                         reads=[f"dtmp{ei}"], writes=[f"dtmp{ei}"])
                for jj in range(2):
                    T.op("dve", I("tensor_tensor", out=yT[:, 2 * g + jj, c0:c0 + n], in0=pb[bo][:, jj * 128:jj * 128 + n],
                                                               in1=d_[:, jj * 128:jj * 128 + n], op=ALU.mult),
                         reads=[f"pb{bo}", f"dtmp{ei}"], writes=[f"yT{2 * g + jj}"])

            if kind == "p":
                its = [(g, pt) for g in range(4) for pt in range(TT // 128)]

                def emit_scores(i):
                    g, pt = its[i]
                    ei = i % 2
                    E = Eb[ei]
                    en = f"E{ei}"
                    hasA = not (tix == 0 and pt == 0)
                    acol = pt * 128
                    bcol = 128 + pt * 128
                    for which, kc0, use in ((0, acol, hasA), (1, bcol, True)):
                        if not use:
                            continue
                        bs = nextbank()
                        fns = []
                        for hh in range(4):
                            j = 2 * g + hh // 2
                            fns.append(mm(pb[bs][:, hh * 128:(hh + 1) * 128], kZ[:, hh % 2, g, kc0:kc0 + 128],
                                          qT[:, j, pt * 128:(pt + 1) * 128]))
                        T.op("pe", seq(fns), reads=[f"kT{g}", f"kTh{g}", f"qT{2 * g}", f"qT{2 * g + 1}"], writes=[f"pb{bs}"])
                        T.op("act", I("activation", out=E[:, which, :], in_=pb[bs][:, :], func=AF.Exp, scale=0.125),
                             reads=[f"pb{bs}"], writes=[en])
                        if which == 0:
                            T.op("dve", I("memset", E[0:64, 0, :].rearrange("p (h q) -> p h q", h=4)[:, :, 64:128], 0.0), writes=[en])
                        else:
                            T.op("dve", I("memset", E[64:128, 1, :].rearrange("p (h q) -> p h q", h=4)[:, :, 0:64], 0.0), writes=[en])

                def emit_pv(i):
                    g, pt = its[i]
                    ei = i % 2
                    E = Eb[ei]
                    en = f"E{ei}"
                    hasA = not (tix == 0 and pt == 0)
                    blkA = pt
                    blkB = pt + 1
                    bo = nextbank()
                    fns = []
                    for isden in (0, 1):
                        for jj in range(2):
                            oc = isden * 256 + jj * 128
                            terms = []
                            for hl in range(2):
                                hh = 2 * jj + hl
                                if hasA:
                                    lhs = (ones_lo if hl == 0 else ones_hi) if isden else Vpad[:, blkA, g, hl, :]
                                    terms.append((lhs, E[:, 0, hh * 128:(hh + 1) * 128]))
                                lhs = (ones_lo if hl == 0 else ones_hi) if isden else Vpad[:, blkB, g, hl, :]
                                terms.append((lhs, E[:, 1, hh * 128:(hh + 1) * 128]))
                            for ti, (lhs, rhs) in enumerate(terms):
                                fns.append(mm(pb[bo][:, oc:oc + 128], lhs, rhs, start=(ti == 0), stop=(ti == len(terms) - 1)))
                    T.op("pe", seq(fns), reads=[en, f"Vp{blkA}", f"Vp{blkB}", "cstb"], writes=[f"pb{bo}"])
                    finish_pair(g, bo, pt * 128, 128, ei)

                emit_scores(0)
                for i in range(len(its)):
                    if i + 1 < len(its):
                        emit_scores(i + 1)
                    emit_pv(i)
                T.op("dve", I("tensor_copy", out=khalo[:, l, :, :, :], in_=kZ[:, :, :, TT:TT + 128]), reads=[f"kT{g}" for g in range(4)], writes=[f"khalo{l}"])
                T.op("dve", I("tensor_copy", out=vhalo[:, l, :, :, :], in_=Vpad[:, 4, :, :, :]), reads=["Vp4"], writes=[f"vhalo{l}"])
            else:
                for s in range(NSS):
                    i2 = s % 2
                    T.dma("sp", f"ckd{i2}", [I("dma_start", out=ckd[s % 2][:, :, h, :], in_=D["ck"][l, s].rearrange("r (g d) -> r g d", g=4))
                                            for h in range(2)], writes=[f"ckd{i2}"])
                    SKIP = os.environ.get("SKIP", "")
                    if "vc" in SKIP:
                        T.mute = True
                    T.op("dve", I("memset", Vc[i2][:], 0.0), writes=[f"Vc{i2}"])
                    T.dma("pool", f"Vc{i2}", [I("dma_start", out=Vc[s % 2][:, :, 0, 0:64], in_=D["cv"][l, s].rearrange("r (g d) -> r g d", g=4)),
                                             I("dma_start", out=Vc[s % 2][:, :, 1, 64:128], in_=D["cv"][l, s].rearrange("r (g d) -> r g d", g=4))],
                          writes=[f"Vc{i2}"])
                    T.mute = ("tr" in SKIP)
                    T.op("dve", I("memset", kcT[i2][64:128, 0, :, :], 0.0), writes=[f"kcT{i2}"])
                    T.op("dve", I("memset", kcT[i2][0:64, 1, :, :], 0.0), writes=[f"kcT{i2}"])
                    for g in range(4):
                        bt = nextbank()
                        T.op("pe", I("transpose", out=pb[bt][:, 0:128], in_=ckd[i2][:, g, :, :].rearrange("p h d -> p (h d)"),
                                                                         identity=cs32("ident")),
                             reads=[f"ckd{i2}", "cst32"], writes=[f"pb{bt}"])
                        T.op("act", I("activation", out=kcT[i2][0:64, 0, g, :], in_=pb[bt][0:64, 0:128], func=AF.Copy),
                             reads=[f"pb{bt}"], writes=[f"kcT{i2}"])
                        T.op("act", I("activation", out=kcT[i2][64:128, 1, g, :], in_=pb[bt][64:128, 0:128], func=AF.Copy),
                             reads=[f"pb{bt}"], writes=[f"kcT{i2}"])
                    T.mute = False
                    ck_(4)
                    for g in range(4):
                        ei = alt()
                        E = Eb[ei]
                        en = f"E{ei}"
                        bs = nextbank()
                        fns = []
                        for hh in range(4):
                            j = 2 * g + hh // 2
                            fns.append(mm(pb[bs][:, hh * 16:(hh + 1) * 16], kcT[i2][:, hh % 2, g, :], qT[:, j, s * LS:(s + 1) * LS]))
                        for hh in range(4):
                            j = 2 * g + hh // 2
                            fns.append(mm(pb[bs][0:64, 64 + hh * 16:64 + (hh + 1) * 16], kZ[:, hh % 2, g, 128:128 + TT], qT[:, j, s * LS:(s + 1) * LS]))
                        T.op("pe", seq(fns), reads=[f"kcT{i2}", f"kT{g}", f"qT{2 * g}", f"qT{2 * g + 1}"], writes=[f"pb{bs}"])
                        T.op("act", I("activation", out=E[:, 0, 0:64], in_=pb[bs][:, 0:64], func=AF.Exp, scale=0.125),
                             reads=[f"pb{bs}"], writes=[en])
                        T.op("act", I("activation", out=E[0:64, 1, 0:64], in_=pb[bs][0:64, 64:128], func=AF.Exp, scale=0.125),
                             reads=[f"pb{bs}"], writes=[en])
                        T.op("dve", I("memset", E[64:128, 1, 0:64], 0.0), writes=[en])
                        T.op("dve", I("tensor_scalar", out=E[0:64, 1, 0:64], in0=E[0:64, 1, 0:64],
                                                                      scalar1=cst32[0:64, CST["rowm"][0] + s:CST["rowm"][0] + s + 1], scalar2=None, op0=ALU.mult),
                             reads=[en, "cst32"], writes=[en])
                        ck_(5)
                        bo = nextbank()
                        fns = []
                        for isden in (0, 1):
                            for jj in range(2):
                                oc = isden * 256 + jj * 128
                                terms = []
                                for hl in range(2):
                                    hh = 2 * jj + hl
                                    lhs = (ones_lo if hl == 0 else ones_hi) if isden else Vc[i2][:, g, hl, :]
                                    terms.append((lhs, E[:, 0, hh * 16:(hh + 1) * 16]))
                                    lhs = (ones_lo if hl == 0 else ones_hi) if isden else Vpad[:, 1, g, hl, :]
                                    terms.append((lhs, E[:, 1, hh * 16:(hh + 1) * 16]))
                                for ti, (lhs, rhs) in enumerate(terms):
                                    fns.append(mm(pb[bo][:, oc:oc + LS], lhs, rhs, start=(ti == 0), stop=(ti == len(terms) - 1)))
                        T.op("pe", seq(fns), reads=[en, f"Vc{i2}", "Vp1", "cstb"], writes=[f"pb{bo}"])
                        ck_(6)
                        finish_pair(g, bo, s * LS, LS, ei)
            ck_(7)
            dbg("ya", l, kind, yT[:, :, 0:TTS], [f"yT{j}" for j in range(8)])
            dbg("q", l, kind, qT[:, :, 0:TTS], [f"qT{j}" for j in range(8)])
            gate_and_out(l, C_GA, "wao", TT, True)
            dbg("m1", l, kind, merged[:, :, 0:TTS], [f"mg{j}" for j in range(8)])
            T.mute = False

        def hgrn(l, kind, tix, TT):
            CO["v"] = (kind == "s" and SCOARSE) or ("hgrn" not in FINEPH)
            off = 0
            tmp = []
            toffs = []
            for i in range(16):
                toffs.append(off)
                v_, off = carve("ht", off, (TTP // 2,), F32, tname=f"ht{i}"); tmp.append(v_)
            qeT, off = carve("qeT", off, (4, TTP), BF16, sub=True)
            keT, off = carve("keT", off, (4, TTP), BF16, sub=True)
            kd32 = []; kdtok = []; attm = []
            for i in range(4):
                v_, off = carve("kd32", off, (128,), F32, tname=f"kd32{i}"); kd32.append(v_)
                v_, off = carve("kdtok", off, (2, 128), BF16, tname=f"kdtok{i}"); kdtok.append(v_)
                v_, off = carve("attm", off, (128,), BF16, tname=f"attm{i}"); attm.append(v_)
            eL, off = carve("eL", off, (4, 8), F32)
            Vh, off = carve("Vh", off, (4, 4, 128), BF16)
            VhM, off = carve("VhM", off, (4, 128), BF16)
            sgg, off = carve("sgg", off, (4, TTP), BF16, sub=True)
            oTs = []
            for i in range(4):
                v_, _o = carve("oTs", toffs[2 * i], (TTP,), F32, tname=f"oTs{i}"); oTs.append(v_)
            Sbf, off = carve("Sbf", off, (8, 128), BF16, sub=True)
            assert off <= ARENA, off
            names = ([f"ht{i}" for i in range(16)] + [f"qeT{i}" for i in range(4)] + [f"keT{i}" for i in range(4)]
                     + [f"kd32{i}" for i in range(4)] + [f"kdtok{i}" for i in range(4)] + [f"attm{i}" for i in range(4)] + ["eL", "Vh", "VhM"]
                     + [f"sgg{i}" for i in range(4)] + [f"oTs{i}" for i in range(4)] + [f"Sbf{h}" for h in range(8)])
            reg(names, 0, off)
            L = 64 if kind == "p" else LS
            nch = TT // L
            scanm = cs32("scanp") if kind == "p" else cs32("scans")
            trim = cs("tri2") if kind == "p" else cs("tri4")
            nblk = max(TT // 128, 1)
            nt = min(128, TT)
            is_last_p = (kind == "p" and tix == NTILE - 1)

            def Sf(h):
                return Sst[:, l, h, :]

            for pi_ in range(4):
                T.op("dve", I("memset", kdtok[pi_][:, :, :], 0.0), writes=[f"kdtok{pi_}"])
                T.op("dve", I("memset", attm[pi_][:, :], 0.0), writes=[f"attm{pi_}"])
            T.op("dve", I("memset", VhM[:, :, :], 0.0), writes=["VhM"])
            T.op("dve", I("memset", Vh[:, :, :, :], 0.0), writes=["Vh"])
            if kind == "p":
                T.op("act", I("activation", out=Sbf[:, :, :], in_=Sst[:, l, :, :], func=AF.Copy), reads=[f"Sst{l}h{h_}" for h_ in range(8)], writes=[f"Sbf{h}" for h in range(8)])

            for half in range(2):
                s_q = wnext("in", l, C_HQ + 512 * half); vq = wv512(s_q)
                s_f = wnext("in", l, C_HF + 512 * half); vf = wv512(s_f)
                nth = 2 if kind == "p" else 1
                Wd = TT // nth
                HH4 = range(4)
                for th in range(nth):
                    c0 = th * Wd
                    cs_ = slice(c0, c0 + Wd)
                    tt = {hh: tmp[4 * hh:4 * hh + 4] for hh in HH4}
                    tnn = {hh: [f"ht{4 * hh + i}" for i in range(4)] for hh in HH4}
                    bfs, bqs = {}, {}
                    for hh in HH4:
                        for which, vv_, ss_, dd in ((0, vf, s_f, bfs), (1, vq, s_q, bqs)):
                            b = hh
                            o0 = which * 256
                            fns = [mm(pb[b][:, o0:o0 + Wd], vv_[:, kc, hh * 128:(hh + 1) * 128], hT[:, kc, cs_], start=(kc == 0), stop=(kc == 7)) for kc in range(8)]
                            T.op("pe", seq(fns), reads=[f"ws{ss_}"] + HT, writes=[f"pb{b}"])
                            dd[hh] = (b, o0)
                    for hh in HH4:
                        t1, t2, t3, t4 = tt[hh]; tn = tnn[hh]
                        b, o0 = bfs[hh]
                        T.op("act", I("activation", out=t1[:, 0:Wd], in_=pb[b][:, o0:o0 + Wd], func=AF.Sigmoid), reads=[f"pb{b}"], writes=[tn[0]])
                    for hh in HH4:
                        t1, t2, t3, t4 = tt[hh]; tn = tnn[hh]
                        b, o0 = bqs[hh]
                        T.op("act", I("activation", out=t4[:, 0:Wd], in_=pb[b][:, o0:o0 + Wd], func=AF.Silu), reads=[f"pb{b}"], writes=[tn[3]])
                    for hh in HH4:
                        h = 4 * half + hh
                        t1, t2, t3, t4 = tt[hh]; tn = tnn[hh]
                        T.op("act", I("activation", out=t2[:, 0:Wd], in_=t1[:, 0:Wd], func=AF.Ln, scale=dcol(l, 2, h), bias=dcol(l, 1, h)),
                             reads=[tn[0]] + DER_ALL, writes=[tn[1]])
                    for hh in HH4:
                        h = 4 * half + hh
                        t1, t2, t3, t4 = tt[hh]; tn = tnn[hh]
                        T.op("dve", I("tensor_scalar", out=t1[:, 0:Wd], in0=t1[:, 0:Wd], scalar1=dcol(l, 3, h), scalar2=dcol(l, 2, h), op0=ALU.mult, op1=ALU.add),
                             reads=[tn[0]] + DER_ALL, writes=[tn[0]])
                        T.op("dve", I("tensor_scalar", out=t2[:, 0:Wd], in0=t2[:, 0:Wd], scalar1=-60.0, scalar2=None, op0=ALU.max),
                             reads=[tn[1]], writes=[tn[1]])
                        T.op("dve", I("tensor_tensor_scan", out=t3[:, 0:Wd], data0=scanm[:, 0:Wd], data1=t2[:, 0:Wd], initial=0.0, op0=ALU.mult, op1=ALU.add),
                             reads=[tn[1], "cst32"], writes=[tn[2]])
                        T.op("dve", I("tensor_scalar", out=t2[:, 0:Wd], in0=t3[:, 0:Wd], scalar1=-1.0, scalar2=80.0, op0=ALU.mult, op1=ALU.min),
                             reads=[tn[2]], writes=[tn[1]])
                    for hh in HH4:
                        t1, t2, t3, t4 = tt[hh]; tn = tnn[hh]
                        T.op("act", I("activation", out=t3[:, 0:Wd], in_=t3[:, 0:Wd], func=AF.Exp), reads=[tn[2]], writes=[tn[2]])
                        T.op("act", I("activation", out=t2[:, 0:Wd], in_=t2[:, 0:Wd], func=AF.Exp), reads=[tn[1]], writes=[tn[1]])
                    for hh in HH4:
                        t1, t2, t3, t4 = tt[hh]; tn = tnn[hh]
                        T.op("dve", I("tensor_tensor", out=qeT[:, hh, cs_], in0=t4[:, 0:Wd], in1=t3[:, 0:Wd], op=ALU.mult),
                             reads=[tn[3], tn[2]], writes=[f"qeT{hh}"])
                        T.op("dve", I("tensor_tensor", out=keT[:, hh, cs_], in0=t1[:, 0:Wd], in1=t2[:, 0:Wd], op=ALU.mult),
                             reads=[tn[0], tn[1]], writes=[f"keT{hh}"])
                        nchh = Wd // L
                        T.op("dve", I("tensor_copy", out=eL[:, hh, th * nchh:(th + 1) * nchh], in_=t3[:, 0:Wd].rearrange("p (c l) -> p c l", l=L)[:, :, L - 1]),
                             reads=[tn[2]], writes=["eL"])
                SUBH = int(os.environ.get("SUBH", "99"))
                if SUBH < 1:
                    T.mute = True
                s_i = wnext("in", l, C_HI + 512 * half); vi = wv512(s_i)
                for bk in range(nblk):
                    b = nextbank()
                    fns = [mm(pb[b][0:nt, 0:512], hT[:, kc, bk * 128:bk * 128 + nt], vi[:, kc, :], start=(kc == 0), stop=(kc == 7)) for kc in range(8)]
                    T.op("pe", seq(fns), reads=[f"ws{s_i}"] + HT, writes=[f"pb{b}"])
                    T.op("act", I("activation", out=Vh[0:nt, bk, :, :], in_=pb[b][0:nt, 0:512].rearrange("p (h d) -> p h d", h=4), func=AF.Copy),
                         reads=[f"pb{b}"], writes=["Vh"])
                s_g = wnext("in", l, C_HG + 512 * half); vg = wv512(s_g)
                for hh in range(4):
                    bg = proj(s_g, lambda kc: vg[:, kc, hh * 128:(hh + 1) * 128], TT, hT, HT)
                    T.op("act", I("activation", out=sgg[:, hh, 0:TT], in_=pb[bg][:, 0:TT], func=AF.Silu), reads=[f"pb{bg}"], writes=[f"sgg{hh}"])
                if SUBH < 2:
                    T.mute = True
                def onorm(hh, h):
                    oT = oTs[hh]
                    on = f"oTs{hh}"
                    s_ = sq[alt()]
                    sn = "sq0" if s_ is sq[0] else "sq1"
                    bank, r_, rn = nextstat()
                    T.op("act", I("activation", out=s_[:, 0:TT], in_=oT[:, 0:TT], func=AF.Square), reads=[on], writes=[sn])
                    T.op("pe", mm(pb[bank][:, 0:TT], cs("ones"), s_[:, 0:TT]), reads=[sn, "cstb"], writes=[f"pb{bank}"])
                    rsqrt_from(bank, r_, rn, TT, 1.0 / 128)
                    T.op("dve", I("scalar_tensor_tensor", out=oT[:, 0:TT], in0=oT[:, 0:TT], scalar=pc(l, P_GO, 0), in1=r_[:, 0:TT], op0=ALU.mult, op1=ALU.mult),
                         reads=[on, rn, "pcol"], writes=[on])
                    T.op("dve", I("tensor_tensor", out=yT[:, h, 0:TT], in0=oT[:, 0:TT], in1=sgg[:, hh, 0:TT], op=ALU.mult),
                         reads=[on, f"sgg{hh}"], writes=[f"yT{h}"])

                if kind == "p":
                    HH = range(4)
                    for pr in range(TT // 128):
                        c0 = pr * 128
                        for hh in HH:
                            for cc in range(2):
                                T.op("dve", I("tensor_scalar", out=kd32[hh][:, cc * 64:(cc + 1) * 64], in0=keT[:, hh, c0 + cc * 64:c0 + (cc + 1) * 64],
                                              scalar1=eL[:, hh, 2 * pr + cc:2 * pr + cc + 1], scalar2=None, op0=ALU.mult),
                                     reads=[f"keT{hh}", "eL"], writes=[f"kd32{hh}"])
                        for hh in HH:
                            bk_ = 6 + hh % 2
                            T.op("pe", seq([I("transpose", out=pb[bk_][:, 0:128], in_=kd32[hh][:, :], identity=cs32("ident")),
                                            mm(pb[bk_][:, 128:256], keT[:, hh, c0:c0 + 128], qeT[:, hh, c0:c0 + 128])]),
                                 reads=[f"kd32{hh}", "cst32", f"keT{hh}", f"qeT{hh}"], writes=[f"pb{bk_}"])
                            T.op("act", I("activation", out=kdtok[hh][0:64, 0, :], in_=pb[bk_][0:64, 0:128], func=AF.Copy), reads=[f"pb{bk_}"], writes=[f"kdtok{hh}"])
                            T.op("act", I("activation", out=kdtok[hh][64:128, 1, :], in_=pb[bk_][64:128, 0:128], func=AF.Copy), reads=[f"pb{bk_}"], writes=[f"kdtok{hh}"])
                            T.op("dve", I("tensor_tensor", out=attm[hh][:, :], in0=pb[bk_][:, 128:256], in1=trim, op=ALU.mult),
                                 reads=[f"pb{bk_}", "cstb"], writes=[f"attm{hh}"])
                        for hh in HH:
                            h = 4 * half + hh
                            T.op("pe", seq([mm(pb[hh][:, 0:128], kdtok[hh][:, 0, :], Vh[:, pr, hh, :]),
                                            mm(pb[hh][:, 128:256], kdtok[hh][:, 1, :], Vh[:, pr, hh, :]),
                                            mm(pb[hh][:, 256:384], Vh[:, pr, hh, :], attm[hh][:, :], start=True, stop=False),
                                            mm(pb[hh][:, 256:320], Sbf[:, h, :], qeT[:, hh, c0:c0 + 64], start=False, stop=False)]),
                                 reads=[f"kdtok{hh}", "Vh", f"attm{hh}", f"Sbf{h}", f"qeT{hh}"], writes=[f"pb{hh}"])
                        for hh in HH:
                            h = 4 * half + hh
                            T.op("dve", I("scalar_tensor_tensor", out=Sf(h), in0=Sf(h), scalar=eL[:, hh, 2 * pr:2 * pr + 1], in1=pb[hh][:, 0:128],
                                          op0=ALU.mult, op1=ALU.add),
                                 reads=[f"pb{hh}", "eL", f"Sst{l}h{h}"], writes=[f"Sst{l}h{h}"])
                            T.op("act", I("activation", out=Sbf[:, h, :], in_=Sf(h), func=AF.Copy), reads=[f"Sst{l}h{h}"], writes=[f"Sbf{h}"])
                        for hh in HH:
                            h = 4 * half + hh
                            T.op("pe", mm(pb[hh][:, 320:384], Sbf[:, h, :], qeT[:, hh, c0 + 64:c0 + 128], start=False, stop=True),
                                 reads=[f"Sbf{h}", f"qeT{hh}"], writes=[f"pb{hh}"])
                        for hh in HH:
                            h = 4 * half + hh
                            T.op("dve", I("scalar_tensor_tensor", out=Sf(h), in0=Sf(h), scalar=eL[:, hh, 2 * pr + 1:2 * pr + 2], in1=pb[hh][:, 128:256],
                                          op0=ALU.mult, op1=ALU.add),
                                 reads=[f"pb{hh}", "eL", f"Sst{l}h{h}"], writes=[f"Sst{l}h{h}"])
                            T.op("act", I("activation", out=Sbf[:, h, :], in_=Sf(h), func=AF.Copy), reads=[f"Sst{l}h{h}"], writes=[f"Sbf{h}"])
                            T.op("act", I("activation", out=oTs[hh][:, c0:c0 + 128], in_=pb[hh][:, 256:384], func=AF.Copy), reads=[f"pb{hh}"], writes=[f"oTs{hh}"])
                    if SUBH < 3:
                        T.mute = True
                    for hh in HH:
                        onorm(hh, 4 * half + hh)
                else:
                  for hh in range(4):
                    h = 4 * half + hh
                    oT = oTs[hh]
                    on = f"oTs{hh}"
                    if True:
                        pi = hh
                        for s in range(NSS):
                            T.op("dve", I("tensor_scalar", out=kd32[pi][:, s * LS:(s + 1) * LS], in0=keT[:, hh, s * LS:(s + 1) * LS],
                                                                            scalar1=eL[:, hh, s:s + 1], scalar2=None, op0=ALU.mult),
                                 reads=[f"keT{hh}", "eL"], writes=[f"kd32{pi}"])
                        T.op("pe", seq([I("transpose", out=pb[6][0:64, 0:128], in_=kd32[pi][:, 0:64], identity=cs32("ident")),
                                        mm(pb[6][0:64, 128:192], keT[:, hh, 0:64], qeT[:, hh, 0:64])]),
                             reads=[f"kd32{pi}", "cst32", f"keT{hh}", f"qeT{hh}"], writes=["pb6"])
                        T.op("act", I("activation", out=kdtok[pi][0:64, 0, :], in_=pb[6][0:64, 0:128], func=AF.Copy), reads=["pb6"], writes=[f"kdtok{pi}"])
                        T.op("dve", I("tensor_tensor", out=attm[pi][0:64, 0:64], in0=pb[6][0:64, 128:192], in1=trim[0:64, :], op=ALU.mult),
                             reads=["pb6", "cstb"], writes=[f"attm{pi}"])
                        T.op("pe", mm(pb[7][:, 256:320], Vh[:, 0, hh, :], attm[pi][:, 0:64], start=True, stop=False),
                             reads=["Vh", f"attm{pi}"], writes=["pb7b"])
                        for s in range(NSS):
                            T.dma("sp", f"Ssm{h}", [I("dma_start", out=Ssm[:, h, :], in_=D["shg"][l, s, h])], writes=[f"Ssm{h}"])
                            T.op("act", I("activation", out=Sbf[:, h, :], in_=Ssm[:, h, :], func=AF.Copy), reads=[f"Ssm{h}"], writes=[f"Sbf{h}"])
                            T.op("pe", mm(pb[7][:, 256 + s * LS:256 + (s + 1) * LS], Sbf[:, h, :], qeT[:, hh, s * LS:(s + 1) * LS], start=False, stop=(s == NSS - 1)),
                                 reads=[f"Sbf{h}", f"qeT{hh}"], writes=["pb7b"])
                            T.op("dve", I("tensor_scalar", out=VhM[0:64, hh, :], in0=Vh[0:64, 0, hh, :],
                                                                     scalar1=cst32[0:64, CST["rowm"][0] + s:CST["rowm"][0] + s + 1], scalar2=None, op0=ALU.mult),
                                 reads=["Vh", "cst32"], writes=["VhM"])
                            T.op("pe", mm(pb[6][:, 256:384], kdtok[pi][:, 0, :], VhM[:, hh, :]), reads=[f"kdtok{pi}", "VhM"], writes=["pb6b"])
                            T.op("dve", I("scalar_tensor_tensor", out=Ssm[:, h, :], in0=Ssm[:, h, :], scalar=eL[:, hh, s:s + 1], in1=pb[6][:, 256:384],
                                                                              op0=ALU.mult, op1=ALU.add),
                                 reads=["pb6b", "eL", f"Ssm{h}"], writes=[f"Ssm{h}"])
                            T.dma("sp", f"ohs{h}", [I("dma_start", out=D["nhs"][l, s, h], in_=Ssm[:, h, :])], reads=[f"Ssm{h}"], is_output=True)
                        T.op("act", I("activation", out=oT[:, 0:64], in_=pb[7][:, 256:320], func=AF.Copy), reads=["pb7b"], writes=[on])
                    onorm(hh, h)
            if is_last_p:
                T.dma("sp", "ohp", [I("dma_start", out=D["nhp"][l].rearrange("h k v -> k h v"), in_=Sst[:, l, :, :])],
                      reads=[f"Sst{l}h{h}" for h in range(8)], is_output=True)
            T.mute = False
            dbg("yb", l, kind, yT[:, :, 0:TTS], [f"yT{j}" for j in range(8)])
            gate_and_out(l, C_GB, "who", TT, False)
            dbg("m2", l, kind, merged[:, :, 0:TTS], [f"mg{j}" for j in range(8)])

        def lru(l, kind, tix, TT):
            CO["v"] = (kind == "s" and SCOARSE) or ("lru" not in FINEPH)
            off = 0
            sets = []
            for i in range(2):
                d = {}
                d["X"], off = carve("X", off, (TTP + 12,), F32, tname=f"L{i}X")
                for nm in ("xc", "r", "ig", "a", "a2", "h", "gl"):
                    d[nm], off = carve(nm, off, (TTP,), F32, tname=f"L{i}{nm}")
                d["xcb"], off = carve("xcb", off, (TTP,), BF16, tname=f"L{i}xcb")
                sets.append(d)
            assert off <= ARENA, off
            keys = ("X", "xc", "r", "ig", "a", "a2", "h", "gl", "xcb")
            reg([f"L{i}{k}" for i in range(2) for k in keys], 0, off)
            nseq = 1 if kind == "p" else NSS
            Ls = TT // nseq
            is_last_p = (kind == "p" and tix == NTILE - 1)
            if kind == "s":
                T.dma("sp", "h0s", [I("dma_start", out=h0s[:, s, :], in_=D["slr"][l, s].rearrange("(j p) -> p j", p=128), allow_slow_non_contiguous=True)
                                    for s in range(NSS)], writes=["h0s"])
            for half in range(2):
                s_x = wnext("in", l, C_LX + 512 * half); vx = wv512(s_x)
                s_g = wnext("in", l, C_LG + 512 * half); vg = wv512(s_g)
                stv = {}

                def stA(jj):
                    j = 4 * half + jj
                    d = sets[j % 2]
                    n = lambda k, j=j: f"L{j % 2}{k}"
                    Xv = d["X"][:, 0:nseq * (Ls + 3)].rearrange("p (s t) -> p s t", s=nseq)
                    v3 = lambda ap: ap[:, 0:TT].rearrange("p (s t) -> p s t", s=nseq)
                    bx = proj(s_x, lambda kc: vx[:, kc, jj * 128:(jj + 1) * 128], TT, hT, HT)
                    if kind == "p":
                        T.op("dve", I("tensor_copy", out=Xv[:, 0, 0:3], in_=lxhalo[:, l, j, :]), reads=[f"lxhalo{l}"], writes=[n("X")])
                    else:
                        T.dma("sp", f"xhs{j}", [I("dma_start", out=xhs[:, j, s, :], in_=D["scv"][l, s, :, j * 128:(j + 1) * 128].rearrange("t p -> p t"),
                                                                             allow_slow_non_contiguous=True) for s in range(NSS)], writes=[f"xhs{j}"])
                        T.op("dve", I("tensor_copy", out=Xv[:, :, 0:3], in_=xhs[:, j, :, :]), reads=[f"xhs{j}"], writes=[n("X")])
                    T.op("act", I("activation", out=Xv[:, :, 3:3 + Ls], in_=pb[bx][:, 0:TT].rearrange("p (s t) -> p s t", s=nseq), func=AF.Copy),
                         reads=[f"pb{bx}"], writes=[n("X")])
                    if kind == "p":
                        T.op("dve", I("tensor_copy", out=lxhalo[:, l, j, :], in_=Xv[:, 0, Ls:Ls + 3]), reads=[n("X")], writes=[f"lxhalo{l}"])
                        if is_last_p:
                            T.op("dve", I("tensor_copy", out=cstage[:, j, 0, :], in_=Xv[:, 0, Ls:Ls + 3]), reads=[n("X")], writes=["cstage"])
                    else:
                        T.op("dve", I("tensor_copy", out=cstage[:, j, :, :], in_=Xv[:, :, Ls:Ls + 3]), reads=[n("X")], writes=["cstage"])
                    xc3 = v3(d["xc"])
                    T.op("dve", I("tensor_scalar", out=xc3, in0=Xv[:, :, 0:Ls], scalar1=pc(l, P_CW + 0, j), scalar2=pc(l, P_CB, j), op0=ALU.mult, op1=ALU.add),
                         reads=[n("X"), "pcol"], writes=[n("xc")])
                    for k in range(1, 4):
                        T.op("dve", I("scalar_tensor_tensor", out=xc3, in0=Xv[:, :, k:k + Ls], scalar=pc(l, P_CW + k, j), in1=xc3, op0=ALU.mult, op1=ALU.add),
                             reads=[n("X"), n("xc"), "pcol"], writes=[n("xc")])
                    T.op("act", I("activation", out=d["xcb"][:, 0:TT], in_=d["xc"][:, 0:TT], func=AF.Copy), reads=[n("xc")], writes=[n("xcb")])
                    ba = 6 if jj % 2 == 0 else nextbank()
                    T.op("pe", mm(pb[ba][:, 0:TT], bda[:, l, 0, j, :], d["xcb"][:, 0:TT]), reads=["bda", n("xcb")], writes=[f"pb{ba}"])
                    bi = 7 if jj % 2 == 0 else nextbank()
                    T.op("pe", mm(pb[bi][:, 0:TT], bda[:, l, 1, j, :], d["xcb"][:, 0:TT]), reads=["bda", n("xcb")], writes=[f"pb{bi}"])
                    stv[jj] = (j, d, n, ba, bi)

                def stB(jj):
                    j, d, n, ba, bi = stv[jj]
                    T.op("act", I("activation", out=d["r"][:, 0:TT], in_=pb[ba][:, 0:TT], func=AF.Sigmoid, bias=pc(l, P_BA, j)), reads=[f"pb{ba}", "pcol"], writes=[n("r")])
                    T.op("act", I("activation", out=d["ig"][:, 0:TT], in_=pb[bi][:, 0:TT], func=AF.Sigmoid, bias=pc(l, P_BX, j)), reads=[f"pb{bi}", "pcol"], writes=[n("ig")])
                    T.op("act", I("activation", out=d["a"][:, 0:TT], in_=d["r"][:, 0:TT], func=AF.Exp, scale=dcol(l, 4, j)), reads=[n("r")] + DER_ALL, writes=[n("a")])
                    T.op("act", I("activation", out=d["a2"][:, 0:TT], in_=d["r"][:, 0:TT], func=AF.Exp, scale=dcol(l, 5, j)), reads=[n("r")] + DER_ALL, writes=[n("a2")])
                    T.op("dve", I("tensor_scalar", out=d["a2"][:, 0:TT], in0=d["a2"][:, 0:TT], scalar1=-1.0, scalar2=1.0, op0=ALU.mult, op1=ALU.add), reads=[n("a2")], writes=[n("a2")])
                    T.op("act", I("activation", out=d["a2"][:, 0:TT], in_=d["a2"][:, 0:TT], func=AF.Sqrt), reads=[n("a2")], writes=[n("a2")])
                    T.op("dve", I("tensor_tensor", out=d["ig"][:, 0:TT], in0=d["ig"][:, 0:TT], in1=d["xc"][:, 0:TT], op=ALU.mult), reads=[n("ig"), n("xc")], writes=[n("ig")])
                    T.op("dve", I("tensor_tensor", out=d["a2"][:, 0:TT], in0=d["a2"][:, 0:TT], in1=d["ig"][:, 0:TT], op=ALU.mult), reads=[n("a2"), n("ig")], writes=[n("a2")])
                    if kind == "p":
                        T.op("dve", I("tensor_tensor_scan", out=d["h"][:, 0:TT], data0=d["a"][:, 0:TT], data1=d["a2"][:, 0:TT], initial=hprev[:, l, j:j + 1], op0=ALU.mult, op1=ALU.add),
                             reads=[n("a"), n("a2"), f"hprev{l}"], writes=[n("h")])
                        T.op("dve", I("tensor_copy", out=hprev[:, l, j:j + 1], in_=d["h"][:, TT - 1:TT]), reads=[n("h")], writes=[f"hprev{l}"])
                    else:
                        for s in range(NSS):
                            T.op("dve", I("tensor_tensor_scan", out=d["h"][:, s * Ls:(s + 1) * Ls], data0=d["a"][:, s * Ls:(s + 1) * Ls], data1=d["a2"][:, s * Ls:(s + 1) * Ls],
                                                                                   initial=h0s[:, s, j:j + 1], op0=ALU.mult, op1=ALU.add),
                                 reads=[n("a"), n("a2"), "h0s"], writes=[n("h")])
                        T.op("dve", I("tensor_copy", out=hstage[:, j, :], in_=d["h"][:, 0:TT].rearrange("p (s t) -> p s t", s=NSS)[:, :, Ls - 1]), reads=[n("h")], writes=["hstage"])
                    bg = proj(s_g, lambda kc: vg[:, kc, jj * 128:(jj + 1) * 128], TT, hT, HT)
                    T.op("act", I("activation", out=d["gl"][:, 0:TT], in_=pb[bg][:, 0:TT], func=AF.Gelu_apprx_tanh), reads=[f"pb{bg}"], writes=[n("gl")])
                    T.op("dve", I("tensor_tensor", out=yT[:, j, 0:TT], in0=d["h"][:, 0:TT], in1=d["gl"][:, 0:TT], op=ALU.mult), reads=[n("h"), n("gl")], writes=[f"yT{j}"])

                stA(0)
                for jj in range(4):
                    if jj + 1 < 4:
                        stA(jj + 1)
                    stB(jj)
            if kind == "s":
                T.dma("sp", "ocs", [I("dma_start", out=D["ncs"][l, s, t].rearrange("(j p) -> p j", p=128), in_=cstage[:, :, s, t], allow_slow_non_contiguous=True)
                                    for s in range(NSS) for t in range(3)], reads=["cstage"], is_output=True)
                T.dma("sp", "ols", [I("dma_start", out=D["nls"][l, s].rearrange("(j p) -> p j", p=128), in_=hstage[:, :, s], allow_slow_non_contiguous=True)
                                    for s in range(NSS)], reads=["hstage"], is_output=True)
            elif is_last_p:
                T.dma("sp", "ocsP", [I("dma_start", out=D["ncp"][l, t].rearrange("(j p) -> p j", p=128), in_=cstage[:, :, 0, t], allow_slow_non_contiguous=True)
                                     for t in range(3)], reads=["cstage"], is_output=True)
                T.dma("sp", "olsP", [I("dma_start", out=D["nlp"][l].rearrange("(j p) -> p j", p=128), in_=hprev[:, l, :], allow_slow_non_contiguous=True)],
                      reads=[f"hprev{l}"], is_output=True)
            dbg("yc", l, kind, yT[:, :, 0:TTS], [f"yT{j}" for j in range(8)])
            gate_and_out(l, C_GC, "wlo", TT, False)
            dbg("m3", l, kind, merged[:, :, 0:TTS], [f"mg{j}" for j in range(8)])

        def load_x(kind, tix, TT):
            src = D["xp"] if kind == "p" else D["xs"]
            for bk in range(max(TT // 128, 1)):
                nt = min(128, TT)
                xi = xin[bk % 2]
                r0 = (tix * TTP if kind == "p" else 0) + bk * 128
                T.dma("sp", "xin0", [I("dma_start", out=xi[0:nt, :], in_=src[r0:r0 + nt, :])], writes=["xin0"])
                for j in range(8):
                    b = nextbank()
                    T.op("pe", I("transpose", out=pb[b][:, 0:nt], in_=xi[0:nt, j * 128:(j + 1) * 128], identity=cs32("ident", slice(0, nt))[:, 0:nt]),
                         reads=["xin0", "cst32"], writes=[f"pb{b}"])
                    eng = "act" if j % 2 else "dve"
                    if eng == "act":
                        T.op("act", I("activation", out=xT[:, j, bk * 128:bk * 128 + nt], in_=pb[b][:, 0:nt], func=AF.Copy), reads=[f"pb{b}"], writes=[f"xT{j}"])
                    else:
                        T.op("dve", I("tensor_copy", out=xT[:, j, bk * 128:bk * 128 + nt], in_=pb[b][:, 0:nt]), reads=[f"pb{b}"], writes=[f"xT{j}"])

        def store_x(kind, tix, TT):
            dst = D["yp"] if kind == "p" else D["ys"]
            for bk in range(max(TT // 128, 1)):
                nt = min(128, TT)
                xi = xin[bk % 2]
                r0 = (tix * TTP if kind == "p" else 0) + bk * 128
                for j in range(8):
                    b = nextbank()
                    T.op("pe", I("transpose", out=pb[b][0:nt, 0:128], in_=xT[:, j, bk * 128:bk * 128 + nt], identity=cs32("ident")),
                         reads=[f"xT{j}", "cst32"], writes=[f"pb{b}"])
                    if j % 2:
                        T.op("act", I("activation", out=xi[0:nt, j * 128:(j + 1) * 128], in_=pb[b][0:nt, 0:128], func=AF.Copy), reads=[f"pb{b}"], writes=["xin0"])
                    else:
                        T.op("dve", I("tensor_copy", out=xi[0:nt, j * 128:(j + 1) * 128], in_=pb[b][0:nt, 0:128]), reads=[f"pb{b}"], writes=["xin0"])
                T.dma("sp", "xin0", [I("dma_start", out=dst[r0:r0 + nt, :], in_=xi[0:nt, :])], reads=["xin0"], is_output=True)

        def ffn_phase(l, tag, prow, TT, kind="p"):
            CO["v"] = (kind == "s" and SCOARSE) or ("ffn" not in FINEPH)
            off = 0
            aT, off = carve("aT", off, (22, TTP), BF16, sub=True)
            sgb = []
            for i in range(2):
                v_, off = carve("sg", off, (TTP,), F32, tname=f"sg{i}"); sgb.append(v_)
            assert off <= ARENA
            reg([f"aT{f}" for f in range(22)] + ["sg0", "sg1"], 0, off)
            ffn(l, tag, prow, TT, aT, sgb)

        phase_ctr = {"n": 0}

        def phase(nm):
            phase_ctr["n"] += 1
            if stop is not None and phase_ctr["n"] > stop:
                raise _Stop(nm)

        def run_all():
          for kind, tix in tiles:
            TT = TTP if kind == "p" else TTS
            phase("load")
            load_x(kind, tix, TT)
            for l in range(nlayer):
                phase("ffn1")
                ffn_phase(l, "1", P_G1, TT, kind)
                phase("attn")
                dbg("x1", l, kind, xT[:, :, 0:TTS], [f"xT{j}" for j in range(8)])
                rmsnorm(l, P_GM, TT, None)
                attention(l, kind, tix, TT)
                phase("hgrn")
                hgrn(l, kind, tix, TT)
                phase("lru")
                lru(l, kind, tix, TT)
                phase("out")
                for j in range(8):
                    if j % 2:
                        T.op("act", I("activation", out=yT[:, j, 0:TT], in_=merged[:, j, 0:TT], func=AF.Copy), reads=[f"mg{j}"], writes=[f"yT{j}"])
                    else:
                        T.op("dve", I("tensor_copy", out=yT[:, j, 0:TT], in_=merged[:, j, 0:TT]), reads=[f"mg{j}"], writes=[f"yT{j}"])
                YT = [f"yT{j}" for j in range(8)]
                for half in range(2):
                    slot = wnext("sq", l, "wout", 512 * half)
                    v = wv512(slot)
                    for jj in range(4):
                        j = 4 * half + jj
                        b = proj(slot, lambda kc: v[:, kc, jj * 128:(jj + 1) * 128], TT, yT, YT)
                        T.op("dve", I("tensor_tensor", out=xT[:, j, 0:TT], in0=pb[b][:, 0:TT], in1=xT[:, j, 0:TT], op=ALU.add),
                             reads=[f"pb{b}", f"xT{j}"], writes=[f"xT{j}"])
                dbg("xm", l, kind, xT[:, :, 0:TTS], [f"xT{j}" for j in range(8)])
                ffn_phase(l, "2", P_G2, TT, kind)
            phase("store")
            store_x(kind, tix, TT)
        try:
            run_all()
            assert wstate["i"] == len(wsched)
        except _Stop as ex:
            print("build stopped before phase", ex)
        T.finish()
        T.emit()
    return nc


TT_DUMMY = None
_NC_CACHE = {}


def _get_nc(key=(NTILE, True, 2)):
    if key not in _NC_CACHE:
        _NC_CACHE[key] = build(*key)
    return _NC_CACHE[key]


PROMPT_CORES = (0, 1, 4, 5)


def make_in_maps(inp):
    f = lambda a: np.ascontiguousarray(np.asarray(a, dtype=np.float32))
    P = np.zeros((NPR, 1024), np.float32)
    for l in range(2):
        b = 16 * l
        P[b + P_G1] = inp["norm_ffn1"][l]; P[b + P_GM] = inp["norm_mix"][l]; P[b + P_G2] = inp["norm_ffn2"][l]
        P[b + P_LB] = inp["hgrn_lb_logits"][l]
        P[b + P_CW:b + P_CW + 4] = inp["conv_w"][l]
        P[b + P_CB] = inp["conv_b"][l]; P[b + P_BA] = inp["lru_b_a"][l]; P[b + P_BX] = inp["lru_b_x"][l]; P[b + P_LAM] = inp["lru_lambda"][l]
        P[b + P_GQ] = np.tile(inp["q_norm"][l], 16); P[b + P_GK] = np.tile(inp["k_norm"][l], 16)
        P[b + P_SINK] = np.repeat(inp["attn_sinks"][l], 64); P[b + P_GO] = np.tile(inp["hgrn_o_norm"][l], 8)
    cst = make_consts()
    shared = {"w1u": f(inp["w_ffn1_up"]), "w1d": f(inp["w_ffn1_down"]), "win": f(inp["w_in"]), "wao": f(inp["w_attn_o"]),
              "who": f(inp["w_hgrn_o"]), "wlo": f(inp["w_lru_o"]), "wout": f(inp["w_out"]), "w2u": f(inp["w_ffn2_up"]),
              "w2d": f(inp["w_ffn2_down"]), "lwa": f(inp["lru_w_a"]), "lwx": f(inp["lru_w_x"]), "P": P, "cst": cst}
    maps = []
    zero_xp = np.zeros((SEQ, 1024), np.float32)
    for c in range(8):
        s0 = NSS * c
        m = dict(shared)
        m["xp"] = f(inp["x_prompt"][PROMPT_CORES.index(c)]) if c in PROMPT_CORES else zero_xp
        m["xs"] = f(inp["x_sample"][s0:s0 + NSS]).reshape(TTS, 1024)
        m["ck"] = f(inp["cache_attn_k"][:, s0:s0 + NSS]).reshape(2, NSS, 128, 256)
        m["cv"] = f(inp["cache_attn_v"][:, s0:s0 + NSS]).reshape(2, NSS, 128, 256)
        m["shg"] = f(inp["state_hgrn"][:, s0:s0 + NSS])
        m["scv"] = f(inp["state_conv"][:, s0:s0 + NSS])
        m["slr"] = f(inp["state_lru"][:, s0:s0 + NSS])
        maps.append(m)
    return maps


def assemble(res):
    R = res
    cat = lambda k, ax, cores: np.concatenate([R[c][k] for c in cores], axis=ax)
    pc_ = PROMPT_CORES
    ac = range(8)
    yp = np.stack([R[c]["yp"] for c in pc_], 0)
    ys = np.concatenate([R[c]["ys"].reshape(NSS, LS, 1024) for c in ac], 0)
    nkp = np.stack([R[c]["nkp"].reshape(2, 128, 4, 64) for c in pc_], 1)
    nvp = np.stack([R[c]["nvp"].reshape(2, 128, 4, 64) for c in pc_], 1)
    nhp = np.stack([R[c]["nhp"] for c in pc_], 1)
    ncp = np.stack([R[c]["ncp"] for c in pc_], 1)
    nlp = np.stack([R[c]["nlp"] for c in pc_], 1)
    nks = np.concatenate([R[c]["nks"].reshape(2, NSS, LS, 4, 64) for c in ac], 1)
    nvs = np.concatenate([R[c]["nvs"].reshape(2, NSS, LS, 4, 64) for c in ac], 1)
    nhs = cat("nhs", 1, ac)
    ncs = cat("ncs", 1, ac)
    nls = cat("nls", 1, ac)
    outs = (yp, ys, nkp, nvp, nhp, ncp, nlp, nks, nvs, nhs, ncs, nls)
    return tuple(np.ascontiguousarray(o.astype(np.float32)) for o in outs)


def kernel(**inputs):
    nc = _get_nc()
    maps = make_in_maps(inputs)
    res = run_bass_kernel_spmd(nc, maps, core_ids=list(range(8)))
    return assemble(res.results)
```
